# Optimizing a Trainium2 kernel written in Bass

```python
import jax, jax.numpy as jnp
from jax import lax
import numpy as np

D_MODEL = 4096
BATCH = 2
SEQ = 4096
DEPTH = 1
DEC_BATCH = 2
DEC_SEQ = 8192
PAST_LEN = 128

HEAD_DIM = 128
N_HEADS = D_MODEL // HEAD_DIM
N_KV_HEADS = N_HEADS // 4
Q_BLOCK = 128
ROPE_THETA = 10000.0
GRID_W = 64
ATT_WIDTH = N_HEADS * HEAD_DIM
KV_WIDTH = N_KV_HEADS * HEAD_DIM
N_FGROUPS = 8
FGROUP_DIM = D_MODEL // 16
F_WIDTH = N_FGROUPS * FGROUP_DIM
SPLITS = [int(v) for v in np.cumsum([ATT_WIDTH, KV_WIDTH, KV_WIDTH, F_WIDTH, D_MODEL])]
IN_WIDTH = ATT_WIDTH + 2 * KV_WIDTH + F_WIDTH + 2 * D_MODEL
PEER_HEADS = 8
PEER_NKEYS = 128
PEER_N = PEER_NKEYS * PEER_NKEYS
PEER_DKEY = 256
PEER_HALF = PEER_DKEY // 2
PEER_TOPK = 16
TOKEN_BLOCK = 128
EPS = 1e-6

kernel_name = 'hybrid_gqa_fnet_peer_encoder'


def rmsnorm(x, g):
    xf = x.astype(jnp.float32)
    y = xf * lax.rsqrt(jnp.mean(xf * xf, axis=-1, keepdims=True) + EPS)
    return (y * g.astype(jnp.float32)).astype(x.dtype)


def axial_rope_tables(seq_len):
    rows = seq_len // GRID_W
    row_ids = jnp.repeat(jnp.arange(rows), GRID_W).astype(jnp.float32)
    col_ids = jnp.tile(jnp.arange(GRID_W), rows).astype(jnp.float32)
    axis_dim = HEAD_DIM // 2
    inv_freq = ROPE_THETA ** (-jnp.arange(0, axis_dim, 2, dtype=jnp.float32) / axis_dim)
    ang = jnp.concatenate([row_ids[:, None] * inv_freq, col_ids[:, None] * inv_freq], axis=-1)
    return jnp.cos(ang), jnp.sin(ang)


def apply_rope(x, cos, sin):
    xf = x.astype(jnp.float32).reshape(x.shape[:-1] + (HEAD_DIM // 2, 2))
    x1, x2 = xf[..., 0], xf[..., 1]
    c = cos[None, :, None, :]
    s = sin[None, :, None, :]
    out = jnp.stack([x1 * c - x2 * s, x1 * s + x2 * c], axis=-1).reshape(x.shape)
    return out.astype(x.dtype)


def block_attention(q, k, v):
    b, s = q.shape[0], q.shape[1]
    grp = N_HEADS // N_KV_HEADS
    nblk = s // Q_BLOCK
    qb = q.reshape(b, nblk, Q_BLOCK, N_KV_HEADS, grp, HEAD_DIM).transpose(1, 0, 2, 3, 4, 5)
    scale = HEAD_DIM ** -0.5

    def one_block(qblk):
        sc = jnp.einsum('bqhgd,bkhd->bhgqk', qblk, k, preferred_element_type=jnp.float32) * scale
        p = jax.nn.softmax(sc, axis=-1).astype(v.dtype)
        return jnp.einsum('bhgqk,bkhd->bqhgd', p, v)

    out = lax.map(one_block, qb)
    return out.transpose(1, 0, 2, 3, 4, 5).reshape(b, s, ATT_WIDTH)


def fourier_mix(u):
    b, s = u.shape[0], u.shape[1]
    ug = u.astype(jnp.float32).reshape(b, s, N_FGROUPS, FGROUP_DIM)
    y = jnp.fft.fftn(ug, axes=(1, 3), norm='ortho').real
    return y.reshape(b, s, F_WIDTH).astype(u.dtype)


def peer_ffn(h, w_q, keys, u_tab, v_tab):
    b, s, d = h.shape
    t = b * s
    hf = h.reshape(t, d)
    q = (hf @ w_q).reshape(t, PEER_HEADS, 2, PEER_HALF)
    sc = jnp.einsum('thpc,hpnc->thpn', q, keys).astype(jnp.float32)
    top_v, top_i = lax.top_k(sc, PEER_TOPK)
    cand_v = top_v[:, :, 0, :, None] + top_v[:, :, 1, None, :]
    cand_i = top_i[:, :, 0, :, None] * PEER_NKEYS + top_i[:, :, 1, None, :]
    kk = PEER_TOPK * PEER_TOPK
    best_v, best_pos = lax.top_k(cand_v.reshape(t, PEER_HEADS, kk), PEER_TOPK)
    best_i = jnp.take_along_axis(cand_i.reshape(t, PEER_HEADS, kk), best_pos, axis=-1)
    gates = jax.nn.softmax(best_v, axis=-1)
    n_sel = PEER_HEADS * PEER_TOPK
    idx = best_i.reshape(t, n_sel).astype(jnp.int32)
    gw = gates.reshape(t, n_sel).astype(h.dtype)
    nblk = t // TOKEN_BLOCK

    def one_block(args):
        xb, ib, gb = args
        ub = jnp.take(u_tab, ib, axis=0)
        act = jax.nn.gelu(jnp.einsum('td,ted->te', xb, ub), approximate=False)
        vb = jnp.take(v_tab, ib, axis=0)
        return jnp.einsum('te,ted->td', act * gb, vb)

    out = lax.map(one_block, (hf.reshape(nblk, TOKEN_BLOCK, d),
                              idx.reshape(nblk, TOKEN_BLOCK, n_sel),
                              gw.reshape(nblk, TOKEN_BLOCK, n_sel)))
    return out.reshape(b, s, d)


def encoder_layer(x, c, cos, sin, w_ada, b_ada, norm1_g, norm2_g, w_in, q_norm_g, k_norm_g,
                  w_attn_br, w_four_br, w_out, w_peer_q, peer_keys, peer_u, peer_v):
    b, s, _ = x.shape
    mod = (jax.nn.silu(c) @ w_ada + b_ada)[:, None, :]
    sh1, sc1, g1, sh2, sc2, g2 = jnp.split(mod, 6, axis=-1)

    h = rmsnorm(x, norm1_g) * (1.0 + sc1) + sh1
    proj = h @ w_in
    q, k, v, f, ga, gf = jnp.split(proj, SPLITS, axis=-1)
    q = apply_rope(rmsnorm(q.reshape(b, s, N_HEADS, HEAD_DIM), q_norm_g), cos, sin)
    k = apply_rope(rmsnorm(k.reshape(b, s, N_KV_HEADS, HEAD_DIM), k_norm_g), cos, sin)
    v = v.reshape(b, s, N_KV_HEADS, HEAD_DIM)
    a_br = block_attention(q, k, v) @ w_attn_br
    f_br = fourier_mix(f) @ w_four_br
    merged = jax.nn.sigmoid(ga) * a_br + jax.nn.sigmoid(gf) * f_br
    x = x + g1 * (merged @ w_out)

    h2 = rmsnorm(x, norm2_g) * (1.0 + sc2) + sh2
    x = x + g2 * peer_ffn(h2, w_peer_q, peer_keys, peer_u, peer_v)
    return x


def run_trunk(x, c, w_ada, b_ada, norm1_g, norm2_g, w_in, q_norm_g, k_norm_g, w_attn_br,
              w_four_br, w_out, w_peer_q, peer_keys, peer_u, peer_v, final_g):
    cos, sin = axial_rope_tables(x.shape[1])
    for l in range(DEPTH):
        x = encoder_layer(x, c, cos, sin, w_ada[l], b_ada[l], norm1_g[l], norm2_g[l], w_in[l],
                          q_norm_g[l], k_norm_g[l], w_attn_br[l], w_four_br[l], w_out[l],
                          w_peer_q[l], peer_keys[l], peer_u[l], peer_v[l])
    return rmsnorm(x, final_g)


def setup_inputs(seed: int = 0) -> dict:
    key = jax.random.key(seed)
    ks = jax.random.split(key, 20)
    f32 = jnp.float32
    nrm = lambda k, shape, sc: jax.random.normal(k, shape, f32) * sc
    gain = lambda k, shape: 1.0 + 0.02 * jax.random.normal(k, shape, f32)
    L, D = DEPTH, D_MODEL
    return {
        'x_prompt': nrm(ks[0], (BATCH, SEQ, D), 1.0),
        'x_sample': nrm(ks[1], (DEC_BATCH, DEC_SEQ, D), 1.0),
        'c_prompt': nrm(ks[2], (BATCH, D), 1.0),
        'c_sample': nrm(ks[3], (DEC_BATCH, D), 1.0),
        'w_ada': nrm(ks[4], (L, D, 6 * D), 0.5 * D ** -0.5),
        'b_ada': nrm(ks[5], (L, 6 * D), 0.01),
        'norm1_g': gain(ks[6], (L, D)),
        'norm2_g': gain(ks[7], (L, D)),
        'w_in': nrm(ks[8], (L, D, IN_WIDTH), D ** -0.5),
        'q_norm_g': gain(ks[9], (L, HEAD_DIM)),
        'k_norm_g': gain(ks[10], (L, HEAD_DIM)),
        'w_attn_br': nrm(ks[11], (L, ATT_WIDTH, D), ATT_WIDTH ** -0.5),
        'w_four_br': nrm(ks[12], (L, F_WIDTH, D), F_WIDTH ** -0.5),
        'w_out': nrm(ks[13], (L, D, D), D ** -0.5),
        'w_peer_q': nrm(ks[14], (L, D, PEER_HEADS * PEER_DKEY), D ** -0.5),
        'peer_keys': nrm(ks[15], (L, PEER_HEADS, 2, PEER_NKEYS, PEER_HALF), PEER_HALF ** -0.5),
        'peer_u': nrm(ks[16], (L, PEER_N, D), D ** -0.5),
        'peer_v': nrm(ks[17], (L, PEER_N, D), PEER_HEADS ** -0.5),
        'final_g': gain(ks[18], (D,)),
    }


def reference(x_prompt, x_sample, c_prompt, c_sample, w_ada, b_ada, norm1_g, norm2_g, w_in,
              q_norm_g, k_norm_g, w_attn_br, w_four_br, w_out, w_peer_q, peer_keys, peer_u,
              peer_v, final_g):
    y_prompt = run_trunk(x_prompt, c_prompt, w_ada, b_ada, norm1_g, norm2_g, w_in, q_norm_g,
                         k_norm_g, w_attn_br, w_four_br, w_out, w_peer_q, peer_keys, peer_u,
                         peer_v, final_g)
    y_sample = run_trunk(x_sample, c_sample, w_ada, b_ada, norm1_g, norm2_g, w_in, q_norm_g,
                         k_norm_g, w_attn_br, w_four_br, w_out, w_peer_q, peer_keys, peer_u,
                         peer_v, final_g)
    return (y_prompt, y_sample)
```

```python
import contextlib
import numpy as np
import ml_dtypes
import concourse.bass as bass
import concourse.mybir as mybir
from concourse.bass_utils import run_bass_kernel_spmd

F32 = mybir.dt.float32
BF16 = mybir.dt.bfloat16
AF = mybir.ActivationFunctionType
ALU = mybir.AluOpType
AX = mybir.AxisListType


class Buf:
    __slots__ = ("name", "t", "w", "r", "dsem", "dcnt")

    def __init__(self, name, t):
        self.name = name
        self.t = t
        self.w = None
        self.r = {}
        self.dsem = None
        self.dcnt = 0

    def __getitem__(self, k):
        return self.t[k]


class Sync:
    ENG = ("pe", "act", "dve", "pool", "sp")

    def __init__(self, nc, stack):
        self.nc = nc
        self.stack = stack
        self.eng = {"pe": nc.tensor, "act": nc.scalar, "dve": nc.vector, "pool": nc.gpsimd, "sp": nc.sync}
        self.esem = {}
        for e in ("pe", "act", "dve", "pool"):
            self.esem[e] = stack.enter_context(nc.semaphore("es_" + e))
        self.cnt = {e: 0 for e in self.ENG}
        self.seen = {e: {} for e in self.ENG}
        self.out_dma = {e: {} for e in self.ENG}
        self.free_dsems = []
        self.stage_dsems = []
        self.nsem = 0

    def uname(self, name):
        self.nname = getattr(self, "nname", 0) + 1
        return "s%d_%s" % (self.nname, name)

    def raw_sbuf(self, stack, name, shape, dt):
        return stack.enter_context(self.nc.sbuf_tensor(self.uname(name), list(shape), dt))

    def sbuf(self, stack, name, shape, dt):
        return Buf(name, self.raw_sbuf(stack, name, shape, dt))

    def psum(self, stack, name, shape, dt=F32):
        t = stack.enter_context(self.nc.psum_tensor(self.uname(name), list(shape), dt))
        return Buf(name, t)

    def _dsem(self, b):
        if b.dsem is None:
            if self.free_dsems:
                b.dsem = self.free_dsems.pop()
            else:
                self.nsem += 1
                b.dsem = [self.stack.enter_context(self.nc.semaphore("ds%d" % self.nsem)), 0]
            self.stage_dsems.append(b.dsem)
        return b.dsem

    def _wait(self, e, tok):
        sem, val = tok
        k = id(sem)
        if self.seen[e].get(k, 0) >= val:
            return
        self.eng[e].wait_ge(sem, val)
        self.seen[e][k] = val

    def _deps(self, e, reads, writes, is_dma=False):
        own = None if is_dma else self.esem.get(e)
        toks = []
        for b in reads:
            if b.w is not None:
                toks.append(b.w)
        for b in writes:
            if b.w is not None and b.w[0] is not own:
                toks.append(b.w)
            toks.extend(t for t in b.r.values() if t[0] is not own)
        for t in toks:
            if e == "pe" and t[0] is own:
                continue
            self._wait(e, t)

    def _commit(self, tok, reads, writes):
        k = id(tok[0])
        for b in reads:
            b.r[k] = tok
        for b in writes:
            b.w = tok
            b.r = {}

    def op(self, e, fn, reads=(), writes=()):
        self._deps(e, reads, writes)
        ins = fn()
        self.cnt[e] += 1
        ins.then_inc(self.esem[e], 1)
        self._commit((self.esem[e], self.cnt[e]), reads, writes)

    def pe(self, fns, reads=(), writes=()):
        self._deps("pe", reads, writes)
        ins = None
        for fn in fns:
            ins = fn()
        self.cnt["pe"] += 1
        ins.then_inc(self.esem["pe"], 1)
        self._commit((self.esem["pe"], self.cnt["pe"]), reads, writes)

    def dma(self, q, out, in_, tokbuf, reads=(), writes=()):
        self._deps(q, reads, writes, is_dma=True)
        cell = self._dsem(tokbuf)
        sem = cell[0]
        ins = self.eng[q].dma_start(out=out, in_=in_)
        cell[1] += 16
        ins.then_inc(sem, 16)
        tok = (sem, cell[1])
        self._commit(tok, reads, writes)
        self.out_dma[q][id(sem)] = tok

    def drain(self):
        for q in self.ENG:
            for tok in self.out_dma[q].values():
                self._wait(q, tok)
            self.out_dma[q] = {}
        for e in ("act", "dve", "pool"):
            if self.cnt[e] > 0:
                self._wait(e, (self.esem[e], self.cnt[e]))

    def barrier(self):
        self.drain()
        self.nc.all_engine_barrier()
        self.free_dsems.extend(self.stage_dsems)
        self.stage_dsems = []


class Cfg:
    def __init__(self, D=4096, SP=4096, SS=8192, NH=32, NKV=8, FG=8, FGD=256, PH=8, GRID_W=64,
                 TT=512, WC=512, debug=False):
        self.D, self.SP, self.SS, self.NH, self.NKV = D, SP, SS, NH, NKV
        self.FG, self.FGD, self.PH, self.GRID_W, self.TT, self.WC = FG, FGD, PH, GRID_W, TT, WC
        self.debug = debug
        self.KC = D // 128
        self.ATT = NH * 128
        self.KVW = NKV * 128
        self.FW = FG * FGD
        self.INW = self.ATT + 2 * self.KVW + self.FW + 2 * D
        self.cQ, self.cK = 0, self.ATT
        self.cV = self.cK + self.KVW
        self.cF = self.cV + self.KVW
        self.cGA = self.cF + self.FW
        self.cGF = self.cGA + D
        self.OP, self.OS = SP // 4, SS // 4
        self.NOWN = self.OP + self.OS
        self.NALL = SP + SS
        self.HP = PH * 2
        self.PQ = self.HP * 128
        self.NE = 128 * 128
        self.EPS = 1e-6
        self.seqs = [(SP, 0, 0, self.OP, 0), (SS, SP, self.OP, self.OS, 1)]


class Rot:
    def __init__(self, bufs):
        self.bufs = bufs
        self.i = -1

    def next(self):
        self.i = (self.i + 1) % len(self.bufs)
        return self.bufs[self.i]


def build(cfg):
    c = cfg
    D, KC, TT = c.D, c.KC, c.TT
    NSUB = TT // 128
    nc = bass.Bass("TRN2", target_bir_lowering=False)
    T = {}

    def din(name, shape, dt=F32):
        T[name] = nc.dram_tensor(name, list(shape), dt, kind="ExternalInput").ap()

    def dscr(name, shape, dt):
        kind = "ExternalOutput" if c.debug else "Internal"
        T[name] = nc.dram_tensor(name, list(shape), dt, kind=kind).ap()

    din("xall", [c.NALL, D]); din("xown", [c.NOWN, D])
    din("cT", [128, KC, 2])
    din("w_ada", [D, 6 * D]); din("b_ada", [1, 6 * D])
    din("n1g", [1, D]); din("n2g", [1, D]); din("fg", [1, D])
    din("w_in", [D, c.INW])
    din("qg", [128, 1]); din("kg", [128, 1])
    din("w_attn", [c.ATT, D]); din("w_four", [c.FW, D]); din("w_out", [D, D]); din("w_pq", [D, c.PQ])
    din("keysT", [128, c.HP, 128])
    din("uT", [D, c.NE]); din("vtab", [c.NE, D])
    din("cosA", [128, c.SS]); din("sinA", [128, c.SS]); din("cosO", [128, c.NOWN]); din("sinO", [128, c.NOWN])
    din("rotR", [128, 128])
    din("dftP_c", [c.SP, c.OP], BF16); din("dftP_s", [c.SP, c.OP], BF16)
    din("dftS_c", [c.SS, c.OS], BF16); din("dftS_s", [c.SS, c.OS], BF16)
    din("dftC_c", [c.FGD, c.FGD], BF16); din("dftC_s", [c.FGD, c.FGD], BF16)
    T["y"] = nc.dram_tensor("y", [c.NOWN, D], F32, kind="ExternalOutput").ap()
    dscr("G1", [2, D], F32); dscr("G2", [2, D], F32)
    dscr("KT", [c.NKV, 128, c.NALL], BF16); dscr("V", [c.NALL, c.KVW], BF16); dscr("U", [c.NALL, c.FW], BF16)
    dscr("QT", [c.NH, 128, c.NOWN], BF16); dscr("GAT", [D, c.NOWN], BF16); dscr("GFT", [D, c.NOWN], BF16)
    dscr("ATT_T", [c.ATT, c.NOWN], BF16); dscr("YT", [c.FW, c.NOWN], BF16)
    dscr("X1", [c.NOWN, D], F32); dscr("H2T", [D, c.NOWN], BF16)
    dscr("EB", [c.NOWN, c.HP, 128], BF16); dscr("EPSD", [c.NOWN, c.PH], F32); dscr("RZD", [c.NOWN, c.PH], F32)
    dscr("PO", [c.NOWN, D], F32)

    V_, A_, G_, PE_ = nc.vector, nc.scalar, nc.gpsimd, nc.tensor
    top = contextlib.ExitStack()
    S = Sync(nc, top)

    def mm(out, lhsT, rhs, st, sp):
        return lambda: PE_.matmul(out, lhsT=lhsT, rhs=rhs, start=st, stop=sp)

    def load_w(Wb, Wd, r0, nrows, c0, ncols, q="pool"):
        kcs = nrows // 128
        step = 8
        for k0 in range(0, kcs, step):
            k1 = min(kcs, k0 + step)
            S.dma(q, Wb[:, k0:k1, 0:ncols],
                  Wd[r0 + k0 * 128:r0 + k1 * 128, c0:c0 + ncols].rearrange("(c p) n -> p c n", p=128),
                  Wb, writes=[Wb])

    ident = S.sbuf(top, "ident", [128, 128], BF16)
    identf = S.sbuf(top, "identf", [128, 128], F32)
    onesf = S.sbuf(top, "onesf", [128, 128], F32)
    onesb = S.sbuf(top, "onesb", [128, 128], BF16)
    rotR = S.sbuf(top, "rotR", [128, 128], F32)
    qg = S.sbuf(top, "qg", [128, 1], F32)
    kg = S.sbuf(top, "kg", [128, 1], F32)
    A1 = S.sbuf(top, "A1", [128, 2, KC], F32); B1 = S.sbuf(top, "B1", [128, 2, KC], F32)
    A2 = S.sbuf(top, "A2", [128, 2, KC], F32); B2 = S.sbuf(top, "B2", [128, 2, KC], F32)
    S.op("pool", lambda: G_.memset(identf[:], 1.0), writes=[identf])
    S.op("pool", lambda: G_.affine_select(out=identf[:], in_=identf[:], pattern=[[-1, 128]], compare_op=ALU.is_equal,
                                          fill=0.0, base=0, channel_multiplier=1), reads=[identf], writes=[identf])
    S.op("dve", lambda: V_.tensor_copy(out=ident[:], in_=identf[:]), reads=[identf], writes=[ident])
    S.op("dve", lambda: V_.memset(onesf[:], 1.0), writes=[onesf])
    S.op("dve", lambda: V_.memset(onesb[:], 1.0), writes=[onesb])
    S.dma("sp", rotR[:], T["rotR"], rotR, writes=[rotR])
    S.dma("sp", qg[:], T["qg"], qg, writes=[qg])
    S.dma("sp", kg[:], T["kg"], kg, writes=[kg])

    def stage_mod():
        st = contextlib.ExitStack()
        cTf = S.sbuf(st, "cTf", [128, KC, 2], F32)
        cs = S.sbuf(st, "cs", [128, KC, 2], BF16)
        Wb = Rot([S.sbuf(st, "Wm%d" % i, [128, KC, 512], BF16) for i in range(2)])
        psr = Rot([S.psum(st, "psr%d" % i, [128, 512]) for i in range(2)])
        psc = Rot([S.psum(st, "psc%d" % i, [128, 4]) for i in range(2)])
        brow = Rot([S.sbuf(st, "brow%d" % i, [1, 512], F32) for i in range(2)])
        grow = Rot([S.sbuf(st, "grow%d" % i, [1, 512], F32) for i in range(2)])
        row = Rot([S.sbuf(st, "row%d" % i, [1, 512], F32) for i in range(4)])
        S.dma("sp", cTf[:], T["cT"], cTf, writes=[cTf])
        S.op("act", lambda: A_.activation(out=cs[:], in_=cTf[:], func=AF.Silu), reads=[cTf], writes=[cs])
        ntile = 6 * D // 512
        for j in range(ntile):
            W = Wb.next()
            load_w(W, T["w_ada"], 0, D, j * 512, 512)
            kind = (j * 512) // D
            off = (j * 512) % D
            bb = brow.next()
            S.dma("sp", bb[:], T["b_ada"][0:1, j * 512:(j + 1) * 512], bb, writes=[bb])
            gg = None
            if kind in (1, 4):
                gg = grow.next()
                S.dma("sp", gg[:], T["n1g" if kind == 1 else "n2g"][0:1, off:off + 512], gg, writes=[gg])
            for grp in range(2):
                ps = psr.next()
                S.pe([mm(ps[0:1, :], cs[:, kc, grp:grp + 1], W[:, kc, :], kc == 0, kc == KC - 1) for kc in range(KC)],
                     reads=[cs, W], writes=[ps])
                r = row.next()
                S.op("dve", lambda: V_.tensor_tensor(out=r[:], in0=ps[0:1, :], in1=bb[:], op=ALU.add), reads=[ps, bb], writes=[r])
                if kind in (2, 5):
                    S.dma("sp", T["G1" if kind == 2 else "G2"][grp:grp + 1, off:off + 512], r[:], r, reads=[r])
                    continue
                if kind in (1, 4):
                    S.op("dve", lambda: V_.scalar_tensor_tensor(out=r[:], in0=r[:], scalar=1.0, in1=gg[:], op0=ALU.add, op1=ALU.mult),
                         reads=[r, gg], writes=[r])
                pc = psc.next()
                S.pe([mm(pc[:, q:q + 1], r[0:1, q * 128:(q + 1) * 128], onesf[0:1, 0:1], True, True) for q in range(4)],
                     reads=[r, onesf], writes=[pc])
                dst = {0: B1, 1: A1, 3: B2, 4: A2}[kind]
                k0 = off // 128
                S.op("dve", lambda: V_.tensor_copy(out=dst[:, grp, k0:k0 + 4], in_=pc[:]), reads=[pc], writes=[dst])
        S.barrier()
        st.close()

    G8 = min(8, KC)

    def alloc_prologue(st):
        P = {}
        P["xs"] = Rot([S.sbuf(st, "xs%d" % i, [128, D], F32) for i in range(2)])
        P["ss"] = Rot([S.sbuf(st, "ss%d" % i, [128, 1], F32) for i in range(2)])
        P["xn"] = Rot([S.sbuf(st, "xn%d" % i, [128, D], BF16) for i in range(2)])
        P["ptr"] = Rot([S.psum(st, "ptr%d" % i, [128, G8, 128], BF16) for i in range(2)])
        return P

    def prologue(P, xsrc, tok0, grp, Acol, Bcol, hTs):
        for sub in range(NSUB):
            xb = P["xs"].next()
            S.dma("sp", xb[:], xsrc[tok0 + sub * 128:tok0 + (sub + 1) * 128, :], xb, writes=[xb])
            ssb = P["ss"].next()
            S.op("dve", lambda: V_.memset(ssb[:], 0.0), writes=[ssb])
            xnb = P["xn"].next()
            S.op("act", lambda: A_.activation(out=xnb[:], in_=xb[:], func=AF.Square, accum_out=ssb[:]),
                 reads=[xb, ssb], writes=[xnb, ssb])
            S.op("dve", lambda: V_.tensor_scalar(out=ssb[:], in0=ssb[:], scalar1=1.0 / D, scalar2=c.EPS, op0=ALU.mult, op1=ALU.add),
                 reads=[ssb], writes=[ssb])
            S.op("act", lambda: A_.sqrt(out=ssb[:], in_=ssb[:]), reads=[ssb], writes=[ssb])
            S.op("dve", lambda: V_.reciprocal(out=ssb[:], in_=ssb[:]), reads=[ssb], writes=[ssb])
            S.op("dve", lambda: V_.tensor_scalar(out=xnb[:], in0=xb[:], scalar1=ssb[:, 0:1], scalar2=None, op0=ALU.mult),
                 reads=[xb, ssb], writes=[xnb])
            hb = hTs[sub]
            for kg_ in range(KC // G8):
                pt = P["ptr"].next()
                S.pe([(lambda j=j: PE_.transpose(out=pt[:, j, :], in_=xnb[:, (kg_ * G8 + j) * 128:(kg_ * G8 + j + 1) * 128], identity=ident[:]))
                      for j in range(G8)], reads=[xnb, ident], writes=[pt])
                for j in range(G8):
                    kc = kg_ * G8 + j
                    o = hb[:, kc, sub * 128:(sub + 1) * 128]
                    if kg_ % 2 == 0:
                        S.op("act", lambda: A_.activation(out=o, in_=pt[:, j, :], func=AF.Identity,
                                                          scale=Acol[:, grp, kc:kc + 1], bias=Bcol[:, grp, kc:kc + 1]),
                             reads=[pt, Acol, Bcol], writes=[hb])
                    else:
                        S.op("dve", lambda: V_.tensor_scalar(out=o, in0=pt[:, j, :], scalar1=Acol[:, grp, kc:kc + 1],
                                                             scalar2=Bcol[:, grp, kc:kc + 1], op0=ALU.mult, op1=ALU.add),
                             reads=[pt, Acol, Bcol], writes=[hb])

    def alloc_rope(st):
        Rp = {}
        Rp["xsb"] = Rot([S.sbuf(st, "rxs%d" % i, [128, TT], F32) for i in range(2)])
        Rp["sqb"] = Rot([S.sbuf(st, "rsq%d" % i, [128, TT], F32) for i in range(1)])
        Rp["rsb"] = Rot([S.sbuf(st, "rrs%d" % i, [128, TT], F32) for i in range(1)])
        Rp["t1"] = Rot([S.sbuf(st, "rt1%d" % i, [128, TT], F32) for i in range(1)])
        Rp["t2"] = Rot([S.sbuf(st, "rt2%d" % i, [128, TT], F32) for i in range(1)])
        Rp["pss"] = Rot([S.psum(st, "pss%d" % i, [128, TT]) for i in range(1)])
        Rp["psw"] = Rot([S.psum(st, "psw%d" % i, [128, TT]) for i in range(1)])
        return Rp

    def rope_epi(Rp, ps, gcol, cosb, sinb, ob):
        xsb, sqb, rsb, t1, t2 = Rp["xsb"].next(), Rp["sqb"].next(), Rp["rsb"].next(), Rp["t1"].next(), Rp["t2"].next()
        pss, psw = Rp["pss"].next(), Rp["psw"].next()
        S.op("act", lambda: A_.activation(out=xsb[:], in_=ps[:, 0:TT], func=AF.Identity, scale=gcol[:, 0:1]), reads=[ps, gcol], writes=[xsb])
        S.op("act", lambda: A_.activation(out=sqb[:], in_=ps[:, 0:TT], func=AF.Square), reads=[ps], writes=[sqb])
        S.pe([mm(pss[:], onesf[:], sqb[:], True, True)], reads=[onesf, sqb], writes=[pss])
        S.pe([mm(psw[:], rotR[:], xsb[:], True, True)], reads=[rotR, xsb], writes=[psw])
        S.op("dve", lambda: V_.tensor_scalar(out=rsb[:], in0=pss[:], scalar1=1.0 / 128, scalar2=c.EPS, op0=ALU.mult, op1=ALU.add),
             reads=[pss], writes=[rsb])
        S.op("act", lambda: A_.sqrt(out=rsb[:], in_=rsb[:]), reads=[rsb], writes=[rsb])
        S.op("dve", lambda: V_.reciprocal(out=rsb[:], in_=rsb[:]), reads=[rsb], writes=[rsb])
        S.op("dve", lambda: V_.tensor_tensor(out=t1[:], in0=xsb[:], in1=cosb[:], op=ALU.mult), reads=[xsb, cosb], writes=[t1])
        S.op("dve", lambda: V_.tensor_tensor(out=t2[:], in0=psw[:], in1=sinb[:], op=ALU.mult), reads=[psw, sinb], writes=[t2])
        S.op("dve", lambda: V_.tensor_tensor(out=t1[:], in0=t1[:], in1=t2[:], op=ALU.add), reads=[t1, t2], writes=[t1])
        S.op("dve", lambda: V_.tensor_tensor(out=ob[:], in0=t1[:], in1=rsb[:], op=ALU.mult), reads=[t1, rsb], writes=[ob])

    def stage_kvf():
        st = contextlib.ExitStack()
        P = alloc_prologue(st)
        Rp = alloc_rope(st)
        hTt = [S.raw_sbuf(st, "hT%d" % i, [128, KC, TT], BF16) for i in range(1)]
        hTr = Rot([[Buf("hT%d_%d" % (i, s_), hTt[i]) for s_ in range(NSUB)] for i in range(1)])
        Wb = Rot([S.sbuf(st, "W%d" % i, [128, KC, 512], BF16) for i in range(2)])
        psg = Rot([S.psum(st, "psg%d" % i, [128, 512]) for i in range(2)])
        cosb = Rot([S.sbuf(st, "cos%d" % i, [128, TT], F32) for i in range(2)])
        sinb = Rot([S.sbuf(st, "sin%d" % i, [128, TT], F32) for i in range(2)])
        ko = Rot([S.sbuf(st, "ko%d" % i, [128, TT], BF16) for i in range(2)])
        vo = Rot([S.sbuf(st, "vo%d" % i, [128, 512], BF16) for i in range(3)])
        for (slen, aoff, ooff, ocnt, grp) in c.seqs:
            for tt in range(slen // TT):
                tok0 = aoff + tt * TT
                pos0 = tt * TT
                hTs = hTr.next()
                prologue(P, T["xall"], tok0, grp, A1, B1, hTs)
                hT = hTs[0]
                cb, sb = cosb.next(), sinb.next()
                S.dma("sp", cb[:], T["cosA"][:, pos0:pos0 + TT], cb, writes=[cb])
                S.dma("sp", sb[:], T["sinA"][:, pos0:pos0 + TT], sb, writes=[sb])
                for w0 in range(0, c.KVW, 512):
                    wn = min(512, c.KVW - w0)
                    W = Wb.next()
                    load_w(W, T["w_in"], 0, D, c.cK + w0, wn)
                    for hb in range(wn // 128):
                        g = (w0 + hb * 128) // 128
                        ps = psg.next()
                        S.pe([mm(ps[:, 0:TT], W[:, kc, hb * 128:(hb + 1) * 128], hT[:, kc, :], kc == 0, kc == KC - 1) for kc in range(KC)],
                             reads=[W] + hTs, writes=[ps])
                        ob = ko.next()
                        rope_epi(Rp, ps, kg, cb, sb, ob)
                        S.dma("sp", T["KT"][g, :, tok0:tok0 + TT], ob[:], ob, reads=[ob])
                for (c0, width, dst) in ((c.cV, c.KVW, "V"), (c.cF, c.FW, "U")):
                    for w0 in range(0, width, 512):
                        wn = min(512, width - w0)
                        W = Wb.next()
                        load_w(W, T["w_in"], 0, D, c0 + w0, wn)
                        for sub in range(NSUB):
                            ps = psg.next()
                            S.pe([mm(ps[:, 0:wn], hT[:, kc, sub * 128:(sub + 1) * 128], W[:, kc, 0:wn], kc == 0, kc == KC - 1) for kc in range(KC)],
                                 reads=[W, hTs[sub]], writes=[ps])
                            ob = vo.next()
                            S.op("act", lambda: A_.activation(out=ob[:, 0:wn], in_=ps[:, 0:wn], func=AF.Copy), reads=[ps], writes=[ob])
                            S.dma("sp", T[dst][tok0 + sub * 128:tok0 + (sub + 1) * 128, w0:w0 + wn], ob[:, 0:wn], ob, reads=[ob])
        S.barrier()
        st.close()

    def stage_qg():
        st = contextlib.ExitStack()
        P = alloc_prologue(st)
        Rp = alloc_rope(st)
        hTt = [S.raw_sbuf(st, "hT%d" % i, [128, KC, TT], BF16) for i in range(1)]
        hTr = Rot([[Buf("hT%d_%d" % (i, s_), hTt[i]) for s_ in range(NSUB)] for i in range(1)])
        Wb = Rot([S.sbuf(st, "W%d" % i, [128, KC, 512], BF16) for i in range(2)])
        psg = Rot([S.psum(st, "psg%d" % i, [128, 512]) for i in range(2)])
        cosb = Rot([S.sbuf(st, "cos%d" % i, [128, TT], F32) for i in range(2)])
        sinb = Rot([S.sbuf(st, "sin%d" % i, [128, TT], F32) for i in range(2)])
        ko = Rot([S.sbuf(st, "ko%d" % i, [128, TT], BF16) for i in range(3)])
        for (slen, aoff, ooff, ocnt, grp) in c.seqs:
            for tt in range(ocnt // TT):
                tok0 = ooff + tt * TT
                hTs = hTr.next()
                prologue(P, T["xown"], tok0, grp, A1, B1, hTs)
                hT = hTs[0]
                cb, sb = cosb.next(), sinb.next()
                S.dma("sp", cb[:], T["cosO"][:, tok0:tok0 + TT], cb, writes=[cb])
                S.dma("sp", sb[:], T["sinO"][:, tok0:tok0 + TT], sb, writes=[sb])
                for w0 in range(0, c.ATT, 512):
                    wn = min(512, c.ATT - w0)
                    W = Wb.next()
                    load_w(W, T["w_in"], 0, D, c.cQ + w0, wn)
                    for hb in range(wn // 128):
                        h = (w0 + hb * 128) // 128
                        ps = psg.next()
                        S.pe([mm(ps[:, 0:TT], W[:, kc, hb * 128:(hb + 1) * 128], hT[:, kc, :], kc == 0, kc == KC - 1) for kc in range(KC)],
                             reads=[W] + hTs, writes=[ps])
                        ob = ko.next()
                        rope_epi(Rp, ps, qg, cb, sb, ob)
                        S.dma("sp", T["QT"][h, :, tok0:tok0 + TT], ob[:], ob, reads=[ob])
                for (c0, dst) in ((c.cGA, "GAT"), (c.cGF, "GFT")):
                    for w0 in range(0, D, 512):
                        wn = min(512, D - w0)
                        W = Wb.next()
                        load_w(W, T["w_in"], 0, D, c0 + w0, wn)
                        for hb in range(wn // 128):
                            ps = psg.next()
                            S.pe([mm(ps[:, 0:TT], W[:, kc, hb * 128:(hb + 1) * 128], hT[:, kc, :], kc == 0, kc == KC - 1) for kc in range(KC)],
                                 reads=[W] + hTs, writes=[ps])
                            ob = ko.next()
                            S.op("act", lambda: A_.activation(out=ob[:], in_=ps[:, 0:TT], func=AF.Sigmoid), reads=[ps], writes=[ob])
                            r0 = w0 + hb * 128
                            S.dma("sp", T[dst][r0:r0 + 128, tok0:tok0 + TT], ob[:], ob, reads=[ob])
        S.barrier()
        st.close()

    def stage_attn():
        st = contextlib.ExitStack()
        SMAX = max(c.SP, c.SS)
        QB = TT
        KTb = Rot([S.sbuf(st, "KTb%d" % i, [128, SMAX], BF16) for i in range(2)])
        Vb = Rot([S.sbuf(st, "Vb%d" % i, [128, SMAX // 128, 128], BF16) for i in range(2)])
        Qb = Rot([S.sbuf(st, "Qb%d" % i, [128, QB], BF16) for i in range(2)])
        pT = Rot([S.sbuf(st, "pT%d" % i, [128, QB], BF16) for i in range(3)])
        pss = Rot([S.psum(st, "pss%d" % i, [128, QB]) for i in range(2)])
        pso = Rot([S.psum(st, "pso%d" % i, [128, QB]) for i in range(2)])
        psl = Rot([S.psum(st, "psl%d" % i, [128, QB]) for i in range(2)])
        rl = Rot([S.sbuf(st, "rl%d" % i, [128, QB], F32) for i in range(2)])
        ob_ = Rot([S.sbuf(st, "ao%d" % i, [128, QB], BF16) for i in range(2)])
        scale = 128 ** -0.5
        for (slen, aoff, ooff, ocnt, grp) in c.seqs:
            nkc = slen // 128
            for g in range(c.NKV):
                Kt, Vt = KTb.next(), Vb.next()
                S.dma("sp", Kt[:, 0:slen], T["KT"][g, :, aoff:aoff + slen], Kt, writes=[Kt])
                for k0 in range(0, nkc, 16):
                    k1 = min(nkc, k0 + 16)
                    S.dma("sp", Vt[:, k0:k1, :],
                          T["V"][aoff + k0 * 128:aoff + k1 * 128, g * 128:(g + 1) * 128].rearrange("(c p) d -> p c d", p=128),
                          Vt, writes=[Vt])
                for qh in range(c.NH // c.NKV):
                    h = g * (c.NH // c.NKV) + qh
                    for qb in range(ocnt // QB):
                        q0 = ooff + qb * QB
                        Qt = Qb.next()
                        S.dma("sp", Qt[:], T["QT"][h, :, q0:q0 + QB], Qt, writes=[Qt])
                        po, pl = pso.next(), psl.next()
                        pend = None
                        for kc in range(nkc + 1):
                            if kc < nkc:
                                ps = pss.next()
                                S.pe([mm(ps[:], Kt[:, kc * 128:(kc + 1) * 128], Qt[:], True, True)], reads=[Kt, Qt], writes=[ps])
                                p = pT.next()
                                S.op("act", lambda: A_.activation(out=p[:], in_=ps[:], func=AF.Exp, scale=scale), reads=[ps], writes=[p])
                            if pend is not None:
                                pk, pp = pend
                                S.pe([mm(po[:], Vt[:, pk, :], pp[:], pk == 0, pk == nkc - 1),
                                      mm(pl[:], onesb[:], pp[:], pk == 0, pk == nkc - 1)], reads=[Vt, pp, onesb], writes=[po, pl])
                            pend = (kc, p) if kc < nkc else None
                        r_, o_ = rl.next(), ob_.next()
                        S.op("dve", lambda: V_.reciprocal(out=r_[:], in_=pl[:]), reads=[pl], writes=[r_])
                        S.op("dve", lambda: V_.tensor_tensor(out=o_[:], in0=po[:], in1=r_[:], op=ALU.mult), reads=[po, r_], writes=[o_])
                        S.dma("sp", T["ATT_T"][h * 128:(h + 1) * 128, q0:q0 + QB], o_[:], o_, reads=[o_])
        S.barrier()
        st.close()

    def stage_dft():
        st = contextlib.ExitStack()
        NB = c.FW // 128
        GS = min(3, NB)
        SCB = 16
        GC = c.FGD // 128
        SB = TT
        Ct = Rot([S.sbuf(st, "Ct%d" % i, [128, SCB, SB], BF16) for i in range(2)])
        St_ = Rot([S.sbuf(st, "St%d" % i, [128, SCB, SB], BF16) for i in range(2)])
        Ut = Rot([S.sbuf(st, "Ut%d" % i, [128, SCB, GS * 128], BF16) for i in range(2)])
        P1 = S.sbuf(st, "P1", [128, NB, SB], BF16)
        P2 = S.sbuf(st, "P2", [128, NB, SB], BF16)
        CC = S.sbuf(st, "CC", [128, GC, c.FGD], BF16)
        SC = S.sbuf(st, "SC", [128, GC, c.FGD], BF16)
        acc = [S.psum(st, "dacc%d" % i, [128, SB]) for i in range(2 * GS)]
        psy = Rot([S.psum(st, "psy%d" % i, [128, SB]) for i in range(2)])
        yo = Rot([S.sbuf(st, "yo%d" % i, [128, SB], BF16) for i in range(2)])
        S.dma("sp", CC[:], T["dftC_c"].rearrange("(c p) n -> p c n", p=128), CC, writes=[CC])
        S.dma("sp", SC[:], T["dftC_s"].rearrange("(c p) n -> p c n", p=128), SC, writes=[SC])
        for si, (slen, aoff, ooff, ocnt, grp) in enumerate(c.seqs):
            tc_, ts_ = (T["dftP_c"], T["dftP_s"]) if si == 0 else (T["dftS_c"], T["dftS_s"])
            nkc = slen // 128
            for sb in range(ocnt // SB):
                for b0 in range(0, NB, GS):
                    gs = min(GS, NB - b0)
                    for sc0 in range(0, nkc, SCB):
                        sc1 = min(nkc, sc0 + SCB)
                        n = sc1 - sc0
                        ct, stt, ut = Ct.next(), St_.next(), Ut.next()
                        S.dma("sp", ct[:, 0:n, :], tc_[sc0 * 128:sc1 * 128, sb * SB:(sb + 1) * SB].rearrange("(c p) n -> p c n", p=128), ct, writes=[ct])
                        S.dma("sp", stt[:, 0:n, :], ts_[sc0 * 128:sc1 * 128, sb * SB:(sb + 1) * SB].rearrange("(c p) n -> p c n", p=128), stt, writes=[stt])
                        S.dma("sp", ut[:, 0:n, 0:gs * 128],
                              T["U"][aoff + sc0 * 128:aoff + sc1 * 128, b0 * 128:(b0 + gs) * 128].rearrange("(c p) n -> p c n", p=128),
                              ut, writes=[ut])
                        fns = []
                        for ci in range(n):
                            first, last = (sc0 + ci == 0), (sc0 + ci == nkc - 1)
                            for b in range(gs):
                                fns.append(mm(acc[2 * b][:], ut[:, ci, b * 128:(b + 1) * 128], ct[:, ci, :], first, last))
                                fns.append(mm(acc[2 * b + 1][:], ut[:, ci, b * 128:(b + 1) * 128], stt[:, ci, :], first, last))
                        S.pe(fns, reads=[ct, stt, ut], writes=acc[0:2 * gs])
                    for b in range(gs):
                        S.op("act", lambda: A_.activation(out=P1[:, b0 + b, :], in_=acc[2 * b][:], func=AF.Copy), reads=[acc[2 * b]], writes=[P1])
                        S.op("dve", lambda: V_.tensor_copy(out=P2[:, b0 + b, :], in_=acc[2 * b + 1][:]), reads=[acc[2 * b + 1]], writes=[P2])
                for ob in range(NB):
                    gi, ol = ob // GC, ob % GC
                    ps = psy.next()
                    fns = []
                    for cc in range(GC):
                        fns.append(mm(ps[:], CC[:, cc, ol * 128:(ol + 1) * 128], P1[:, gi * GC + cc, :], cc == 0, False))
                        fns.append(mm(ps[:], SC[:, cc, ol * 128:(ol + 1) * 128], P2[:, gi * GC + cc, :], False, cc == GC - 1))
                    S.pe(fns, reads=[CC, SC, P1, P2], writes=[ps])
                    y_ = yo.next()
                    S.op("act", lambda: A_.activation(out=y_[:], in_=ps[:], func=AF.Copy), reads=[ps], writes=[y_])
                    q0 = ooff + sb * SB
                    S.dma("sp", T["YT"][ob * 128:(ob + 1) * 128, q0:q0 + SB], y_[:], y_, reads=[y_])
        S.barrier()
        st.close()

    def stage_merge():
        st = contextlib.ExitStack()
        WC = 256
        KA, KF = c.ATT // 128, c.FW // 128
        atT = S.sbuf(st, "atT", [128, KA, TT], BF16)
        ytT = S.sbuf(st, "ytT", [128, KF, TT], BF16)
        mTt = S.raw_sbuf(st, "mT", [128, KC, TT], BF16)
        mT = Buf("mT", mTt)
        Wa = Rot([S.sbuf(st, "Wa%d" % i, [128, max(KA, KC), WC], BF16) for i in range(2)])
        Wf = Rot([S.sbuf(st, "Wf%d" % i, [128, KF, WC], BF16) for i in range(2)])
        gat = Rot([S.sbuf(st, "gat%d" % i, [128, WC // 128, TT], BF16) for i in range(2)])
        gft = Rot([S.sbuf(st, "gft%d" % i, [128, WC // 128, TT], BF16) for i in range(2)])
        g1bc = S.sbuf(st, "g1bc", [128, D], F32)
        psA = Rot([S.psum(st, "psA%d" % i, [128, 512]) for i in range(2)])
        psB = Rot([S.psum(st, "psB%d" % i, [128, 512]) for i in range(2)])
        psO = Rot([S.psum(st, "psO%d" % i, [128, 512]) for i in range(2)])
        ta = Rot([S.sbuf(st, "ta%d" % i, [128, TT], F32) for i in range(2)])
        tb = Rot([S.sbuf(st, "tb%d" % i, [128, TT], F32) for i in range(2)])
        xp = Rot([S.sbuf(st, "xp%d" % i, [128, WC], F32) for i in range(3)])
        xo = Rot([S.sbuf(st, "xo%d" % i, [128, WC], F32) for i in range(3)])
        for (slen, aoff, ooff, ocnt, grp) in c.seqs:
            S.dma("sp", g1bc[:], T["G1"][grp:grp + 1, :].partition_broadcast(128), g1bc, writes=[g1bc])
            for tt in range(ocnt // TT):
                tok0 = ooff + tt * TT
                S.dma("sp", atT[:], T["ATT_T"][:, tok0:tok0 + TT].rearrange("(c p) n -> p c n", p=128), atT, writes=[atT])
                S.dma("sp", ytT[:], T["YT"][:, tok0:tok0 + TT].rearrange("(c p) n -> p c n", p=128), ytT, writes=[ytT])
                for w0 in range(0, D, WC):
                    wa, wf, ga_, gf_ = Wa.next(), Wf.next(), gat.next(), gft.next()
                    load_w(wa, T["w_attn"], 0, c.ATT, w0, WC)
                    load_w(wf, T["w_four"], 0, c.FW, w0, WC)
                    S.dma("sp", ga_[:], T["GAT"][w0:w0 + WC, tok0:tok0 + TT].rearrange("(c p) n -> p c n", p=128), ga_, writes=[ga_])
                    S.dma("sp", gf_[:], T["GFT"][w0:w0 + WC, tok0:tok0 + TT].rearrange("(c p) n -> p c n", p=128), gf_, writes=[gf_])
                    for blk in range(WC // 128):
                        pa, pb = psA.next(), psB.next()
                        S.pe([mm(pa[:, 0:TT], wa[:, kc, blk * 128:(blk + 1) * 128], atT[:, kc, :], kc == 0, kc == KA - 1) for kc in range(KA)],
                             reads=[wa, atT], writes=[pa])
                        S.pe([mm(pb[:, 0:TT], wf[:, kc, blk * 128:(blk + 1) * 128], ytT[:, kc, :], kc == 0, kc == KF - 1) for kc in range(KF)],
                             reads=[wf, ytT], writes=[pb])
                        t1, t2 = ta.next(), tb.next()
                        S.op("dve", lambda: V_.tensor_tensor(out=t1[:], in0=pa[:, 0:TT], in1=ga_[:, blk, :], op=ALU.mult), reads=[pa, ga_], writes=[t1])
                        S.op("dve", lambda: V_.tensor_tensor(out=t2[:], in0=pb[:, 0:TT], in1=gf_[:, blk, :], op=ALU.mult), reads=[pb, gf_], writes=[t2])
                        ob = (w0 + blk * 128) // 128
                        S.op("pool", lambda: G_.tensor_tensor(out=mT[:, ob, :], in0=t1[:], in1=t2[:], op=ALU.add), reads=[t1, t2], writes=[mT])
                for w0 in range(0, D, WC):
                    wo = Wa.next()
                    load_w(wo, T["w_out"], 0, D, w0, WC)
                    for sub in range(NSUB):
                        r0 = tok0 + sub * 128
                        ps = psO.next()
                        S.pe([mm(ps[:, 0:WC], mT[:, kc, sub * 128:(sub + 1) * 128], wo[:, kc, 0:WC], kc == 0, kc == KC - 1) for kc in range(KC)],
                             reads=[wo, mT], writes=[ps])
                        xp_, xo_ = xp.next(), xo.next()
                        S.dma("sp", xp_[:], T["xown"][r0:r0 + 128, w0:w0 + WC], xp_, writes=[xp_])
                        S.op("dve", lambda: V_.tensor_tensor(out=xo_[:], in0=ps[:, 0:WC], in1=g1bc[:, w0:w0 + WC], op=ALU.mult), reads=[ps, g1bc], writes=[xo_])
                        S.op("pool", lambda: G_.tensor_tensor(out=xo_[:], in0=xo_[:], in1=xp_[:], op=ALU.add), reads=[xo_, xp_], writes=[xo_])
                        S.dma("sp", T["X1"][r0:r0 + 128, w0:w0 + WC], xo_[:], xo_, reads=[xo_])
        S.barrier()
        st.close()

    def stage_peer_q():
        st = contextlib.ExitStack()
        HP, PH = c.HP, c.PH
        P = alloc_prologue(st)
        hTt = [S.raw_sbuf(st, "hT%d" % i, [128, KC, TT], BF16) for i in range(1)]
        hTr = Rot([[Buf("hT%d_%d" % (i, s_), hTt[i]) for s_ in range(NSUB)] for i in range(1)])
        Wb = Rot([S.sbuf(st, "W%d" % i, [128, KC, 256], BF16) for i in range(2)])
        keysT = S.sbuf(st, "keysT", [128, HP, 128], F32)
        psg = Rot([S.psum(st, "psg%d" % i, [128, 512]) for i in range(2)])
        pssc = Rot([S.psum(st, "pssc%d" % i, [128, 4, 128]) for i in range(2)])
        pqb = Rot([S.sbuf(st, "pqb%d" % i, [128, TT], F32) for i in range(2)])
        sc = [S.sbuf(st, "sc%d" % i, [128, HP, 128], F32) for i in range(NSUB)]
        mx = S.sbuf(st, "mx", [128, HP], F32)
        eb = Rot([S.sbuf(st, "eb%d" % i, [128, HP, 128], BF16) for i in range(2)])
        ef = S.sbuf(st, "ef", [128, HP, 128], F32)
        wk = S.sbuf(st, "wk", [128, 256], F32)
        tv = S.sbuf(st, "tv", [128, HP, 16], F32)
        cand = S.sbuf(st, "cand", [128, PH, 16, 16], F32)
        tcd = S.sbuf(st, "tcd", [128, PH, 16], F32)
        epz = Rot([S.sbuf(st, "epz%d" % i, [128, 2, PH], F32) for i in range(2)])
        S.dma("sp", keysT[:], T["keysT"], keysT, writes=[keysT])
        for (slen, aoff, ooff, ocnt, grp) in c.seqs:
            for tt in range(ocnt // TT):
                tok0 = ooff + tt * TT
                hTs = hTr.next()
                prologue(P, T["X1"], tok0, grp, A2, B2, hTs)
                hT = hTs[0]
                S.dma("sp", T["H2T"][:, tok0:tok0 + TT].rearrange("(c p) n -> p c n", p=128), hT[:], hT, reads=hTs)
                for w0 in range(0, c.PQ, 256):
                    wn = min(256, c.PQ - w0)
                    W = Wb.next()
                    load_w(W, T["w_pq"], 0, D, w0, wn)
                    for blk in range(wn // 128):
                        hp = (w0 + blk * 128) // 128
                        ps = psg.next()
                        S.pe([mm(ps[:, 0:TT], W[:, kc, blk * 128:(blk + 1) * 128], hT[:, kc, :], kc == 0, kc == KC - 1) for kc in range(KC)],
                             reads=[W] + hTs, writes=[ps])
                        pq = pqb.next()
                        S.op("act", lambda: A_.activation(out=pq[:], in_=ps[:, 0:TT], func=AF.Copy), reads=[ps], writes=[pq])
                        pk = pssc.next()
                        S.pe([mm(pk[:, sub, :], pq[:, sub * 128:(sub + 1) * 128], keysT[:, hp, :], True, True) for sub in range(NSUB)],
                             reads=[pq, keysT], writes=[pk])
                        for sub in range(NSUB):
                            S.op("dve", lambda: V_.tensor_copy(out=sc[sub][:, hp, :], in_=pk[:, sub, :]), reads=[pk], writes=[sc[sub]])
                for sub in range(NSUB):
                    s_ = sc[sub]
                    r0 = tok0 + sub * 128
                    S.op("dve", lambda: V_.tensor_reduce(out=mx[:], in_=s_[:], axis=AX.X, op=ALU.max), reads=[s_], writes=[mx])
                    S.op("dve", lambda: V_.tensor_tensor(out=s_[:], in0=s_[:], in1=mx[:].unsqueeze(2).broadcast_to([128, HP, 128]), op=ALU.subtract),
                         reads=[s_, mx], writes=[s_])
                    e_ = eb.next()
                    S.op("act", lambda: A_.activation(out=e_[:], in_=s_[:], func=AF.Exp), reads=[s_], writes=[e_])
                    S.dma("sp", T["EB"][r0:r0 + 128, :, :], e_[:], e_, reads=[e_])
                    S.op("dve", lambda: V_.tensor_copy(out=ef[:], in_=e_[:]), reads=[e_], writes=[ef])
                    for hp in range(HP):
                        S.op("dve", lambda: V_.max(out=tv[:, hp, 0:8], in_=ef[:, hp, :]), reads=[ef], writes=[tv])
                        S.op("dve", lambda: V_.match_replace(out=wk[:, 0:128], in_to_replace=tv[:, hp, 0:8], in_values=ef[:, hp, :], imm_value=-1.0),
                             reads=[ef, tv], writes=[wk])
                        S.op("dve", lambda: V_.max(out=tv[:, hp, 8:16], in_=wk[:, 0:128]), reads=[wk], writes=[tv])
                    tvv = tv[:].rearrange("p (h two) k -> p h two k", two=2)
                    S.op("dve", lambda: V_.tensor_tensor(out=cand[:], in0=tvv[:, :, 0, :].unsqueeze(3).broadcast_to([128, PH, 16, 16]),
                                                         in1=tvv[:, :, 1, :].unsqueeze(2).broadcast_to([128, PH, 16, 16]), op=ALU.mult),
                         reads=[tv], writes=[cand])
                    for h in range(PH):
                        cv = cand[:, h, :, :].rearrange("p a b -> p (a b)")
                        S.op("dve", lambda: V_.max(out=tcd[:, h, 0:8], in_=cv), reads=[cand], writes=[tcd])
                        S.op("dve", lambda: V_.match_replace(out=wk[:, 0:256], in_to_replace=tcd[:, h, 0:8], in_values=cv, imm_value=-1.0),
                             reads=[cand, tcd], writes=[wk])
                        S.op("dve", lambda: V_.max(out=tcd[:, h, 8:16], in_=wk[:, 0:256]), reads=[wk], writes=[tcd])
                    ez = epz.next()
                    S.op("dve", lambda: V_.tensor_reduce(out=ez[:, 0, :], in_=tcd[:], axis=AX.X, op=ALU.min), reads=[tcd], writes=[ez])
                    S.op("dve", lambda: V_.tensor_reduce(out=ez[:, 1, :], in_=tcd[:], axis=AX.X, op=ALU.add), reads=[tcd], writes=[ez])
                    S.op("dve", lambda: V_.reciprocal(out=ez[:, 1, :], in_=ez[:, 1, :]), reads=[ez], writes=[ez])
                    S.dma("sp", T["EPSD"][r0:r0 + 128, :], ez[:, 0, :], ez, reads=[ez])
                    S.dma("sp", T["RZD"][r0:r0 + 128, :], ez[:, 1, :], ez, reads=[ez])
        S.barrier()
        st.close()

    def stage_peer_e():
        st = contextlib.ExitStack()
        HP, PH = c.HP, c.PH
        SBE = 8
        IB = 2
        h2T = S.sbuf(st, "h2T", [128, KC, TT], BF16)
        ebt = S.sbuf(st, "ebt", [128, NSUB, HP, 128], BF16)
        epst = S.sbuf(st, "epst", [128, NSUB, PH], F32)
        rzt = S.sbuf(st, "rzt", [128, NSUB, PH], F32)
        Dg = S.sbuf(st, "Dg", [128, NSUB * PH, 128], BF16)
        acct = S.raw_sbuf(st, "acc", [128, NSUB, D], F32)
        accs = [Buf("acc%d" % i, acct) for i in range(NSUB)]
        uTt = Rot([S.sbuf(st, "uT%d" % i, [128, KC, 128], BF16) for i in range(2)])
        vt = Rot([S.sbuf(st, "vt%d" % i, [128, SBE, 512], BF16) for i in range(2)])
        AGTt = S.raw_sbuf(st, "AGT", [128, SBE, TT], BF16)
        AGT = [Buf("AGT%d" % i, AGTt) for i in range(SBE)]
        gl = Rot([S.sbuf(st, "gl%d" % i, [128, TT], F32) for i in range(2)])
        Et = Rot([S.sbuf(st, "Et%d" % i, [128, IB, 128], F32) for i in range(3)])
        Gt = [[S.sbuf(st, "G%d_%d" % (s_, h), [128, IB, 128], BF16) for h in range(PH)] for s_ in range(NSUB)]
        psE = Rot([S.psum(st, "psE%d" % i, [128, 512]) for i in range(2)])
        psG = Rot([S.psum(st, "psG%d" % i, [128, 512]) for i in range(2)])
        ps2 = Rot([S.psum(st, "ps2%d" % i, [128, 512]) for i in range(3)])
        for (slen, aoff, ooff, ocnt, grp) in c.seqs:
            for tt in range(ocnt // TT):
                tok0 = ooff + tt * TT
                S.dma("sp", h2T[:], T["H2T"][:, tok0:tok0 + TT].rearrange("(c p) n -> p c n", p=128), h2T, writes=[h2T])
                S.dma("sp", ebt[:], T["EB"][tok0:tok0 + TT, :, :].rearrange("(s p) h n -> p s h n", p=128), ebt, writes=[ebt])
                S.dma("sp", epst[:], T["EPSD"][tok0:tok0 + TT, :].rearrange("(s p) h -> p s h", p=128), epst, writes=[epst])
                S.dma("sp", rzt[:], T["RZD"][tok0:tok0 + TT, :].rearrange("(s p) h -> p s h", p=128), rzt, writes=[rzt])
                for sub in range(NSUB):
                    for h in range(PH):
                        S.op("dve", lambda: V_.tensor_scalar(out=Dg[:, sub * PH + h, :], in0=identf[:], scalar1=rzt[:, sub, h:h + 1],
                                                             scalar2=None, op0=ALU.mult), reads=[identf, rzt], writes=[Dg])
                for sbk in range(c.NE // (128 * SBE)):
                    for ib in range(SBE // IB):
                        i0 = sbk * SBE + ib * IB
                        for sub in range(NSUB):
                            for h in range(PH):
                                E = Et.next()
                                G = Gt[sub][h]
                                S.op("pool", lambda: G_.tensor_tensor(
                                    out=E[:], in0=ebt[:, sub, 2 * h, i0:i0 + IB].unsqueeze(2).broadcast_to([128, IB, 128]),
                                    in1=ebt[:, sub, 2 * h + 1, :].unsqueeze(1).broadcast_to([128, IB, 128]), op=ALU.mult),
                                    reads=[ebt], writes=[E])
                                S.op("dve", lambda: V_.scalar_tensor_tensor(out=G[:], in0=E[:], scalar=epst[:, sub, h:h + 1], in1=E[:],
                                                                            op0=ALU.is_ge, op1=ALU.mult), reads=[E, epst], writes=[G])
                        for ii in range(IB):
                            ci = ib * IB + ii
                            ch = sbk * SBE + ci
                            ut = uTt.next()
                            load_w(ut, T["uT"], 0, D, ch * 128, 128)
                            pe_ = psE.next()
                            S.pe([mm(pe_[:, 0:TT], ut[:, kc, :], h2T[:, kc, :], kc == 0, kc == KC - 1) for kc in range(KC)],
                                 reads=[ut, h2T], writes=[pe_])
                            g_ = gl.next()
                            S.op("act", lambda: A_.activation(out=g_[:], in_=pe_[:, 0:TT], func=AF.Gelu), reads=[pe_], writes=[g_])
                            pg = psG.next()
                            fns = []
                            for sub in range(NSUB):
                                for h in range(PH):
                                    fns.append(mm(pg[:, sub * 128:(sub + 1) * 128], Gt[sub][h][:, ii, :], Dg[:, sub * PH + h, :], h == 0, h == PH - 1))
                            S.pe(fns, reads=[Dg] + [Gt[s_][h] for s_ in range(NSUB) for h in range(PH)], writes=[pg])
                            S.op("dve", lambda: V_.tensor_tensor(out=AGT[ci][:, ci, :], in0=pg[:, 0:TT], in1=g_[:], op=ALU.mult),
                                 reads=[pg, g_], writes=[AGT[ci]])
                    for w0 in range(0, D, 512):
                        wn = min(512, D - w0)
                        v_ = vt.next()
                        S.dma("pool", v_[:, :, 0:wn],
                              T["vtab"][sbk * SBE * 128:(sbk + 1) * SBE * 128, w0:w0 + wn].rearrange("(c p) n -> p c n", p=128),
                              v_, writes=[v_])
                        for sub in range(NSUB):
                            p2 = ps2.next()
                            S.pe([mm(p2[:, 0:wn], AGT[ci][:, ci, sub * 128:(sub + 1) * 128], v_[:, ci, 0:wn], ci == 0, ci == SBE - 1) for ci in range(SBE)],
                                 reads=[v_] + AGT, writes=[p2])
                            a_ = accs[sub]
                            if sbk == 0:
                                S.op("dve", lambda: V_.tensor_copy(out=a_[:, sub, w0:w0 + wn], in_=p2[:, 0:wn]), reads=[p2], writes=[a_])
                            else:
                                S.op("dve", lambda: V_.tensor_tensor(out=a_[:, sub, w0:w0 + wn], in0=a_[:, sub, w0:w0 + wn], in1=p2[:, 0:wn], op=ALU.add),
                                     reads=[p2, a_], writes=[a_])
                for sub in range(NSUB):
                    r0 = tok0 + sub * 128
                    S.dma("sp", T["PO"][r0:r0 + 128, :], accs[sub][:, sub, :], accs[sub], reads=[accs[sub]])
        S.barrier()
        st.close()

    def stage_final():
        st = contextlib.ExitStack()
        g2bc = S.sbuf(st, "g2bc", [128, D], F32)
        fgbc = S.sbuf(st, "fgbc", [128, D], F32)
        x1 = Rot([S.sbuf(st, "x1_%d" % i, [128, D], F32) for i in range(2)])
        po = Rot([S.sbuf(st, "po_%d" % i, [128, D], F32) for i in range(2)])
        junk = S.sbuf(st, "junkf", [128, D], BF16)
        ss = Rot([S.sbuf(st, "ssf%d" % i, [128, 1], F32) for i in range(2)])
        S.dma("sp", fgbc[:], T["fg"][0:1, :].partition_broadcast(128), fgbc, writes=[fgbc])
        for (slen, aoff, ooff, ocnt, grp) in c.seqs:
            S.dma("sp", g2bc[:], T["G2"][grp:grp + 1, :].partition_broadcast(128), g2bc, writes=[g2bc])
            for t in range(ocnt // 128):
                r0 = ooff + t * 128
                a, b = x1.next(), po.next()
                S.dma("sp", a[:], T["X1"][r0:r0 + 128, :], a, writes=[a])
                S.dma("sp", b[:], T["PO"][r0:r0 + 128, :], b, writes=[b])
                S.op("dve", lambda: V_.tensor_tensor(out=b[:], in0=b[:], in1=g2bc[:], op=ALU.mult), reads=[b, g2bc], writes=[b])
                S.op("pool", lambda: G_.tensor_tensor(out=a[:], in0=a[:], in1=b[:], op=ALU.add), reads=[a, b], writes=[a])
                s_ = ss.next()
                S.op("dve", lambda: V_.memset(s_[:], 0.0), writes=[s_])
                S.op("act", lambda: A_.activation(out=junk[:], in_=a[:], func=AF.Square, accum_out=s_[:]), reads=[a, s_], writes=[junk, s_])
                S.op("dve", lambda: V_.tensor_scalar(out=s_[:], in0=s_[:], scalar1=1.0 / D, scalar2=c.EPS, op0=ALU.mult, op1=ALU.add), reads=[s_], writes=[s_])
                S.op("act", lambda: A_.sqrt(out=s_[:], in_=s_[:]), reads=[s_], writes=[s_])
                S.op("dve", lambda: V_.reciprocal(out=s_[:], in_=s_[:]), reads=[s_], writes=[s_])
                S.op("dve", lambda: V_.scalar_tensor_tensor(out=b[:], in0=a[:], scalar=s_[:, 0:1], in1=fgbc[:], op0=ALU.mult, op1=ALU.mult),
                     reads=[a, s_, fgbc], writes=[b])
                S.dma("sp", T["y"][r0:r0 + 128, :], b[:], b, reads=[b])
        S.barrier()
        st.close()

    stages = [stage_mod, stage_kvf, stage_qg, stage_attn, stage_dft, stage_merge, stage_peer_q, stage_peer_e, stage_final]
    for i, sfn in enumerate(stages):
        if i < getattr(c, "nstages", 99):
            sfn()
    S.drain()
    top.close()
    return nc


def _rope_tables(cfg, positions):
    axis_dim = 64
    inv_freq = (10000.0 ** (-np.arange(0, axis_dim, 2, dtype=np.float32) / axis_dim)).astype(np.float32)
    pos = np.asarray(positions)
    row = (pos // cfg.GRID_W).astype(np.float32)
    col = (pos % cfg.GRID_W).astype(np.float32)
    ang = np.concatenate([row[:, None] * inv_freq, col[:, None] * inv_freq], axis=-1).astype(np.float32)
    cos = np.cos(ang).astype(np.float32)
    sin = np.sin(ang).astype(np.float32)
    return np.ascontiguousarray(np.repeat(cos, 2, axis=1).T), np.ascontiguousarray(np.repeat(sin, 2, axis=1).T)


def _dft_tables(S_len, own_pos):
    s = np.arange(S_len, dtype=np.int64)[:, None]
    k = np.asarray(own_pos, dtype=np.int64)[None, :]
    ang = 2.0 * np.pi * ((s * k) % S_len).astype(np.float64) / S_len
    sc = 1.0 / np.sqrt(S_len)
    return (np.cos(ang) * sc).astype(ml_dtypes.bfloat16), (np.sin(ang) * sc).astype(ml_dtypes.bfloat16)


_NC_CACHE = {}


def run(cfg, x_prompt, x_sample, c_prompt, c_sample, w_ada, b_ada, norm1_g, norm2_g, w_in, q_norm_g, k_norm_g,
        w_attn_br, w_four_br, w_out, w_peer_q, peer_keys, peer_u, peer_v, final_g, return_all=False):
    c = cfg
    f32 = np.float32
    A = lambda a: np.ascontiguousarray(np.asarray(a, dtype=f32))
    x_prompt, x_sample = A(x_prompt), A(x_sample)
    key = (c.D, c.SP, c.SS, c.NH, c.NKV, c.FG, c.FGD, c.PH, c.TT, c.debug, getattr(c, "nstages", 99))
    if key not in _NC_CACHE:
        _NC_CACHE[key] = build(c)
    nc = _NC_CACHE[key]
    shared = {
        "w_ada": A(w_ada[0]), "b_ada": A(b_ada[0])[None, :], "n1g": A(norm1_g[0])[None, :], "n2g": A(norm2_g[0])[None, :],
        "fg": A(final_g)[None, :], "w_in": A(w_in[0]), "qg": A(q_norm_g[0]).reshape(128, 1), "kg": A(k_norm_g[0]).reshape(128, 1),
        "w_attn": A(w_attn_br[0]), "w_four": A(w_four_br[0]), "w_out": A(w_out[0]), "w_pq": A(w_peer_q[0]),
        "keysT": np.ascontiguousarray(A(peer_keys[0]).reshape(c.HP, 128, 128).transpose(2, 0, 1)),
        "uT": np.ascontiguousarray(A(peer_u[0]).T), "vtab": A(peer_v[0]),
    }
    R = np.zeros((128, 128), f32)
    for i in range(64):
        R[2 * i + 1, 2 * i] = -1.0
        R[2 * i, 2 * i + 1] = 1.0
    shared["rotR"] = R
    cosA, sinA = _rope_tables(c, np.arange(c.SS))
    shared["cosA"], shared["sinA"] = cosA, sinA
    n = np.arange(c.FGD, dtype=np.int64)
    angc = 2.0 * np.pi * ((n[:, None] * n[None, :]) % c.FGD).astype(np.float64) / c.FGD
    shared["dftC_c"] = (np.cos(angc) / np.sqrt(c.FGD)).astype(ml_dtypes.bfloat16)
    shared["dftC_s"] = (-np.sin(angc) / np.sqrt(c.FGD)).astype(ml_dtypes.bfloat16)
    in_maps = []
    for core in range(8):
        b, r = core // 4, core % 4
        posP = np.arange(r * c.OP, (r + 1) * c.OP)
        posS = np.arange(r * c.OS, (r + 1) * c.OS)
        m = dict(shared)
        m["xall"] = np.concatenate([x_prompt[b], x_sample[b]], axis=0)
        m["xown"] = np.concatenate([x_prompt[b, posP], x_sample[b, posS]], axis=0)
        cT = np.stack([A(c_prompt[b]).reshape(c.KC, 128).T, A(c_sample[b]).reshape(c.KC, 128).T], axis=-1)
        m["cT"] = np.ascontiguousarray(cT)
        cp, sp_ = _rope_tables(c, posP)
        cs_, ss_ = _rope_tables(c, posS)
        m["cosO"] = np.ascontiguousarray(np.concatenate([cp, cs_], axis=1))
        m["sinO"] = np.ascontiguousarray(np.concatenate([sp_, ss_], axis=1))
        m["dftP_c"], m["dftP_s"] = _dft_tables(c.SP, posP)
        m["dftS_c"], m["dftS_s"] = _dft_tables(c.SS, posS)
        in_maps.append(m)
    res = run_bass_kernel_spmd(nc, in_maps, core_ids=list(range(8)))
    yp = np.zeros((2, c.SP, c.D), f32)
    ys = np.zeros((2, c.SS, c.D), f32)
    for core in range(8):
        b, r = core // 4, core % 4
        y = res.results[core]["y"]
        yp[b, r * c.OP:(r + 1) * c.OP] = y[0:c.OP]
        ys[b, r * c.OS:(r + 1) * c.OS] = y[c.OP:]
    if return_all:
        return (yp, ys), res.results
    return (yp, ys)


def kernel(**inputs):
    return run(Cfg(), **inputs)
```

```python
import contextlib
import numpy as np
import ml_dtypes
import concourse.bass as bass
import concourse.mybir as mybir
from concourse.bass_utils import run_bass_kernel_spmd

F32 = mybir.dt.float32
BF16 = mybir.dt.bfloat16
AF = mybir.ActivationFunctionType
ALU = mybir.AluOpType
AX = mybir.AxisListType


class Buf:
    __slots__ = ("name", "t", "w", "r", "dsem", "dcnt")

    def __init__(self, name, t):
        self.name = name
        self.t = t
        self.w = None
        self.r = {}
        self.dsem = None
        self.dcnt = 0

    def __getitem__(self, k):
        return self.t[k]


class Sync:
    ENG = ("pe", "act", "dve", "pool", "sp")

    def __init__(self, nc, stack):
        self.nc = nc
        self.stack = stack
        self.eng = {"pe": nc.tensor, "act": nc.scalar, "dve": nc.vector, "pool": nc.gpsimd, "sp": nc.sync}
        self.esem = {}
        for e in ("pe", "act", "dve", "pool"):
            self.esem[e] = stack.enter_context(nc.semaphore("es_" + e))
        self.cnt = {e: 0 for e in self.ENG}
        self.seen = {e: {} for e in self.ENG}
        self.out_dma = {e: {} for e in self.ENG}
        self.free_dsems = []
        self.stage_dsems = []
        self.nsem = 0

    def uname(self, name):
        self.nname = getattr(self, "nname", 0) + 1
        return "s%d_%s" % (self.nname, name)

    def raw_sbuf(self, stack, name, shape, dt):
        return stack.enter_context(self.nc.sbuf_tensor(self.uname(name), list(shape), dt))

    def sbuf(self, stack, name, shape, dt):
        return Buf(name, self.raw_sbuf(stack, name, shape, dt))

    def psum(self, stack, name, shape, dt=F32):
        t = stack.enter_context(self.nc.psum_tensor(self.uname(name), list(shape), dt))
        return Buf(name, t)

    def _dsem(self, b):
        if b.dsem is None:
            if self.free_dsems:
                b.dsem = self.free_dsems.pop()
            else:
                self.nsem += 1
                b.dsem = [self.stack.enter_context(self.nc.semaphore("ds%d" % self.nsem)), 0]
            self.stage_dsems.append(b.dsem)
        return b.dsem

    def _wait(self, e, tok):
        sem, val = tok
        k = id(sem)
        if self.seen[e].get(k, 0) >= val:
            return
        self.eng[e].wait_ge(sem, val)
        self.seen[e][k] = val

    def _deps(self, e, reads, writes, is_dma=False):
        own = None if is_dma else self.esem.get(e)
        toks = []
        for b in reads:
            if b.w is not None:
                toks.append(b.w)
        for b in writes:
            if b.w is not None and b.w[0] is not own:
                toks.append(b.w)
            toks.extend(t for t in b.r.values() if t[0] is not own)
        for t in toks:
            if e == "pe" and t[0] is own:
                continue
            self._wait(e, t)

    def _commit(self, tok, reads, writes):
        k = id(tok[0])
        for b in reads:
            b.r[k] = tok
        for b in writes:
            b.w = tok
            b.r = {}

    def op(self, e, fn, reads=(), writes=()):
        self._deps(e, reads, writes)
        ins = fn()
        self.cnt[e] += 1
        ins.then_inc(self.esem[e], 1)
        self._commit((self.esem[e], self.cnt[e]), reads, writes)

    def pe(self, fns, reads=(), writes=()):
        self._deps("pe", reads, writes)
        ins = None
        for fn in fns:
            ins = fn()
        self.cnt["pe"] += 1
        ins.then_inc(self.esem["pe"], 1)
        self._commit((self.esem["pe"], self.cnt["pe"]), reads, writes)

    def dma(self, q, out, in_, tokbuf, reads=(), writes=()):
        self._deps(q, reads, writes, is_dma=True)
        cell = self._dsem(tokbuf)
        sem = cell[0]
        ins = self.eng[q].dma_start(out=out, in_=in_)
        cell[1] += 16
        ins.then_inc(sem, 16)
        tok = (sem, cell[1])
        self._commit(tok, reads, writes)
        self.out_dma[q][id(sem)] = tok

    def drain(self):
        for q in self.ENG:
            for tok in self.out_dma[q].values():
                self._wait(q, tok)
            self.out_dma[q] = {}
        for e in ("act", "dve", "pool"):
            if self.cnt[e] > 0:
                self._wait(e, (self.esem[e], self.cnt[e]))

    def barrier(self):
        self.drain()
        self.nc.all_engine_barrier()
        self.free_dsems.extend(self.stage_dsems)
        self.stage_dsems = []


class Cfg:
    def __init__(self, D=4096, SP=4096, SS=8192, NH=32, NKV=8, FG=8, FGD=256, PH=8, GRID_W=64,
                 TT=512, WC=512, debug=False):
        self.D, self.SP, self.SS, self.NH, self.NKV = D, SP, SS, NH, NKV
        self.FG, self.FGD, self.PH, self.GRID_W, self.TT, self.WC = FG, FGD, PH, GRID_W, TT, WC
        self.debug = debug
        self.KC = D // 128
        self.ATT = NH * 128
        self.KVW = NKV * 128
        self.FW = FG * FGD
        self.INW = self.ATT + 2 * self.KVW + self.FW + 2 * D
        self.cQ, self.cK = 0, self.ATT
        self.cV = self.cK + self.KVW
        self.cF = self.cV + self.KVW
        self.cGA = self.cF + self.FW
        self.cGF = self.cGA + D
        self.OP, self.OS = SP // 4, SS // 4
        self.NOWN = self.OP + self.OS
        self.NALL = SP + SS
        self.HP = PH * 2
        self.PQ = self.HP * 128
        self.NE = 128 * 128
        self.EPS = 1e-6
        self.seqs = [(SP, 0, 0, self.OP, 0), (SS, SP, self.OP, self.OS, 1)]


class Rot:
    def __init__(self, bufs):
        self.bufs = bufs
        self.i = -1

    def next(self):
        self.i = (self.i + 1) % len(self.bufs)
        return self.bufs[self.i]


def build(cfg):
    c = cfg
    D, KC, TT = c.D, c.KC, c.TT
    NSUB = TT // 128
    nc = bass.Bass("TRN2", target_bir_lowering=False)
    T = {}

    def din(name, shape, dt=F32):
        T[name] = nc.dram_tensor(name, list(shape), dt, kind="ExternalInput").ap()

    def dscr(name, shape, dt):
        kind = "ExternalOutput" if c.debug else "Internal"
        T[name] = nc.dram_tensor(name, list(shape), dt, kind=kind).ap()

    din("xall", [c.NALL, D]); din("xown", [c.NOWN, D])
    din("cT", [128, KC, 2])
    din("w_ada", [D, 6 * D]); din("b_ada", [1, 6 * D])
    din("n1g", [1, D]); din("n2g", [1, D]); din("fg", [1, D])
    din("w_in", [D, c.INW])
    din("qg", [128, 1]); din("kg", [128, 1])
    din("w_attn", [c.ATT, D]); din("w_four", [c.FW, D]); din("w_out", [D, D]); din("w_pq", [D, c.PQ])
    din("keysT", [128, c.HP, 128])
    din("uT", [D, c.NE]); din("vtab", [c.NE, D])
    din("cosA", [128, c.SS]); din("sinA", [128, c.SS]); din("cosO", [128, c.NOWN]); din("sinO", [128, c.NOWN])
    din("rotR", [128, 128])
    din("dftP_c", [c.SP, c.OP], BF16); din("dftP_s", [c.SP, c.OP], BF16)
    din("dftS_c", [c.SS, c.OS], BF16); din("dftS_s", [c.SS, c.OS], BF16)
    din("dftC_c", [c.FGD, c.FGD], BF16); din("dftC_s", [c.FGD, c.FGD], BF16)
    T["y"] = nc.dram_tensor("y", [c.NOWN, D], F32, kind="ExternalOutput").ap()
    dscr("G1", [2, D], F32); dscr("G2", [2, D], F32)
    dscr("KT", [c.NKV, 128, c.NALL], BF16); dscr("V", [c.NALL, c.KVW], BF16); dscr("U", [c.NALL, c.FW], BF16)
    dscr("QT", [c.NH, 128, c.NOWN], BF16); dscr("GAT", [D, c.NOWN], BF16); dscr("GFT", [D, c.NOWN], BF16)
    dscr("ATT_T", [c.ATT, c.NOWN], BF16); dscr("YT", [c.FW, c.NOWN], BF16)
    dscr("X1", [c.NOWN, D], F32); dscr("H2T", [D, c.NOWN], BF16)
    dscr("EB", [c.NOWN, c.HP, 128], BF16); dscr("EPSD", [c.NOWN, c.PH], F32); dscr("RZD", [c.NOWN, c.PH], F32)
    dscr("PO", [c.NOWN, D], F32)
    WCM = 256
    c.SBE = 8
    T["WAb"] = nc.dram_tensor("WAb", [D // WCM, 128, (c.ATT // 128) * WCM], BF16, kind="Internal").ap()
    T["WFb"] = nc.dram_tensor("WFb", [D // WCM, 128, (c.FW // 128) * WCM], BF16, kind="Internal").ap()
    T["WOb"] = nc.dram_tensor("WOb", [D // WCM, 128, KC * WCM], BF16, kind="Internal").ap()
    T["UTb"] = nc.dram_tensor("UTb", [c.NE // 256, 128, KC * 256], BF16, kind="Internal").ap()
    VW = min(512, D)
    T["VBb"] = nc.dram_tensor("VBb", [c.NE // (128 * c.SBE), D // VW, 128, c.SBE * VW], BF16, kind="Internal").ap()
    prep_items = []
    for j in range(D // WCM):
        prep_items.append(("w_attn", 0, c.ATT, j * WCM, WCM, T["WAb"][j]))
        prep_items.append(("w_four", 0, c.FW, j * WCM, WCM, T["WFb"][j]))
        prep_items.append(("w_out", 0, D, j * WCM, WCM, T["WOb"][j]))
    for j in range(c.NE // 256):
        prep_items.append(("uT", 0, D, j * 256, 256, T["UTb"][j]))
    for sbk in range(c.NE // (128 * c.SBE)):
        for j in range(D // VW):
            prep_items.append(("vtab", sbk * c.SBE * 128, c.SBE * 128, j * VW, VW, T["VBb"][sbk, j]))
    prep_state = {"i": 0}

    V_, A_, G_, PE_ = nc.vector, nc.scalar, nc.gpsimd, nc.tensor
    top = contextlib.ExitStack()
    S = Sync(nc, top)

    def mm(out, lhsT, rhs, st, sp):
        return lambda: PE_.matmul(out, lhsT=lhsT, rhs=rhs, start=st, stop=sp)

    def load_w(Wb, Wd, r0, nrows, c0, ncols, q="pool"):
        kcs = nrows // 128
        step = 8
        for k0 in range(0, kcs, step):
            k1 = min(kcs, k0 + step)
            S.dma(q, Wb[:, k0:k1, 0:ncols],
                  Wd[r0 + k0 * 128:r0 + k1 * 128, c0:c0 + ncols].rearrange("(c p) n -> p c n", p=128),
                  Wb, writes=[Wb])

    def prep_emit(bufs, n):
        for _ in range(n):
            if prep_state["i"] >= len(prep_items):
                return
            (wname, r0, nrows, c0, ncols, dst) = prep_items[prep_state["i"]]
            prep_state["i"] += 1
            b = bufs.next()
            kcs = nrows // 128
            bv = b[:, 0:kcs * ncols].rearrange("p (c n) -> p c n", n=ncols)
            step = 8
            for k0 in range(0, kcs, step):
                k1 = min(kcs, k0 + step)
                S.dma("pool", bv[:, k0:k1, :],
                      T[wname][r0 + k0 * 128:r0 + k1 * 128, c0:c0 + ncols].rearrange("(c p) n -> p c n", p=128),
                      b, writes=[b])
            S.dma("pool", dst, b[:, 0:kcs * ncols], b, reads=[b])

    def load_wb(Wb, src, kcs, ncols):
        S.dma("sp", Wb[:, 0:kcs, 0:ncols], src.rearrange("p (c n) -> p c n", n=ncols), Wb, writes=[Wb])

    ident = S.sbuf(top, "ident", [128, 128], BF16)
    identf = S.sbuf(top, "identf", [128, 128], F32)
    onesf = S.sbuf(top, "onesf", [128, 128], F32)
    onesb = S.sbuf(top, "onesb", [128, 128], BF16)
    rotR = S.sbuf(top, "rotR", [128, 128], F32)
    qg = S.sbuf(top, "qg", [128, 1], F32)
    kg = S.sbuf(top, "kg", [128, 1], F32)
    A1 = S.sbuf(top, "A1", [128, 2, KC], F32); B1 = S.sbuf(top, "B1", [128, 2, KC], F32)
    A2 = S.sbuf(top, "A2", [128, 2, KC], F32); B2 = S.sbuf(top, "B2", [128, 2, KC], F32)
    S.op("pool", lambda: G_.memset(identf[:], 1.0), writes=[identf])
    S.op("pool", lambda: G_.affine_select(out=identf[:], in_=identf[:], pattern=[[-1, 128]], compare_op=ALU.is_equal,
                                          fill=0.0, base=0, channel_multiplier=1), reads=[identf], writes=[identf])
    S.op("dve", lambda: V_.tensor_copy(out=ident[:], in_=identf[:]), reads=[identf], writes=[ident])
    S.op("dve", lambda: V_.memset(onesf[:], 1.0), writes=[onesf])
    S.op("dve", lambda: V_.memset(onesb[:], 1.0), writes=[onesb])
    S.dma("sp", rotR[:], T["rotR"], rotR, writes=[rotR])
    S.dma("sp", qg[:], T["qg"], qg, writes=[qg])
    S.dma("sp", kg[:], T["kg"], kg, writes=[kg])

    def stage_mod():
        st = contextlib.ExitStack()
        cTf = S.sbuf(st, "cTf", [128, KC, 2], F32)
        cs = S.sbuf(st, "cs", [128, KC, 2], BF16)
        Wb = Rot([S.sbuf(st, "Wm%d" % i, [128, KC, 512], BF16) for i in range(2)])
        psr = Rot([S.psum(st, "psr%d" % i, [128, 512]) for i in range(2)])
        psc = Rot([S.psum(st, "psc%d" % i, [128, 4]) for i in range(2)])
        brow = Rot([S.sbuf(st, "brow%d" % i, [1, 512], F32) for i in range(2)])
        grow = Rot([S.sbuf(st, "grow%d" % i, [1, 512], F32) for i in range(2)])
        row = Rot([S.sbuf(st, "row%d" % i, [1, 512], F32) for i in range(4)])
        S.dma("sp", cTf[:], T["cT"], cTf, writes=[cTf])
        S.op("act", lambda: A_.activation(out=cs[:], in_=cTf[:], func=AF.Silu), reads=[cTf], writes=[cs])
        ntile = 6 * D // 512
        for j in range(ntile):
            W = Wb.next()
            load_w(W, T["w_ada"], 0, D, j * 512, 512)
            kind = (j * 512) // D
            off = (j * 512) % D
            bb = brow.next()
            S.dma("sp", bb[:], T["b_ada"][0:1, j * 512:(j + 1) * 512], bb, writes=[bb])
            gg = None
            if kind in (1, 4):
                gg = grow.next()
                S.dma("sp", gg[:], T["n1g" if kind == 1 else "n2g"][0:1, off:off + 512], gg, writes=[gg])
            for grp in range(2):
                ps = psr.next()
                S.pe([mm(ps[0:1, :], cs[:, kc, grp:grp + 1], W[:, kc, :], kc == 0, kc == KC - 1) for kc in range(KC)],
                     reads=[cs, W], writes=[ps])
                r = row.next()
                S.op("dve", lambda: V_.tensor_tensor(out=r[:], in0=ps[0:1, :], in1=bb[:], op=ALU.add), reads=[ps, bb], writes=[r])
                if kind in (2, 5):
                    S.dma("sp", T["G1" if kind == 2 else "G2"][grp:grp + 1, off:off + 512], r[:], r, reads=[r])
                    continue
                if kind in (1, 4):
                    S.op("dve", lambda: V_.scalar_tensor_tensor(out=r[:], in0=r[:], scalar=1.0, in1=gg[:], op0=ALU.add, op1=ALU.mult),
                         reads=[r, gg], writes=[r])
                pc = psc.next()
                S.pe([mm(pc[:, q:q + 1], r[0:1, q * 128:(q + 1) * 128], onesf[0:1, 0:1], True, True) for q in range(4)],
                     reads=[r, onesf], writes=[pc])
                dst = {0: B1, 1: A1, 3: B2, 4: A2}[kind]
                k0 = off // 128
                S.op("dve", lambda: V_.tensor_copy(out=dst[:, grp, k0:k0 + 4], in_=pc[:]), reads=[pc], writes=[dst])
        S.barrier()
        st.close()

    G8 = min(8, KC)

    def alloc_prologue(st):
        P = {}
        P["xs"] = Rot([S.sbuf(st, "xs%d" % i, [128, D], F32) for i in range(2)])
        P["ss"] = Rot([S.sbuf(st, "ss%d" % i, [128, 1], F32) for i in range(2)])
        P["xn"] = Rot([S.sbuf(st, "xn%d" % i, [128, D], BF16) for i in range(2)])
        P["ptr"] = Rot([S.psum(st, "ptr%d" % i, [128, G8, 128], BF16) for i in range(2)])
        return P

    def prologue(P, xsrc, tok0, grp, Acol, Bcol, hTs):
        for sub in range(NSUB):
            xb = P["xs"].next()
            S.dma("sp", xb[:], xsrc[tok0 + sub * 128:tok0 + (sub + 1) * 128, :], xb, writes=[xb])
            ssb = P["ss"].next()
            S.op("dve", lambda: V_.memset(ssb[:], 0.0), writes=[ssb])
            xnb = P["xn"].next()
            S.op("act", lambda: A_.activation(out=xnb[:], in_=xb[:], func=AF.Square, accum_out=ssb[:]),
                 reads=[xb, ssb], writes=[xnb, ssb])
            S.op("dve", lambda: V_.tensor_scalar(out=ssb[:], in0=ssb[:], scalar1=1.0 / D, scalar2=c.EPS, op0=ALU.mult, op1=ALU.add),
                 reads=[ssb], writes=[ssb])
            S.op("act", lambda: A_.sqrt(out=ssb[:], in_=ssb[:]), reads=[ssb], writes=[ssb])
            S.op("dve", lambda: V_.reciprocal(out=ssb[:], in_=ssb[:]), reads=[ssb], writes=[ssb])
            S.op("dve", lambda: V_.tensor_scalar(out=xnb[:], in0=xb[:], scalar1=ssb[:, 0:1], scalar2=None, op0=ALU.mult),
                 reads=[xb, ssb], writes=[xnb])
            hb = hTs[sub]
            for kg_ in range(KC // G8):
                pt = P["ptr"].next()
                S.pe([(lambda j=j: PE_.transpose(out=pt[:, j, :], in_=xnb[:, (kg_ * G8 + j) * 128:(kg_ * G8 + j + 1) * 128], identity=ident[:]))
                      for j in range(G8)], reads=[xnb, ident], writes=[pt])
                for j in range(G8):
                    kc = kg_ * G8 + j
                    o = hb[:, kc, sub * 128:(sub + 1) * 128]
                    if kg_ % 2 == 0:
                        S.op("act", lambda: A_.activation(out=o, in_=pt[:, j, :], func=AF.Identity,
                                                          scale=Acol[:, grp, kc:kc + 1], bias=Bcol[:, grp, kc:kc + 1]),
                             reads=[pt, Acol, Bcol], writes=[hb])
                    else:
                        S.op("dve", lambda: V_.tensor_scalar(out=o, in0=pt[:, j, :], scalar1=Acol[:, grp, kc:kc + 1],
                                                             scalar2=Bcol[:, grp, kc:kc + 1], op0=ALU.mult, op1=ALU.add),
                             reads=[pt, Acol, Bcol], writes=[hb])

    def alloc_rope(st):
        Rp = {}
        Rp["xsb"] = Rot([S.sbuf(st, "rxs%d" % i, [128, TT], F32) for i in range(2)])
        Rp["sqb"] = Rot([S.sbuf(st, "rsq%d" % i, [128, TT], F32) for i in range(1)])
        Rp["rsb"] = Rot([S.sbuf(st, "rrs%d" % i, [128, TT], F32) for i in range(1)])
        Rp["t1"] = Rot([S.sbuf(st, "rt1%d" % i, [128, TT], F32) for i in range(1)])
        Rp["t2"] = Rot([S.sbuf(st, "rt2%d" % i, [128, TT], F32) for i in range(1)])
        Rp["pss"] = Rot([S.psum(st, "pss%d" % i, [128, TT]) for i in range(1)])
        Rp["psw"] = Rot([S.psum(st, "psw%d" % i, [128, TT]) for i in range(1)])
        return Rp

    def rope_epi(Rp, ps, gcol, cosb, sinb, ob):
        xsb, sqb, rsb, t1, t2 = Rp["xsb"].next(), Rp["sqb"].next(), Rp["rsb"].next(), Rp["t1"].next(), Rp["t2"].next()
        pss, psw = Rp["pss"].next(), Rp["psw"].next()
        S.op("act", lambda: A_.activation(out=xsb[:], in_=ps[:, 0:TT], func=AF.Identity, scale=gcol[:, 0:1]), reads=[ps, gcol], writes=[xsb])
        S.op("act", lambda: A_.activation(out=sqb[:], in_=ps[:, 0:TT], func=AF.Square), reads=[ps], writes=[sqb])
        S.pe([mm(pss[:], onesf[:], sqb[:], True, True)], reads=[onesf, sqb], writes=[pss])
        S.pe([mm(psw[:], rotR[:], xsb[:], True, True)], reads=[rotR, xsb], writes=[psw])
        S.op("dve", lambda: V_.tensor_scalar(out=rsb[:], in0=pss[:], scalar1=1.0 / 128, scalar2=c.EPS, op0=ALU.mult, op1=ALU.add),
             reads=[pss], writes=[rsb])
        S.op("act", lambda: A_.sqrt(out=rsb[:], in_=rsb[:]), reads=[rsb], writes=[rsb])
        S.op("dve", lambda: V_.reciprocal(out=rsb[:], in_=rsb[:]), reads=[rsb], writes=[rsb])
        S.op("dve", lambda: V_.tensor_tensor(out=t1[:], in0=xsb[:], in1=cosb[:], op=ALU.mult), reads=[xsb, cosb], writes=[t1])
        S.op("dve", lambda: V_.tensor_tensor(out=t2[:], in0=psw[:], in1=sinb[:], op=ALU.mult), reads=[psw, sinb], writes=[t2])
        S.op("dve", lambda: V_.tensor_tensor(out=t1[:], in0=t1[:], in1=t2[:], op=ALU.add), reads=[t1, t2], writes=[t1])
        S.op("dve", lambda: V_.tensor_tensor(out=ob[:], in0=t1[:], in1=rsb[:], op=ALU.mult), reads=[t1, rsb], writes=[ob])

    def stage_kvf():
        st = contextlib.ExitStack()
        P = alloc_prologue(st)
        Rp = alloc_rope(st)
        hTt = [S.raw_sbuf(st, "hT%d" % i, [128, KC, TT], BF16) for i in range(1)]
        hTr = Rot([[Buf("hT%d_%d" % (i, s_), hTt[i]) for s_ in range(NSUB)] for i in range(1)])
        Wb = Rot([S.sbuf(st, "W%d" % i, [128, KC, 512], BF16) for i in range(2)])
        psg = Rot([S.psum(st, "psg%d" % i, [128, 512]) for i in range(2)])
        cosb = Rot([S.sbuf(st, "cos%d" % i, [128, TT], F32) for i in range(2)])
        sinb = Rot([S.sbuf(st, "sin%d" % i, [128, TT], F32) for i in range(2)])
        ko = Rot([S.sbuf(st, "ko%d" % i, [128, TT], BF16) for i in range(2)])
        vo = Rot([S.sbuf(st, "vo%d" % i, [128, 512], BF16) for i in range(3)])
        for (slen, aoff, ooff, ocnt, grp) in c.seqs:
            for tt in range(slen // TT):
                tok0 = aoff + tt * TT
                pos0 = tt * TT
                hTs = hTr.next()
                prologue(P, T["xall"], tok0, grp, A1, B1, hTs)
                hT = hTs[0]
                cb, sb = cosb.next(), sinb.next()
                S.dma("sp", cb[:], T["cosA"][:, pos0:pos0 + TT], cb, writes=[cb])
                S.dma("sp", sb[:], T["sinA"][:, pos0:pos0 + TT], sb, writes=[sb])
                for w0 in range(0, c.KVW, 512):
                    wn = min(512, c.KVW - w0)
                    W = Wb.next()
                    load_w(W, T["w_in"], 0, D, c.cK + w0, wn)
                    for hb in range(wn // 128):
                        g = (w0 + hb * 128) // 128
                        ps = psg.next()
                        S.pe([mm(ps[:, 0:TT], W[:, kc, hb * 128:(hb + 1) * 128], hT[:, kc, :], kc == 0, kc == KC - 1) for kc in range(KC)],
                             reads=[W] + hTs, writes=[ps])
                        ob = ko.next()
                        rope_epi(Rp, ps, kg, cb, sb, ob)
                        S.dma("sp", T["KT"][g, :, tok0:tok0 + TT], ob[:], ob, reads=[ob])
                for (c0, width, dst) in ((c.cV, c.KVW, "V"), (c.cF, c.FW, "U")):
                    for w0 in range(0, width, 512):
                        wn = min(512, width - w0)
                        W = Wb.next()
                        load_w(W, T["w_in"], 0, D, c0 + w0, wn)
                        for sub in range(NSUB):
                            ps = psg.next()
                            S.pe([mm(ps[:, 0:wn], hT[:, kc, sub * 128:(sub + 1) * 128], W[:, kc, 0:wn], kc == 0, kc == KC - 1) for kc in range(KC)],
                                 reads=[W, hTs[sub]], writes=[ps])
                            ob = vo.next()
                            S.op("act", lambda: A_.activation(out=ob[:, 0:wn], in_=ps[:, 0:wn], func=AF.Copy), reads=[ps], writes=[ob])
                            S.dma("sp", T[dst][tok0 + sub * 128:tok0 + (sub + 1) * 128, w0:w0 + wn], ob[:, 0:wn], ob, reads=[ob])
        S.barrier()
        st.close()

    def stage_qg():
        st = contextlib.ExitStack()
        P = alloc_prologue(st)
        Rp = alloc_rope(st)
        hTt = [S.raw_sbuf(st, "hT%d" % i, [128, KC, TT], BF16) for i in range(1)]
        hTr = Rot([[Buf("hT%d_%d" % (i, s_), hTt[i]) for s_ in range(NSUB)] for i in range(1)])
        Wb = Rot([S.sbuf(st, "W%d" % i, [128, KC, 512], BF16) for i in range(2)])
        psg = Rot([S.psum(st, "psg%d" % i, [128, 512]) for i in range(2)])
        cosb = Rot([S.sbuf(st, "cos%d" % i, [128, TT], F32) for i in range(2)])
        sinb = Rot([S.sbuf(st, "sin%d" % i, [128, TT], F32) for i in range(2)])
        ko = Rot([S.sbuf(st, "ko%d" % i, [128, TT], BF16) for i in range(3)])
        for (slen, aoff, ooff, ocnt, grp) in c.seqs:
            for tt in range(ocnt // TT):
                tok0 = ooff + tt * TT
                hTs = hTr.next()
                prologue(P, T["xown"], tok0, grp, A1, B1, hTs)
                hT = hTs[0]
                cb, sb = cosb.next(), sinb.next()
                S.dma("sp", cb[:], T["cosO"][:, tok0:tok0 + TT], cb, writes=[cb])
                S.dma("sp", sb[:], T["sinO"][:, tok0:tok0 + TT], sb, writes=[sb])
                for w0 in range(0, c.ATT, 512):
                    wn = min(512, c.ATT - w0)
                    W = Wb.next()
                    load_w(W, T["w_in"], 0, D, c.cQ + w0, wn)
                    for hb in range(wn // 128):
                        h = (w0 + hb * 128) // 128
                        ps = psg.next()
                        S.pe([mm(ps[:, 0:TT], W[:, kc, hb * 128:(hb + 1) * 128], hT[:, kc, :], kc == 0, kc == KC - 1) for kc in range(KC)],
                             reads=[W] + hTs, writes=[ps])
                        ob = ko.next()
                        rope_epi(Rp, ps, qg, cb, sb, ob)
                        S.dma("sp", T["QT"][h, :, tok0:tok0 + TT], ob[:], ob, reads=[ob])
                for (c0, dst) in ((c.cGA, "GAT"), (c.cGF, "GFT")):
                    for w0 in range(0, D, 512):
                        wn = min(512, D - w0)
                        W = Wb.next()
                        load_w(W, T["w_in"], 0, D, c0 + w0, wn)
                        for hb in range(wn // 128):
                            ps = psg.next()
                            S.pe([mm(ps[:, 0:TT], W[:, kc, hb * 128:(hb + 1) * 128], hT[:, kc, :], kc == 0, kc == KC - 1) for kc in range(KC)],
                                 reads=[W] + hTs, writes=[ps])
                            ob = ko.next()
                            S.op("act", lambda: A_.activation(out=ob[:], in_=ps[:, 0:TT], func=AF.Sigmoid), reads=[ps], writes=[ob])
                            r0 = w0 + hb * 128
                            S.dma("sp", T[dst][r0:r0 + 128, tok0:tok0 + TT], ob[:], ob, reads=[ob])
        S.barrier()
        st.close()

    def stage_attn():
        st = contextlib.ExitStack()
        SMAX = max(c.SP, c.SS)
        QB = TT
        KTb = Rot([S.sbuf(st, "KTb%d" % i, [128, SMAX], BF16) for i in range(2)])
        Vb = Rot([S.sbuf(st, "Vb%d" % i, [128, SMAX // 128, 128], BF16) for i in range(2)])
        Qb = Rot([S.sbuf(st, "Qb%d" % i, [128, QB], BF16) for i in range(2)])
        pT = Rot([S.sbuf(st, "pT%d" % i, [128, QB], BF16) for i in range(3)])
        pss = Rot([S.psum(st, "pss%d" % i, [128, QB]) for i in range(2)])
        pso = Rot([S.psum(st, "pso%d" % i, [128, QB]) for i in range(2)])
        psl = Rot([S.psum(st, "psl%d" % i, [128, QB]) for i in range(2)])
        rl = Rot([S.sbuf(st, "rl%d" % i, [128, QB], F32) for i in range(2)])
        ob_ = Rot([S.sbuf(st, "ao%d" % i, [128, QB], BF16) for i in range(2)])
        PBW = max(KC * 256, c.SBE * min(512, D), (c.ATT // 128) * 256, (c.FW // 128) * 256)
        pbuf = Rot([S.sbuf(st, "prep%d" % i, [128, PBW], BF16) for i in range(3)])
        nblocks = sum(c.NH * (oc // QB) for (_, _, _, oc, _) in c.seqs)
        per_block = -(-len(prep_items) // max(1, nblocks - 2))
        scale = 128 ** -0.5
        for (slen, aoff, ooff, ocnt, grp) in c.seqs:
            nkc = slen // 128
            for g in range(c.NKV):
                Kt, Vt = KTb.next(), Vb.next()
                S.dma("sp", Kt[:, 0:slen], T["KT"][g, :, aoff:aoff + slen], Kt, writes=[Kt])
                for k0 in range(0, nkc, 16):
                    k1 = min(nkc, k0 + 16)
                    S.dma("sp", Vt[:, k0:k1, :],
                          T["V"][aoff + k0 * 128:aoff + k1 * 128, g * 128:(g + 1) * 128].rearrange("(c p) d -> p c d", p=128),
                          Vt, writes=[Vt])
                for qh in range(c.NH // c.NKV):
                    h = g * (c.NH // c.NKV) + qh
                    for qb in range(ocnt // QB):
                        q0 = ooff + qb * QB
                        Qt = Qb.next()
                        S.dma("sp", Qt[:], T["QT"][h, :, q0:q0 + QB], Qt, writes=[Qt])
                        prep_emit(pbuf, per_block)
                        po, pl = pso.next(), psl.next()
                        pend = None
                        for kc in range(nkc + 1):
                            if kc < nkc:
                                ps = pss.next()
                                S.pe([mm(ps[:], Kt[:, kc * 128:(kc + 1) * 128], Qt[:], True, True)], reads=[Kt, Qt], writes=[ps])
                                p = pT.next()
                                S.op("act", lambda: A_.activation(out=p[:], in_=ps[:], func=AF.Exp, scale=scale), reads=[ps], writes=[p])
                            if pend is not None:
                                pk, pp = pend
                                S.pe([mm(po[:], Vt[:, pk, :], pp[:], pk == 0, pk == nkc - 1),
                                      mm(pl[:], onesb[:], pp[:], pk == 0, pk == nkc - 1)], reads=[Vt, pp, onesb], writes=[po, pl])
                            pend = (kc, p) if kc < nkc else None
                        r_, o_ = rl.next(), ob_.next()
                        S.op("dve", lambda: V_.reciprocal(out=r_[:], in_=pl[:]), reads=[pl], writes=[r_])
                        S.op("dve", lambda: V_.tensor_tensor(out=o_[:], in0=po[:], in1=r_[:], op=ALU.mult), reads=[po, r_], writes=[o_])
                        S.dma("sp", T["ATT_T"][h * 128:(h + 1) * 128, q0:q0 + QB], o_[:], o_, reads=[o_])
        prep_emit(pbuf, len(prep_items))
        S.barrier()
        st.close()

    def stage_dft():
        st = contextlib.ExitStack()
        NB = c.FW // 128
        GS = min(3, NB)
        SCB = 16
        GC = c.FGD // 128
        SB = TT
        Ct = Rot([S.sbuf(st, "Ct%d" % i, [128, SCB, SB], BF16) for i in range(2)])
        St_ = Rot([S.sbuf(st, "St%d" % i, [128, SCB, SB], BF16) for i in range(2)])
        Ut = Rot([S.sbuf(st, "Ut%d" % i, [128, SCB, GS * 128], BF16) for i in range(2)])
        P1 = S.sbuf(st, "P1", [128, NB, SB], BF16)
        P2 = S.sbuf(st, "P2", [128, NB, SB], BF16)
        CC = S.sbuf(st, "CC", [128, GC, c.FGD], BF16)
        SC = S.sbuf(st, "SC", [128, GC, c.FGD], BF16)
        acc = [S.psum(st, "dacc%d" % i, [128, SB]) for i in range(2 * GS)]
        psy = Rot([S.psum(st, "psy%d" % i, [128, SB]) for i in range(2)])
        yo = Rot([S.sbuf(st, "yo%d" % i, [128, SB], BF16) for i in range(2)])
        S.dma("sp", CC[:], T["dftC_c"].rearrange("(c p) n -> p c n", p=128), CC, writes=[CC])
        S.dma("sp", SC[:], T["dftC_s"].rearrange("(c p) n -> p c n", p=128), SC, writes=[SC])
        for si, (slen, aoff, ooff, ocnt, grp) in enumerate(c.seqs):
            tc_, ts_ = (T["dftP_c"], T["dftP_s"]) if si == 0 else (T["dftS_c"], T["dftS_s"])
            nkc = slen // 128
            for sb in range(ocnt // SB):
                for b0 in range(0, NB, GS):
                    gs = min(GS, NB - b0)
                    for sc0 in range(0, nkc, SCB):
                        sc1 = min(nkc, sc0 + SCB)
                        n = sc1 - sc0
                        ct, stt, ut = Ct.next(), St_.next(), Ut.next()
                        S.dma("sp", ct[:, 0:n, :], tc_[sc0 * 128:sc1 * 128, sb * SB:(sb + 1) * SB].rearrange("(c p) n -> p c n", p=128), ct, writes=[ct])
                        S.dma("sp", stt[:, 0:n, :], ts_[sc0 * 128:sc1 * 128, sb * SB:(sb + 1) * SB].rearrange("(c p) n -> p c n", p=128), stt, writes=[stt])
                        S.dma("sp", ut[:, 0:n, 0:gs * 128],
                              T["U"][aoff + sc0 * 128:aoff + sc1 * 128, b0 * 128:(b0 + gs) * 128].rearrange("(c p) n -> p c n", p=128),
                              ut, writes=[ut])
                        fns = []
                        for ci in range(n):
                            first, last = (sc0 + ci == 0), (sc0 + ci == nkc - 1)
                            for b in range(gs):
                                fns.append(mm(acc[2 * b][:], ut[:, ci, b * 128:(b + 1) * 128], ct[:, ci, :], first, last))
                                fns.append(mm(acc[2 * b + 1][:], ut[:, ci, b * 128:(b + 1) * 128], stt[:, ci, :], first, last))
                        S.pe(fns, reads=[ct, stt, ut], writes=acc[0:2 * gs])
                    for b in range(gs):
                        S.op("act", lambda: A_.activation(out=P1[:, b0 + b, :], in_=acc[2 * b][:], func=AF.Copy), reads=[acc[2 * b]], writes=[P1])
                        S.op("dve", lambda: V_.tensor_copy(out=P2[:, b0 + b, :], in_=acc[2 * b + 1][:]), reads=[acc[2 * b + 1]], writes=[P2])
                for ob in range(NB):
                    gi, ol = ob // GC, ob % GC
                    ps = psy.next()
                    fns = []
                    for cc in range(GC):
                        fns.append(mm(ps[:], CC[:, cc, ol * 128:(ol + 1) * 128], P1[:, gi * GC + cc, :], cc == 0, False))
                        fns.append(mm(ps[:], SC[:, cc, ol * 128:(ol + 1) * 128], P2[:, gi * GC + cc, :], False, cc == GC - 1))
                    S.pe(fns, reads=[CC, SC, P1, P2], writes=[ps])
                    y_ = yo.next()
                    S.op("act", lambda: A_.activation(out=y_[:], in_=ps[:], func=AF.Copy), reads=[ps], writes=[y_])
                    q0 = ooff + sb * SB
                    S.dma("sp", T["YT"][ob * 128:(ob + 1) * 128, q0:q0 + SB], y_[:], y_, reads=[y_])
        S.barrier()
        st.close()

    def stage_merge():
        st = contextlib.ExitStack()
        WC = 256
        KA, KF = c.ATT // 128, c.FW // 128
        atT = S.sbuf(st, "atT", [128, KA, TT], BF16)
        ytT = S.sbuf(st, "ytT", [128, KF, TT], BF16)
        mTt = S.raw_sbuf(st, "mT", [128, KC, TT], BF16)
        mT = Buf("mT", mTt)
        Wa = Rot([S.sbuf(st, "Wa%d" % i, [128, max(KA, KC), WC], BF16) for i in range(2)])
        Wf = Rot([S.sbuf(st, "Wf%d" % i, [128, KF, WC], BF16) for i in range(2)])
        gat = Rot([S.sbuf(st, "gat%d" % i, [128, WC // 128, TT], BF16) for i in range(2)])
        gft = Rot([S.sbuf(st, "gft%d" % i, [128, WC // 128, TT], BF16) for i in range(2)])
        g1bc = S.sbuf(st, "g1bc", [128, D], F32)
        psA = Rot([S.psum(st, "psA%d" % i, [128, 512]) for i in range(2)])
        psB = Rot([S.psum(st, "psB%d" % i, [128, 512]) for i in range(2)])
        psO = Rot([S.psum(st, "psO%d" % i, [128, 512]) for i in range(2)])
        ta = Rot([S.sbuf(st, "ta%d" % i, [128, TT], F32) for i in range(2)])
        tb = Rot([S.sbuf(st, "tb%d" % i, [128, TT], F32) for i in range(2)])
        xp = Rot([S.sbuf(st, "xp%d" % i, [128, WC], F32) for i in range(3)])
        xo = Rot([S.sbuf(st, "xo%d" % i, [128, WC], F32) for i in range(3)])
        for (slen, aoff, ooff, ocnt, grp) in c.seqs:
            S.dma("sp", g1bc[:], T["G1"][grp:grp + 1, :].partition_broadcast(128), g1bc, writes=[g1bc])
            for tt in range(ocnt // TT):
                tok0 = ooff + tt * TT
                S.dma("sp", atT[:], T["ATT_T"][:, tok0:tok0 + TT].rearrange("(c p) n -> p c n", p=128), atT, writes=[atT])
                S.dma("sp", ytT[:], T["YT"][:, tok0:tok0 + TT].rearrange("(c p) n -> p c n", p=128), ytT, writes=[ytT])
                for w0 in range(0, D, WC):
                    wa, wf, ga_, gf_ = Wa.next(), Wf.next(), gat.next(), gft.next()
                    load_wb(wa, T["WAb"][w0 // WC], KA, WC)
                    load_wb(wf, T["WFb"][w0 // WC], KF, WC)
                    S.dma("sp", ga_[:], T["GAT"][w0:w0 + WC, tok0:tok0 + TT].rearrange("(c p) n -> p c n", p=128), ga_, writes=[ga_])
                    S.dma("sp", gf_[:], T["GFT"][w0:w0 + WC, tok0:tok0 + TT].rearrange("(c p) n -> p c n", p=128), gf_, writes=[gf_])
                    for blk in range(WC // 128):
                        pa, pb = psA.next(), psB.next()
                        S.pe([mm(pa[:, 0:TT], wa[:, kc, blk * 128:(blk + 1) * 128], atT[:, kc, :], kc == 0, kc == KA - 1) for kc in range(KA)],
                             reads=[wa, atT], writes=[pa])
                        S.pe([mm(pb[:, 0:TT], wf[:, kc, blk * 128:(blk + 1) * 128], ytT[:, kc, :], kc == 0, kc == KF - 1) for kc in range(KF)],
                             reads=[wf, ytT], writes=[pb])
                        t1, t2 = ta.next(), tb.next()
                        S.op("dve", lambda: V_.tensor_tensor(out=t1[:], in0=pa[:, 0:TT], in1=ga_[:, blk, :], op=ALU.mult), reads=[pa, ga_], writes=[t1])
                        S.op("dve", lambda: V_.tensor_tensor(out=t2[:], in0=pb[:, 0:TT], in1=gf_[:, blk, :], op=ALU.mult), reads=[pb, gf_], writes=[t2])
                        ob = (w0 + blk * 128) // 128
                        S.op("pool", lambda: G_.tensor_tensor(out=mT[:, ob, :], in0=t1[:], in1=t2[:], op=ALU.add), reads=[t1, t2], writes=[mT])
                for w0 in range(0, D, WC):
                    wo = Wa.next()
                    load_wb(wo, T["WOb"][w0 // WC], KC, WC)
                    for sub in range(NSUB):
                        r0 = tok0 + sub * 128
                        ps = psO.next()
                        S.pe([mm(ps[:, 0:WC], mT[:, kc, sub * 128:(sub + 1) * 128], wo[:, kc, 0:WC], kc == 0, kc == KC - 1) for kc in range(KC)],
                             reads=[wo, mT], writes=[ps])
                        xp_, xo_ = xp.next(), xo.next()
                        S.dma("sp", xp_[:], T["xown"][r0:r0 + 128, w0:w0 + WC], xp_, writes=[xp_])
                        S.op("dve", lambda: V_.tensor_tensor(out=xo_[:], in0=ps[:, 0:WC], in1=g1bc[:, w0:w0 + WC], op=ALU.mult), reads=[ps, g1bc], writes=[xo_])
                        S.op("pool", lambda: G_.tensor_tensor(out=xo_[:], in0=xo_[:], in1=xp_[:], op=ALU.add), reads=[xo_, xp_], writes=[xo_])
                        S.dma("sp", T["X1"][r0:r0 + 128, w0:w0 + WC], xo_[:], xo_, reads=[xo_])
        S.barrier()
        st.close()

    def stage_peer_q():
        st = contextlib.ExitStack()
        HP, PH = c.HP, c.PH
        P = alloc_prologue(st)
        hTt = [S.raw_sbuf(st, "hT%d" % i, [128, KC, TT], BF16) for i in range(1)]
        hTr = Rot([[Buf("hT%d_%d" % (i, s_), hTt[i]) for s_ in range(NSUB)] for i in range(1)])
        Wb = Rot([S.sbuf(st, "W%d" % i, [128, KC, 256], BF16) for i in range(2)])
        keysT = S.sbuf(st, "keysT", [128, HP, 128], F32)
        psg = Rot([S.psum(st, "psg%d" % i, [128, 512]) for i in range(2)])
        pssc = Rot([S.psum(st, "pssc%d" % i, [128, 4, 128]) for i in range(2)])
        pqb = Rot([S.sbuf(st, "pqb%d" % i, [128, TT], F32) for i in range(2)])
        sc = [S.sbuf(st, "sc%d" % i, [128, HP, 128], F32) for i in range(NSUB)]
        mx = S.sbuf(st, "mx", [128, HP], F32)
        eb = Rot([S.sbuf(st, "eb%d" % i, [128, HP, 128], BF16) for i in range(2)])
        ef = S.sbuf(st, "ef", [128, HP, 128], F32)
        wk = S.sbuf(st, "wk", [128, 256], F32)
        tv = S.sbuf(st, "tv", [128, HP, 16], F32)
        cand = S.sbuf(st, "cand", [128, PH, 16, 16], F32)
        tcd = S.sbuf(st, "tcd", [128, PH, 16], F32)
        epz = Rot([S.sbuf(st, "epz%d" % i, [128, 2, PH], F32) for i in range(2)])
        S.dma("sp", keysT[:], T["keysT"], keysT, writes=[keysT])
        for (slen, aoff, ooff, ocnt, grp) in c.seqs:
            for tt in range(ocnt // TT):
                tok0 = ooff + tt * TT
                hTs = hTr.next()
                prologue(P, T["X1"], tok0, grp, A2, B2, hTs)
                hT = hTs[0]
                S.dma("sp", T["H2T"][:, tok0:tok0 + TT].rearrange("(c p) n -> p c n", p=128), hT[:], hT, reads=hTs)
                for w0 in range(0, c.PQ, 256):
                    wn = min(256, c.PQ - w0)
                    W = Wb.next()
                    load_w(W, T["w_pq"], 0, D, w0, wn)
                    for blk in range(wn // 128):
                        hp = (w0 + blk * 128) // 128
                        ps = psg.next()
                        S.pe([mm(ps[:, 0:TT], W[:, kc, blk * 128:(blk + 1) * 128], hT[:, kc, :], kc == 0, kc == KC - 1) for kc in range(KC)],
                             reads=[W] + hTs, writes=[ps])
                        pq = pqb.next()
                        S.op("act", lambda: A_.activation(out=pq[:], in_=ps[:, 0:TT], func=AF.Copy), reads=[ps], writes=[pq])
                        pk = pssc.next()
                        S.pe([mm(pk[:, sub, :], pq[:, sub * 128:(sub + 1) * 128], keysT[:, hp, :], True, True) for sub in range(NSUB)],
                             reads=[pq, keysT], writes=[pk])
                        for sub in range(NSUB):
                            S.op("dve", lambda: V_.tensor_copy(out=sc[sub][:, hp, :], in_=pk[:, sub, :]), reads=[pk], writes=[sc[sub]])
                for sub in range(NSUB):
                    s_ = sc[sub]
                    r0 = tok0 + sub * 128
                    S.op("dve", lambda: V_.tensor_reduce(out=mx[:], in_=s_[:], axis=AX.X, op=ALU.max), reads=[s_], writes=[mx])
                    S.op("dve", lambda: V_.tensor_tensor(out=s_[:], in0=s_[:], in1=mx[:].unsqueeze(2).broadcast_to([128, HP, 128]), op=ALU.subtract),
                         reads=[s_, mx], writes=[s_])
                    e_ = eb.next()
                    S.op("act", lambda: A_.activation(out=e_[:], in_=s_[:], func=AF.Exp), reads=[s_], writes=[e_])
                    S.dma("sp", T["EB"][r0:r0 + 128, :, :], e_[:], e_, reads=[e_])
                    S.op("dve", lambda: V_.tensor_copy(out=ef[:], in_=e_[:]), reads=[e_], writes=[ef])
                    for hp in range(HP):
                        S.op("dve", lambda: V_.max(out=tv[:, hp, 0:8], in_=ef[:, hp, :]), reads=[ef], writes=[tv])
                        S.op("dve", lambda: V_.match_replace(out=wk[:, 0:128], in_to_replace=tv[:, hp, 0:8], in_values=ef[:, hp, :], imm_value=-1.0),
                             reads=[ef, tv], writes=[wk])
                        S.op("dve", lambda: V_.max(out=tv[:, hp, 8:16], in_=wk[:, 0:128]), reads=[wk], writes=[tv])
                    tvv = tv[:].rearrange("p (h two) k -> p h two k", two=2)
                    S.op("dve", lambda: V_.tensor_tensor(out=cand[:], in0=tvv[:, :, 0, :].unsqueeze(3).broadcast_to([128, PH, 16, 16]),
                                                         in1=tvv[:, :, 1, :].unsqueeze(2).broadcast_to([128, PH, 16, 16]), op=ALU.mult),
                         reads=[tv], writes=[cand])
                    for h in range(PH):
                        cv = cand[:, h, :, :].rearrange("p a b -> p (a b)")
                        S.op("dve", lambda: V_.max(out=tcd[:, h, 0:8], in_=cv), reads=[cand], writes=[tcd])
                        S.op("dve", lambda: V_.match_replace(out=wk[:, 0:256], in_to_replace=tcd[:, h, 0:8], in_values=cv, imm_value=-1.0),
                             reads=[cand, tcd], writes=[wk])
                        S.op("dve", lambda: V_.max(out=tcd[:, h, 8:16], in_=wk[:, 0:256]), reads=[wk], writes=[tcd])
                    ez = epz.next()
                    S.op("dve", lambda: V_.tensor_reduce(out=ez[:, 0, :], in_=tcd[:], axis=AX.X, op=ALU.min), reads=[tcd], writes=[ez])
                    S.op("dve", lambda: V_.tensor_reduce(out=ez[:, 1, :], in_=tcd[:], axis=AX.X, op=ALU.add), reads=[tcd], writes=[ez])
                    S.op("dve", lambda: V_.reciprocal(out=ez[:, 1, :], in_=ez[:, 1, :]), reads=[ez], writes=[ez])
                    S.dma("sp", T["EPSD"][r0:r0 + 128, :], ez[:, 0, :], ez, reads=[ez])
                    S.dma("sp", T["RZD"][r0:r0 + 128, :], ez[:, 1, :], ez, reads=[ez])
        S.barrier()
        st.close()

    def stage_peer_e():
        st = contextlib.ExitStack()
        HP, PH = c.HP, c.PH
        SBE = c.SBE
        IB = 2
        h2T = S.sbuf(st, "h2T", [128, KC, TT], BF16)
        ebt = S.sbuf(st, "ebt", [128, NSUB, HP, 128], BF16)
        epst = S.sbuf(st, "epst", [128, NSUB, PH], F32)
        rzt = S.sbuf(st, "rzt", [128, NSUB, PH], F32)
        Dg = S.sbuf(st, "Dg", [128, NSUB * PH, 128], BF16)
        acct = S.raw_sbuf(st, "acc", [128, NSUB, D], F32)
        accs = [Buf("acc%d" % i, acct) for i in range(NSUB)]
        uTt = Rot([S.sbuf(st, "uT%d" % i, [128, KC, 256], BF16) for i in range(2)])
        vt = Rot([S.sbuf(st, "vt%d" % i, [128, SBE, 512], BF16) for i in range(2)])
        AGTt = S.raw_sbuf(st, "AGT", [128, SBE, TT], BF16)
        AGT = [Buf("AGT%d" % i, AGTt) for i in range(SBE)]
        gl = Rot([S.sbuf(st, "gl%d" % i, [128, TT], F32) for i in range(2)])
        Et = Rot([S.sbuf(st, "Et%d" % i, [128, IB, 128], F32) for i in range(3)])
        Gt = [[S.sbuf(st, "G%d_%d" % (s_, h), [128, IB, 128], BF16) for h in range(PH)] for s_ in range(NSUB)]
        psE = Rot([S.psum(st, "psE%d" % i, [128, 512]) for i in range(2)])
        psG = Rot([S.psum(st, "psG%d" % i, [128, 512]) for i in range(2)])
        ps2 = Rot([S.psum(st, "ps2%d" % i, [128, 512]) for i in range(3)])
        for (slen, aoff, ooff, ocnt, grp) in c.seqs:
            for tt in range(ocnt // TT):
                tok0 = ooff + tt * TT
                S.dma("sp", h2T[:], T["H2T"][:, tok0:tok0 + TT].rearrange("(c p) n -> p c n", p=128), h2T, writes=[h2T])
                S.dma("sp", ebt[:], T["EB"][tok0:tok0 + TT, :, :].rearrange("(s p) h n -> p s h n", p=128), ebt, writes=[ebt])
                S.dma("sp", epst[:], T["EPSD"][tok0:tok0 + TT, :].rearrange("(s p) h -> p s h", p=128), epst, writes=[epst])
                S.dma("sp", rzt[:], T["RZD"][tok0:tok0 + TT, :].rearrange("(s p) h -> p s h", p=128), rzt, writes=[rzt])
                for sub in range(NSUB):
                    for h in range(PH):
                        S.op("dve", lambda: V_.tensor_scalar(out=Dg[:, sub * PH + h, :], in0=identf[:], scalar1=rzt[:, sub, h:h + 1],
                                                             scalar2=None, op0=ALU.mult), reads=[identf, rzt], writes=[Dg])
                for sbk in range(c.NE // (128 * SBE)):
                    for ib in range(SBE // IB):
                        i0 = sbk * SBE + ib * IB
                        for sub in range(NSUB):
                            for h in range(PH):
                                E = Et.next()
                                G = Gt[sub][h]
                                S.op("pool", lambda: G_.tensor_tensor(
                                    out=E[:], in0=ebt[:, sub, 2 * h, i0:i0 + IB].unsqueeze(2).broadcast_to([128, IB, 128]),
                                    in1=ebt[:, sub, 2 * h + 1, :].unsqueeze(1).broadcast_to([128, IB, 128]), op=ALU.mult),
                                    reads=[ebt], writes=[E])
                                S.op("dve", lambda: V_.scalar_tensor_tensor(out=G[:], in0=E[:], scalar=epst[:, sub, h:h + 1], in1=E[:],
                                                                            op0=ALU.is_ge, op1=ALU.mult), reads=[E, epst], writes=[G])
                        ut = uTt.next()
                        load_wb(ut, T["UTb"][i0 // IB], KC, 256)
                        for ii in range(IB):
                            ci = ib * IB + ii
                            ch = sbk * SBE + ci
                            pe_ = psE.next()
                            S.pe([mm(pe_[:, 0:TT], ut[:, kc, ii * 128:(ii + 1) * 128], h2T[:, kc, :], kc == 0, kc == KC - 1) for kc in range(KC)],
                                 reads=[ut, h2T], writes=[pe_])
                            g_ = gl.next()
                            S.op("act", lambda: A_.activation(out=g_[:], in_=pe_[:, 0:TT], func=AF.Gelu), reads=[pe_], writes=[g_])
                            pg = psG.next()
                            fns = []
                            for sub in range(NSUB):
                                for h in range(PH):
                                    fns.append(mm(pg[:, sub * 128:(sub + 1) * 128], Gt[sub][h][:, ii, :], Dg[:, sub * PH + h, :], h == 0, h == PH - 1))
                            S.pe(fns, reads=[Dg] + [Gt[s_][h] for s_ in range(NSUB) for h in range(PH)], writes=[pg])
                            S.op("dve", lambda: V_.tensor_tensor(out=AGT[ci][:, ci, :], in0=pg[:, 0:TT], in1=g_[:], op=ALU.mult),
                                 reads=[pg, g_], writes=[AGT[ci]])
                    for w0 in range(0, D, 512):
                        wn = min(512, D - w0)
                        v_ = vt.next()
                        load_wb(v_, T["VBb"][sbk, w0 // 512], SBE, wn)
                        for sub in range(NSUB):
                            p2 = ps2.next()
                            S.pe([mm(p2[:, 0:wn], AGT[ci][:, ci, sub * 128:(sub + 1) * 128], v_[:, ci, 0:wn], ci == 0, ci == SBE - 1) for ci in range(SBE)],
                                 reads=[v_] + AGT, writes=[p2])
                            a_ = accs[sub]
                            if sbk == 0:
                                S.op("dve", lambda: V_.tensor_copy(out=a_[:, sub, w0:w0 + wn], in_=p2[:, 0:wn]), reads=[p2], writes=[a_])
                            else:
                                S.op("dve", lambda: V_.tensor_tensor(out=a_[:, sub, w0:w0 + wn], in0=a_[:, sub, w0:w0 + wn], in1=p2[:, 0:wn], op=ALU.add),
                                     reads=[p2, a_], writes=[a_])
                for sub in range(NSUB):
                    r0 = tok0 + sub * 128
                    S.dma("sp", T["PO"][r0:r0 + 128, :], accs[sub][:, sub, :], accs[sub], reads=[accs[sub]])
        S.barrier()
        st.close()

    def stage_final():
        st = contextlib.ExitStack()
        g2bc = S.sbuf(st, "g2bc", [128, D], F32)
        fgbc = S.sbuf(st, "fgbc", [128, D], F32)
        x1 = Rot([S.sbuf(st, "x1_%d" % i, [128, D], F32) for i in range(2)])
        po = Rot([S.sbuf(st, "po_%d" % i, [128, D], F32) for i in range(2)])
        junk = S.sbuf(st, "junkf", [128, D], BF16)
        ss = Rot([S.sbuf(st, "ssf%d" % i, [128, 1], F32) for i in range(2)])
        S.dma("sp", fgbc[:], T["fg"][0:1, :].partition_broadcast(128), fgbc, writes=[fgbc])
        for (slen, aoff, ooff, ocnt, grp) in c.seqs:
            S.dma("sp", g2bc[:], T["G2"][grp:grp + 1, :].partition_broadcast(128), g2bc, writes=[g2bc])
            for t in range(ocnt // 128):
                r0 = ooff + t * 128
                a, b = x1.next(), po.next()
                S.dma("sp", a[:], T["X1"][r0:r0 + 128, :], a, writes=[a])
                S.dma("sp", b[:], T["PO"][r0:r0 + 128, :], b, writes=[b])
                S.op("dve", lambda: V_.tensor_tensor(out=b[:], in0=b[:], in1=g2bc[:], op=ALU.mult), reads=[b, g2bc], writes=[b])
                S.op("pool", lambda: G_.tensor_tensor(out=a[:], in0=a[:], in1=b[:], op=ALU.add), reads=[a, b], writes=[a])
                s_ = ss.next()
                S.op("dve", lambda: V_.memset(s_[:], 0.0), writes=[s_])
                S.op("act", lambda: A_.activation(out=junk[:], in_=a[:], func=AF.Square, accum_out=s_[:]), reads=[a, s_], writes=[junk, s_])
                S.op("dve", lambda: V_.tensor_scalar(out=s_[:], in0=s_[:], scalar1=1.0 / D, scalar2=c.EPS, op0=ALU.mult, op1=ALU.add), reads=[s_], writes=[s_])
                S.op("act", lambda: A_.sqrt(out=s_[:], in_=s_[:]), reads=[s_], writes=[s_])
                S.op("dve", lambda: V_.reciprocal(out=s_[:], in_=s_[:]), reads=[s_], writes=[s_])
                S.op("dve", lambda: V_.scalar_tensor_tensor(out=b[:], in0=a[:], scalar=s_[:, 0:1], in1=fgbc[:], op0=ALU.mult, op1=ALU.mult),
                     reads=[a, s_, fgbc], writes=[b])
                S.dma("sp", T["y"][r0:r0 + 128, :], b[:], b, reads=[b])
        S.barrier()
        st.close()

    stages = [stage_mod, stage_kvf, stage_qg, stage_attn, stage_dft, stage_merge, stage_peer_q, stage_peer_e, stage_final]
    for i, sfn in enumerate(stages):
        if i < getattr(c, "nstages", 99):
            sfn()
    S.drain()
    top.close()
    return nc


def _rope_tables(cfg, positions):
    axis_dim = 64
    inv_freq = (10000.0 ** (-np.arange(0, axis_dim, 2, dtype=np.float32) / axis_dim)).astype(np.float32)
    pos = np.asarray(positions)
    row = (pos // cfg.GRID_W).astype(np.float32)
    col = (pos % cfg.GRID_W).astype(np.float32)
    ang = np.concatenate([row[:, None] * inv_freq, col[:, None] * inv_freq], axis=-1).astype(np.float32)
    cos = np.cos(ang).astype(np.float32)
    sin = np.sin(ang).astype(np.float32)
    return np.ascontiguousarray(np.repeat(cos, 2, axis=1).T), np.ascontiguousarray(np.repeat(sin, 2, axis=1).T)


def _dft_tables(S_len, own_pos):
    s = np.arange(S_len, dtype=np.int64)[:, None]
    k = np.asarray(own_pos, dtype=np.int64)[None, :]
    ang = 2.0 * np.pi * ((s * k) % S_len).astype(np.float64) / S_len
    sc = 1.0 / np.sqrt(S_len)
    return (np.cos(ang) * sc).astype(ml_dtypes.bfloat16), (np.sin(ang) * sc).astype(ml_dtypes.bfloat16)


_NC_CACHE = {}


def run(cfg, x_prompt, x_sample, c_prompt, c_sample, w_ada, b_ada, norm1_g, norm2_g, w_in, q_norm_g, k_norm_g,
        w_attn_br, w_four_br, w_out, w_peer_q, peer_keys, peer_u, peer_v, final_g, return_all=False):
    c = cfg
    f32 = np.float32
    A = lambda a: np.ascontiguousarray(np.asarray(a, dtype=f32))
    x_prompt, x_sample = A(x_prompt), A(x_sample)
    key = (c.D, c.SP, c.SS, c.NH, c.NKV, c.FG, c.FGD, c.PH, c.TT, c.debug, getattr(c, "nstages", 99))
    if key not in _NC_CACHE:
        _NC_CACHE[key] = build(c)
    nc = _NC_CACHE[key]
    shared = {
        "w_ada": A(w_ada[0]), "b_ada": A(b_ada[0])[None, :], "n1g": A(norm1_g[0])[None, :], "n2g": A(norm2_g[0])[None, :],
        "fg": A(final_g)[None, :], "w_in": A(w_in[0]), "qg": A(q_norm_g[0]).reshape(128, 1), "kg": A(k_norm_g[0]).reshape(128, 1),
        "w_attn": A(w_attn_br[0]), "w_four": A(w_four_br[0]), "w_out": A(w_out[0]), "w_pq": A(w_peer_q[0]),
        "keysT": np.ascontiguousarray(A(peer_keys[0]).reshape(c.HP, 128, 128).transpose(2, 0, 1)),
        "uT": np.ascontiguousarray(A(peer_u[0]).T), "vtab": A(peer_v[0]),
    }
    R = np.zeros((128, 128), f32)
    for i in range(64):
        R[2 * i + 1, 2 * i] = -1.0
        R[2 * i, 2 * i + 1] = 1.0
    shared["rotR"] = R
    cosA, sinA = _rope_tables(c, np.arange(c.SS))
    shared["cosA"], shared["sinA"] = cosA, sinA
    n = np.arange(c.FGD, dtype=np.int64)
    angc = 2.0 * np.pi * ((n[:, None] * n[None, :]) % c.FGD).astype(np.float64) / c.FGD
    shared["dftC_c"] = (np.cos(angc) / np.sqrt(c.FGD)).astype(ml_dtypes.bfloat16)
    shared["dftC_s"] = (-np.sin(angc) / np.sqrt(c.FGD)).astype(ml_dtypes.bfloat16)
    in_maps = []
    for core in range(8):
        b, r = core // 4, core % 4
        posP = np.arange(r * c.OP, (r + 1) * c.OP)
        posS = np.arange(r * c.OS, (r + 1) * c.OS)
        m = dict(shared)
        m["xall"] = np.concatenate([x_prompt[b], x_sample[b]], axis=0)
        m["xown"] = np.concatenate([x_prompt[b, posP], x_sample[b, posS]], axis=0)
        cT = np.stack([A(c_prompt[b]).reshape(c.KC, 128).T, A(c_sample[b]).reshape(c.KC, 128).T], axis=-1)
        m["cT"] = np.ascontiguousarray(cT)
        cp, sp_ = _rope_tables(c, posP)
        cs_, ss_ = _rope_tables(c, posS)
        m["cosO"] = np.ascontiguousarray(np.concatenate([cp, cs_], axis=1))
        m["sinO"] = np.ascontiguousarray(np.concatenate([sp_, ss_], axis=1))
        m["dftP_c"], m["dftP_s"] = _dft_tables(c.SP, posP)
        m["dftS_c"], m["dftS_s"] = _dft_tables(c.SS, posS)
        in_maps.append(m)
    res = run_bass_kernel_spmd(nc, in_maps, core_ids=list(range(8)))
    yp = np.zeros((2, c.SP, c.D), f32)
    ys = np.zeros((2, c.SS, c.D), f32)
    for core in range(8):
        b, r = core // 4, core % 4
        y = res.results[core]["y"]
        yp[b, r * c.OP:(r + 1) * c.OP] = y[0:c.OP]
        ys[b, r * c.OS:(r + 1) * c.OS] = y[c.OP:]
    if return_all:
        return (yp, ys), res.results
    return (yp, ys)


def kernel(**inputs):
    return run(Cfg(), **inputs)
```

```python
import contextlib
import numpy as np
import ml_dtypes
import concourse.bass as bass
import concourse.mybir as mybir
from concourse.bass_utils import run_bass_kernel_spmd

F32 = mybir.dt.float32
BF16 = mybir.dt.bfloat16
AF = mybir.ActivationFunctionType
ALU = mybir.AluOpType
AX = mybir.AxisListType


class Buf:
    __slots__ = ("name", "t", "w", "r", "dsem", "dcnt")

    def __init__(self, name, t):
        self.name = name
        self.t = t
        self.w = None
        self.r = {}
        self.dsem = None
        self.dcnt = 0

    def __getitem__(self, k):
        return self.t[k]


class Sync:
    ENG = ("pe", "act", "dve", "pool", "sp")

    def __init__(self, nc, stack):
        self.nc = nc
        self.stack = stack
        self.eng = {"pe": nc.tensor, "act": nc.scalar, "dve": nc.vector, "pool": nc.gpsimd, "sp": nc.sync}
        self.esem = {}
        for e in ("pe", "act", "dve", "pool"):
            self.esem[e] = stack.enter_context(nc.semaphore("es_" + e))
        self.cnt = {e: 0 for e in self.ENG}
        self.seen = {e: {} for e in self.ENG}
        self.out_dma = {e: {} for e in self.ENG}
        self.free_dsems = []
        self.stage_dsems = []
        self.nsem = 0

    def uname(self, name):
        self.nname = getattr(self, "nname", 0) + 1
        return "s%d_%s" % (self.nname, name)

    def raw_sbuf(self, stack, name, shape, dt):
        return stack.enter_context(self.nc.sbuf_tensor(self.uname(name), list(shape), dt))

    def sbuf(self, stack, name, shape, dt):
        return Buf(name, self.raw_sbuf(stack, name, shape, dt))

    def psum(self, stack, name, shape, dt=F32):
        t = stack.enter_context(self.nc.psum_tensor(self.uname(name), list(shape), dt))
        return Buf(name, t)

    def _dsem(self, b):
        if b.dsem is None:
            if self.free_dsems:
                b.dsem = self.free_dsems.pop()
            else:
                self.nsem += 1
                b.dsem = [self.stack.enter_context(self.nc.semaphore("ds%d" % self.nsem)), 0]
            self.stage_dsems.append(b.dsem)
        return b.dsem

    def _wait(self, e, tok):
        sem, val = tok
        k = id(sem)
        if self.seen[e].get(k, 0) >= val:
            return
        self.eng[e].wait_ge(sem, val)
        self.seen[e][k] = val

    def _deps(self, e, reads, writes, is_dma=False):
        own = None if is_dma else self.esem.get(e)
        toks = []
        for b in reads:
            if b.w is not None:
                toks.append(b.w)
        for b in writes:
            if b.w is not None and b.w[0] is not own:
                toks.append(b.w)
            toks.extend(t for t in b.r.values() if t[0] is not own)
        for t in toks:
            if e == "pe" and t[0] is own:
                continue
            self._wait(e, t)

    def _commit(self, tok, reads, writes):
        k = id(tok[0])
        for b in reads:
            b.r[k] = tok
        for b in writes:
            b.w = tok
            b.r = {}

    def op(self, e, fn, reads=(), writes=()):
        self._deps(e, reads, writes)
        ins = fn()
        self.cnt[e] += 1
        ins.then_inc(self.esem[e], 1)
        self._commit((self.esem[e], self.cnt[e]), reads, writes)

    def pe(self, fns, reads=(), writes=()):
        self._deps("pe", reads, writes)
        ins = None
        for fn in fns:
            ins = fn()
        self.cnt["pe"] += 1
        ins.then_inc(self.esem["pe"], 1)
        self._commit((self.esem["pe"], self.cnt["pe"]), reads, writes)

    def dma(self, q, out, in_, tokbuf, reads=(), writes=()):
        self._deps(q, reads, writes, is_dma=True)
        cell = self._dsem(tokbuf)
        sem = cell[0]
        ins = self.eng[q].dma_start(out=out, in_=in_)
        cell[1] += 16
        ins.then_inc(sem, 16)
        tok = (sem, cell[1])
        self._commit(tok, reads, writes)
        self.out_dma[q][id(sem)] = tok

    def drain(self):
        for q in self.ENG:
            for tok in self.out_dma[q].values():
                self._wait(q, tok)
            self.out_dma[q] = {}
        for e in ("act", "dve", "pool"):
            if self.cnt[e] > 0:
                self._wait(e, (self.esem[e], self.cnt[e]))

    def barrier(self):
        self.drain()
        self.nc.all_engine_barrier()
        self.free_dsems.extend(self.stage_dsems)
        self.stage_dsems = []


class Cfg:
    def __init__(self, D=4096, SP=4096, SS=8192, NH=32, NKV=8, FG=8, FGD=256, PH=8, GRID_W=64,
                 TT=512, WC=512, debug=False):
        self.D, self.SP, self.SS, self.NH, self.NKV = D, SP, SS, NH, NKV
        self.FG, self.FGD, self.PH, self.GRID_W, self.TT, self.WC = FG, FGD, PH, GRID_W, TT, WC
        self.debug = debug
        self.KC = D // 128
        self.ATT = NH * 128
        self.KVW = NKV * 128
        self.FW = FG * FGD
        self.INW = self.ATT + 2 * self.KVW + self.FW + 2 * D
        self.cQ, self.cK = 0, self.ATT
        self.cV = self.cK + self.KVW
        self.cF = self.cV + self.KVW
        self.cGA = self.cF + self.FW
        self.cGF = self.cGA + D
        self.OP, self.OS = SP // 4, SS // 4
        self.NOWN = self.OP + self.OS
        self.NALL = SP + SS
        self.HP = PH * 2
        self.PQ = self.HP * 128
        self.NE = 128 * 128
        self.EPS = 1e-6
        self.seqs = [(SP, 0, 0, self.OP, 0), (SS, SP, self.OP, self.OS, 1)]


class Rot:
    def __init__(self, bufs):
        self.bufs = bufs
        self.i = -1

    def next(self):
        self.i = (self.i + 1) % len(self.bufs)
        return self.bufs[self.i]


def build(cfg):
    c = cfg
    D, KC, TT = c.D, c.KC, c.TT
    NSUB = TT // 128
    nc = bass.Bass("TRN2", target_bir_lowering=False)
    T = {}

    def din(name, shape, dt=F32):
        T[name] = nc.dram_tensor(name, list(shape), dt, kind="ExternalInput").ap()

    def dscr(name, shape, dt):
        kind = "ExternalOutput" if c.debug else "Internal"
        T[name] = nc.dram_tensor(name, list(shape), dt, kind=kind).ap()

    din("xall", [c.NALL, D]); din("xown", [c.NOWN, D])
    din("cT", [128, KC, 2])
    din("w_ada", [D, 6 * D]); din("b_ada", [1, 6 * D])
    din("n1g", [1, D]); din("n2g", [1, D]); din("fg", [1, D])
    din("w_in", [D, c.INW])
    din("qg", [128, 1]); din("kg", [128, 1])
    din("w_attn", [c.ATT, D]); din("w_four", [c.FW, D]); din("w_out", [D, D]); din("w_pq", [D, c.PQ])
    din("keysT", [128, c.HP, 128])
    din("uT", [D, c.NE]); din("vtab", [c.NE, D])
    din("cosA", [128, c.SS]); din("sinA", [128, c.SS]); din("cosO", [128, c.NOWN]); din("sinO", [128, c.NOWN])
    din("rotR", [128, 128])
    din("dftP_c", [c.SP, c.OP], BF16); din("dftP_s", [c.SP, c.OP], BF16)
    din("dftS_c", [c.SS, c.OS], BF16); din("dftS_s", [c.SS, c.OS], BF16)
    din("dftC_c", [c.FGD, c.FGD], BF16); din("dftC_s", [c.FGD, c.FGD], BF16)
    T["y"] = nc.dram_tensor("y", [c.NOWN, D], F32, kind="ExternalOutput").ap()
    dscr("G1", [2, D], F32); dscr("G2", [2, D], F32)
    dscr("KT", [c.NKV, 128, c.NALL], BF16); dscr("V", [c.NALL, c.KVW], BF16); dscr("U", [c.NALL, c.FW], BF16)
    dscr("QT", [c.NH, 128, c.NOWN], BF16); dscr("GAT", [D, c.NOWN], BF16); dscr("GFT", [D, c.NOWN], BF16)
    dscr("ATT_T", [c.ATT, c.NOWN], BF16); dscr("YT", [c.FW, c.NOWN], BF16)
    dscr("X1", [c.NOWN, D], F32); dscr("H2T", [D, c.NOWN], BF16)
    dscr("EB", [c.NOWN, c.HP, 128], BF16); dscr("EPSD", [c.NOWN, c.PH], F32); dscr("RZD", [c.NOWN, c.PH], F32)
    dscr("PO", [c.NOWN, D], F32)
    WCM = 256
    c.SBE = 8
    T["WAb"] = nc.dram_tensor("WAb", [D // WCM, 128, (c.ATT // 128) * WCM], BF16, kind="Internal").ap()
    T["WFb"] = nc.dram_tensor("WFb", [D // WCM, 128, (c.FW // 128) * WCM], BF16, kind="Internal").ap()
    T["WOb"] = nc.dram_tensor("WOb", [D // WCM, 128, KC * WCM], BF16, kind="Internal").ap()
    T["UTb"] = nc.dram_tensor("UTb", [c.NE // 128, 128, KC * 128], BF16, kind="Internal").ap()
    VW = min(512, D)
    T["VBb"] = nc.dram_tensor("VBb", [c.NE // (128 * c.SBE), D // VW, 128, c.SBE * VW], BF16, kind="Internal").ap()
    prep_items = []
    for j in range(D // WCM):
        prep_items.append(("w_attn", 0, c.ATT, j * WCM, WCM, T["WAb"][j]))
        prep_items.append(("w_four", 0, c.FW, j * WCM, WCM, T["WFb"][j]))
        prep_items.append(("w_out", 0, D, j * WCM, WCM, T["WOb"][j]))
    for j in range(c.NE // 128):
        prep_items.append(("uT", 0, D, j * 128, 128, T["UTb"][j]))
    for sbk in range(c.NE // (128 * c.SBE)):
        for j in range(D // VW):
            prep_items.append(("vtab", sbk * c.SBE * 128, c.SBE * 128, j * VW, VW, T["VBb"][sbk, j]))
    prep_state = {"i": 0}
    kvf_tiles = []
    for (cbase, width) in ((c.cK, c.KVW), (c.cV, c.KVW), (c.cF, c.FW)):
        for w0 in range(0, width, 512):
            wn = min(512, width - w0)
            tname = "WKb%d" % len(kvf_tiles)
            T[tname] = nc.dram_tensor(tname, [128, KC * wn], BF16, kind="Internal").ap()
            kvf_tiles.append((cbase + w0, wn, T[tname]))

    V_, A_, G_, PE_ = nc.vector, nc.scalar, nc.gpsimd, nc.tensor
    top = contextlib.ExitStack()
    S = Sync(nc, top)

    def mm(out, lhsT, rhs, st, sp):
        return lambda: PE_.matmul(out, lhsT=lhsT, rhs=rhs, start=st, stop=sp)

    def load_w(Wb, Wd, r0, nrows, c0, ncols, q="pool"):
        kcs = nrows // 128
        step = 8
        for k0 in range(0, kcs, step):
            k1 = min(kcs, k0 + step)
            S.dma(q, Wb[:, k0:k1, 0:ncols],
                  Wd[r0 + k0 * 128:r0 + k1 * 128, c0:c0 + ncols].rearrange("(c p) n -> p c n", p=128),
                  Wb, writes=[Wb])

    def prep_emit(bufs, n):
        for _ in range(n):
            if prep_state["i"] >= len(prep_items):
                return
            (wname, r0, nrows, c0, ncols, dst) = prep_items[prep_state["i"]]
            prep_state["i"] += 1
            b = bufs.next()
            kcs = nrows // 128
            bv = b[:, 0:kcs * ncols].rearrange("p (c n) -> p c n", n=ncols)
            step = 8
            for k0 in range(0, kcs, step):
                k1 = min(kcs, k0 + step)
                S.dma("pool", bv[:, k0:k1, :],
                      T[wname][r0 + k0 * 128:r0 + k1 * 128, c0:c0 + ncols].rearrange("(c p) n -> p c n", p=128),
                      b, writes=[b])
            S.dma("pool", dst, b[:, 0:kcs * ncols], b, reads=[b])

    def load_wb(Wb, src, kcs, ncols):
        S.dma("sp", Wb[:, 0:kcs, 0:ncols], src.rearrange("p (c n) -> p c n", n=ncols), Wb, writes=[Wb])

    ident = S.sbuf(top, "ident", [128, 128], BF16)
    identf = S.sbuf(top, "identf", [128, 128], F32)
    onesf = S.sbuf(top, "onesf", [128, 128], F32)
    onesb = S.sbuf(top, "onesb", [128, 128], BF16)
    rotR = S.sbuf(top, "rotR", [128, 128], F32)
    qg = S.sbuf(top, "qg", [128, 1], F32)
    kg = S.sbuf(top, "kg", [128, 1], F32)
    A1 = S.sbuf(top, "A1", [128, 2, KC], F32); B1 = S.sbuf(top, "B1", [128, 2, KC], F32)
    A2 = S.sbuf(top, "A2", [128, 2, KC], F32); B2 = S.sbuf(top, "B2", [128, 2, KC], F32)
    S.op("pool", lambda: G_.memset(identf[:], 1.0), writes=[identf])
    S.op("pool", lambda: G_.affine_select(out=identf[:], in_=identf[:], pattern=[[-1, 128]], compare_op=ALU.is_equal,
                                          fill=0.0, base=0, channel_multiplier=1), reads=[identf], writes=[identf])
    S.op("dve", lambda: V_.tensor_copy(out=ident[:], in_=identf[:]), reads=[identf], writes=[ident])
    S.op("dve", lambda: V_.memset(onesf[:], 1.0), writes=[onesf])
    S.op("dve", lambda: V_.memset(onesb[:], 1.0), writes=[onesb])
    S.dma("sp", rotR[:], T["rotR"], rotR, writes=[rotR])
    S.dma("sp", qg[:], T["qg"], qg, writes=[qg])
    S.dma("sp", kg[:], T["kg"], kg, writes=[kg])

    def stage_mod():
        st = contextlib.ExitStack()
        cTf = S.sbuf(st, "cTf", [128, KC, 2], F32)
        cs = S.sbuf(st, "cs", [128, KC, 2], BF16)
        Wb = Rot([S.sbuf(st, "Wm%d" % i, [128, KC, 512], BF16) for i in range(2)])
        psr = Rot([S.psum(st, "psr%d" % i, [128, 512]) for i in range(2)])
        psc = Rot([S.psum(st, "psc%d" % i, [128, 4]) for i in range(2)])
        brow = Rot([S.sbuf(st, "brow%d" % i, [1, 512], F32) for i in range(2)])
        grow = Rot([S.sbuf(st, "grow%d" % i, [1, 512], F32) for i in range(2)])
        row = Rot([S.sbuf(st, "row%d" % i, [1, 512], F32) for i in range(4)])
        S.dma("sp", cTf[:], T["cT"], cTf, writes=[cTf])
        S.op("act", lambda: A_.activation(out=cs[:], in_=cTf[:], func=AF.Silu), reads=[cTf], writes=[cs])
        ntile = 6 * D // 512
        kbuf = Rot([S.sbuf(st, "kprep%d" % i, [128, KC * 512], BF16) for i in range(2)])
        kpending = list(kvf_tiles)
        for j in range(ntile):
            W = Wb.next()
            load_w(W, T["w_ada"], 0, D, j * 512, 512)
            if kpending and (j % max(1, ntile // (len(kvf_tiles) + 1)) == 0 or ntile - j <= len(kpending)):
                (kc0, kwn, kdst) = kpending.pop(0)
                kb = kbuf.next()
                kbv = kb[:, 0:KC * kwn].rearrange("p (c n) -> p c n", n=kwn)
                for k0 in range(0, KC, 8):
                    k1 = min(KC, k0 + 8)
                    S.dma("pool", kbv[:, k0:k1, :], T["w_in"][k0 * 128:k1 * 128, kc0:kc0 + kwn].rearrange("(c p) n -> p c n", p=128),
                          kb, writes=[kb])
                S.dma("pool", kdst, kb[:, 0:KC * kwn], kb, reads=[kb])
            kind = (j * 512) // D
            off = (j * 512) % D
            bb = brow.next()
            S.dma("sp", bb[:], T["b_ada"][0:1, j * 512:(j + 1) * 512], bb, writes=[bb])
            gg = None
            if kind in (1, 4):
                gg = grow.next()
                S.dma("sp", gg[:], T["n1g" if kind == 1 else "n2g"][0:1, off:off + 512], gg, writes=[gg])
            for grp in range(2):
                ps = psr.next()
                S.pe([mm(ps[0:1, :], cs[:, kc, grp:grp + 1], W[:, kc, :], kc == 0, kc == KC - 1) for kc in range(KC)],
                     reads=[cs, W], writes=[ps])
                r = row.next()
                S.op("dve", lambda: V_.tensor_tensor(out=r[:], in0=ps[0:1, :], in1=bb[:], op=ALU.add), reads=[ps, bb], writes=[r])
                if kind in (2, 5):
                    S.dma("sp", T["G1" if kind == 2 else "G2"][grp:grp + 1, off:off + 512], r[:], r, reads=[r])
                    continue
                if kind in (1, 4):
                    S.op("dve", lambda: V_.scalar_tensor_tensor(out=r[:], in0=r[:], scalar=1.0, in1=gg[:], op0=ALU.add, op1=ALU.mult),
                         reads=[r, gg], writes=[r])
                pc = psc.next()
                S.pe([mm(pc[:, q:q + 1], r[0:1, q * 128:(q + 1) * 128], onesf[0:1, 0:1], True, True) for q in range(4)],
                     reads=[r, onesf], writes=[pc])
                dst = {0: B1, 1: A1, 3: B2, 4: A2}[kind]
                k0 = off // 128
                S.op("dve", lambda: V_.tensor_copy(out=dst[:, grp, k0:k0 + 4], in_=pc[:]), reads=[pc], writes=[dst])
        S.barrier()
        st.close()

    G8 = min(8, KC)

    def alloc_prologue(st):
        P = {}
        P["xs"] = Rot([S.sbuf(st, "xs%d" % i, [128, D], F32) for i in range(2)])
        P["ss"] = Rot([S.sbuf(st, "ss%d" % i, [128, 1], F32) for i in range(2)])
        P["xn"] = Rot([S.sbuf(st, "xn%d" % i, [128, D], BF16) for i in range(2)])
        P["ptr"] = Rot([S.psum(st, "ptr%d" % i, [128, G8, 128], BF16) for i in range(2)])
        return P

    def prologue(P, xsrc, tok0, grp, Acol, Bcol, hTs):
        for sub in range(NSUB):
            xb = P["xs"].next()
            S.dma("sp", xb[:], xsrc[tok0 + sub * 128:tok0 + (sub + 1) * 128, :], xb, writes=[xb])
            ssb = P["ss"].next()
            S.op("dve", lambda: V_.memset(ssb[:], 0.0), writes=[ssb])
            xnb = P["xn"].next()
            S.op("act", lambda: A_.activation(out=xnb[:], in_=xb[:], func=AF.Square, accum_out=ssb[:]),
                 reads=[xb, ssb], writes=[xnb, ssb])
            S.op("dve", lambda: V_.tensor_scalar(out=ssb[:], in0=ssb[:], scalar1=1.0 / D, scalar2=c.EPS, op0=ALU.mult, op1=ALU.add),
                 reads=[ssb], writes=[ssb])
            S.op("act", lambda: A_.sqrt(out=ssb[:], in_=ssb[:]), reads=[ssb], writes=[ssb])
            S.op("dve", lambda: V_.reciprocal(out=ssb[:], in_=ssb[:]), reads=[ssb], writes=[ssb])
            S.op("dve", lambda: V_.tensor_scalar(out=xnb[:], in0=xb[:], scalar1=ssb[:, 0:1], scalar2=None, op0=ALU.mult),
                 reads=[xb, ssb], writes=[xnb])
            hb = hTs[sub]
            for kg_ in range(KC // G8):
                pt = P["ptr"].next()
                S.pe([(lambda j=j: PE_.transpose(out=pt[:, j, :], in_=xnb[:, (kg_ * G8 + j) * 128:(kg_ * G8 + j + 1) * 128], identity=ident[:]))
                      for j in range(G8)], reads=[xnb, ident], writes=[pt])
                for j in range(G8):
                    kc = kg_ * G8 + j
                    o = hb[:, kc, sub * 128:(sub + 1) * 128]
                    if kg_ % 2 == 0:
                        S.op("act", lambda: A_.activation(out=o, in_=pt[:, j, :], func=AF.Identity,
                                                          scale=Acol[:, grp, kc:kc + 1], bias=Bcol[:, grp, kc:kc + 1]),
                             reads=[pt, Acol, Bcol], writes=[hb])
                    else:
                        S.op("dve", lambda: V_.tensor_scalar(out=o, in0=pt[:, j, :], scalar1=Acol[:, grp, kc:kc + 1],
                                                             scalar2=Bcol[:, grp, kc:kc + 1], op0=ALU.mult, op1=ALU.add),
                             reads=[pt, Acol, Bcol], writes=[hb])

    def alloc_rope(st):
        Rp = {}
        Rp["xsb"] = Rot([S.sbuf(st, "rxs%d" % i, [128, TT], F32) for i in range(2)])
        Rp["sqb"] = Rot([S.sbuf(st, "rsq%d" % i, [128, TT], F32) for i in range(1)])
        Rp["rsb"] = Rot([S.sbuf(st, "rrs%d" % i, [128, TT], F32) for i in range(1)])
        Rp["t1"] = Rot([S.sbuf(st, "rt1%d" % i, [128, TT], F32) for i in range(1)])
        Rp["t2"] = Rot([S.sbuf(st, "rt2%d" % i, [128, TT], F32) for i in range(1)])
        Rp["pss"] = Rot([S.psum(st, "pss%d" % i, [128, TT]) for i in range(1)])
        Rp["psw"] = Rot([S.psum(st, "psw%d" % i, [128, TT]) for i in range(1)])
        return Rp

    def rope_epi(Rp, ps, gcol, cosb, sinb, ob):
        xsb, sqb, rsb, t1, t2 = Rp["xsb"].next(), Rp["sqb"].next(), Rp["rsb"].next(), Rp["t1"].next(), Rp["t2"].next()
        pss, psw = Rp["pss"].next(), Rp["psw"].next()
        S.op("act", lambda: A_.activation(out=xsb[:], in_=ps[:, 0:TT], func=AF.Identity, scale=gcol[:, 0:1]), reads=[ps, gcol], writes=[xsb])
        S.op("act", lambda: A_.activation(out=sqb[:], in_=ps[:, 0:TT], func=AF.Square), reads=[ps], writes=[sqb])
        S.pe([mm(pss[:], onesf[:], sqb[:], True, True)], reads=[onesf, sqb], writes=[pss])
        S.pe([mm(psw[:], rotR[:], xsb[:], True, True)], reads=[rotR, xsb], writes=[psw])
        S.op("dve", lambda: V_.tensor_scalar(out=rsb[:], in0=pss[:], scalar1=1.0 / 128, scalar2=c.EPS, op0=ALU.mult, op1=ALU.add),
             reads=[pss], writes=[rsb])
        S.op("act", lambda: A_.sqrt(out=rsb[:], in_=rsb[:]), reads=[rsb], writes=[rsb])
        S.op("dve", lambda: V_.reciprocal(out=rsb[:], in_=rsb[:]), reads=[rsb], writes=[rsb])
        S.op("dve", lambda: V_.tensor_tensor(out=t1[:], in0=xsb[:], in1=cosb[:], op=ALU.mult), reads=[xsb, cosb], writes=[t1])
        S.op("dve", lambda: V_.tensor_tensor(out=t2[:], in0=psw[:], in1=sinb[:], op=ALU.mult), reads=[psw, sinb], writes=[t2])
        S.op("dve", lambda: V_.tensor_tensor(out=t1[:], in0=t1[:], in1=t2[:], op=ALU.add), reads=[t1, t2], writes=[t1])
        S.op("dve", lambda: V_.tensor_tensor(out=ob[:], in0=t1[:], in1=rsb[:], op=ALU.mult), reads=[t1, rsb], writes=[ob])

    def stage_kvf():
        st = contextlib.ExitStack()
        P = alloc_prologue(st)
        Rp = alloc_rope(st)
        hTt = [S.raw_sbuf(st, "hT%d" % i, [128, KC, TT], BF16) for i in range(2)]
        hTr = Rot([[Buf("hT%d_%d" % (i, s_), hTt[i]) for s_ in range(NSUB)] for i in range(2)])
        Wb = Rot([S.sbuf(st, "W%d" % i, [128, KC, 512], BF16) for i in range(2)])
        psg = Rot([S.psum(st, "psg%d" % i, [128, 512]) for i in range(2)])
        cosb = Rot([S.sbuf(st, "cos%d" % i, [128, TT], F32) for i in range(2)])
        sinb = Rot([S.sbuf(st, "sin%d" % i, [128, TT], F32) for i in range(2)])
        ko = Rot([S.sbuf(st, "ko%d" % i, [128, TT], BF16) for i in range(2)])
        vo = Rot([S.sbuf(st, "vo%d" % i, [128, 512], BF16) for i in range(3)])
        for (slen, aoff, ooff, ocnt, grp) in c.seqs:
            for tt in range(slen // TT):
                kv_i = 0
                tok0 = aoff + tt * TT
                pos0 = tt * TT
                hTs = hTr.next()
                prologue(P, T["xall"], tok0, grp, A1, B1, hTs)
                hT = hTs[0]
                cb, sb = cosb.next(), sinb.next()
                S.dma("sp", cb[:], T["cosA"][:, pos0:pos0 + TT], cb, writes=[cb])
                S.dma("sp", sb[:], T["sinA"][:, pos0:pos0 + TT], sb, writes=[sb])
                for w0 in range(0, c.KVW, 512):
                    wn = min(512, c.KVW - w0)
                    W = Wb.next()
                    assert kvf_tiles[kv_i][0] == c.cK + w0 and kvf_tiles[kv_i][1] == wn
                    load_wb(W, kvf_tiles[kv_i][2], KC, wn)
                    kv_i += 1
                    for hb in range(wn // 128):
                        g = (w0 + hb * 128) // 128
                        ps = psg.next()
                        S.pe([mm(ps[:, 0:TT], W[:, kc, hb * 128:(hb + 1) * 128], hT[:, kc, :], kc == 0, kc == KC - 1) for kc in range(KC)],
                             reads=[W] + hTs, writes=[ps])
                        ob = ko.next()
                        rope_epi(Rp, ps, kg, cb, sb, ob)
                        S.dma("sp", T["KT"][g, :, tok0:tok0 + TT], ob[:], ob, reads=[ob])
                for (c0, width, dst) in ((c.cV, c.KVW, "V"), (c.cF, c.FW, "U")):
                    for w0 in range(0, width, 512):
                        wn = min(512, width - w0)
                        W = Wb.next()
                        assert kvf_tiles[kv_i][0] == c0 + w0 and kvf_tiles[kv_i][1] == wn
                        load_wb(W, kvf_tiles[kv_i][2], KC, wn)
                        kv_i += 1
                        for sub in range(NSUB):
                            ps = psg.next()
                            S.pe([mm(ps[:, 0:wn], hT[:, kc, sub * 128:(sub + 1) * 128], W[:, kc, 0:wn], kc == 0, kc == KC - 1) for kc in range(KC)],
                                 reads=[W, hTs[sub]], writes=[ps])
                            ob = vo.next()
                            S.op("act", lambda: A_.activation(out=ob[:, 0:wn], in_=ps[:, 0:wn], func=AF.Copy), reads=[ps], writes=[ob])
                            S.dma("sp", T[dst][tok0 + sub * 128:tok0 + (sub + 1) * 128, w0:w0 + wn], ob[:, 0:wn], ob, reads=[ob])
        S.barrier()
        st.close()

    def stage_qg():
        st = contextlib.ExitStack()
        P = alloc_prologue(st)
        Rp = alloc_rope(st)
        hTt = [S.raw_sbuf(st, "hT%d" % i, [128, KC, TT], BF16) for i in range(2)]
        hTr = Rot([[Buf("hT%d_%d" % (i, s_), hTt[i]) for s_ in range(NSUB)] for i in range(2)])
        Wb = Rot([S.sbuf(st, "W%d" % i, [128, KC, 512], BF16) for i in range(2)])
        psg = Rot([S.psum(st, "psg%d" % i, [128, 512]) for i in range(2)])
        cosb = Rot([S.sbuf(st, "cos%d" % i, [128, TT], F32) for i in range(2)])
        sinb = Rot([S.sbuf(st, "sin%d" % i, [128, TT], F32) for i in range(2)])
        ko = Rot([S.sbuf(st, "ko%d" % i, [128, TT], BF16) for i in range(3)])
        for (slen, aoff, ooff, ocnt, grp) in c.seqs:
            for tt in range(ocnt // TT):
                tok0 = ooff + tt * TT
                hTs = hTr.next()
                prologue(P, T["xown"], tok0, grp, A1, B1, hTs)
                hT = hTs[0]
                cb, sb = cosb.next(), sinb.next()
                S.dma("sp", cb[:], T["cosO"][:, tok0:tok0 + TT], cb, writes=[cb])
                S.dma("sp", sb[:], T["sinO"][:, tok0:tok0 + TT], sb, writes=[sb])
                for w0 in range(0, c.ATT, 512):
                    wn = min(512, c.ATT - w0)
                    W = Wb.next()
                    load_w(W, T["w_in"], 0, D, c.cQ + w0, wn)
                    for hb in range(wn // 128):
                        h = (w0 + hb * 128) // 128
                        ps = psg.next()
                        S.pe([mm(ps[:, 0:TT], W[:, kc, hb * 128:(hb + 1) * 128], hT[:, kc, :], kc == 0, kc == KC - 1) for kc in range(KC)],
                             reads=[W] + hTs, writes=[ps])
                        ob = ko.next()
                        rope_epi(Rp, ps, qg, cb, sb, ob)
                        S.dma("sp", T["QT"][h, :, tok0:tok0 + TT], ob[:], ob, reads=[ob])
                for (c0, dst) in ((c.cGA, "GAT"), (c.cGF, "GFT")):
                    for w0 in range(0, D, 512):
                        wn = min(512, D - w0)
                        W = Wb.next()
                        load_w(W, T["w_in"], 0, D, c0 + w0, wn)
                        for hb in range(wn // 128):
                            ps = psg.next()
                            S.pe([mm(ps[:, 0:TT], W[:, kc, hb * 128:(hb + 1) * 128], hT[:, kc, :], kc == 0, kc == KC - 1) for kc in range(KC)],
                                 reads=[W] + hTs, writes=[ps])
                            ob = ko.next()
                            S.op("act", lambda: A_.activation(out=ob[:], in_=ps[:, 0:TT], func=AF.Sigmoid), reads=[ps], writes=[ob])
                            r0 = w0 + hb * 128
                            S.dma("sp", T[dst][r0:r0 + 128, tok0:tok0 + TT], ob[:], ob, reads=[ob])
        S.barrier()
        st.close()

    def stage_attn():
        st = contextlib.ExitStack()
        SMAX = max(c.SP, c.SS)
        QB = TT
        KTb = Rot([S.sbuf(st, "KTb%d" % i, [128, SMAX], BF16) for i in range(2)])
        Vb = Rot([S.sbuf(st, "Vb%d" % i, [128, SMAX // 128, 128], BF16) for i in range(2)])
        Qb = Rot([S.sbuf(st, "Qb%d" % i, [128, QB], BF16) for i in range(2)])
        pT = Rot([S.sbuf(st, "pT%d" % i, [128, QB], BF16) for i in range(3)])
        pss = Rot([S.psum(st, "pss%d" % i, [128, QB]) for i in range(2)])
        pso = Rot([S.psum(st, "pso%d" % i, [128, QB]) for i in range(2)])
        psl = Rot([S.psum(st, "psl%d" % i, [128, QB]) for i in range(2)])
        rl = Rot([S.sbuf(st, "rl%d" % i, [128, QB], F32) for i in range(2)])
        ob_ = Rot([S.sbuf(st, "ao%d" % i, [128, QB], BF16) for i in range(2)])
        PBW = max(KC * 256, c.SBE * min(512, D), (c.ATT // 128) * 256, (c.FW // 128) * 256)
        pbuf = Rot([S.sbuf(st, "prep%d" % i, [128, PBW], BF16) for i in range(3)])
        nblocks = sum(c.NH * (oc // QB) for (_, _, _, oc, _) in c.seqs)
        per_block = -(-len(prep_items) // max(1, nblocks - 2))
        scale = 128 ** -0.5
        for (slen, aoff, ooff, ocnt, grp) in c.seqs:
            nkc = slen // 128
            for g in range(c.NKV):
                Kt, Vt = KTb.next(), Vb.next()
                S.dma("sp", Kt[:, 0:slen], T["KT"][g, :, aoff:aoff + slen], Kt, writes=[Kt])
                for k0 in range(0, nkc, 16):
                    k1 = min(nkc, k0 + 16)
                    S.dma("sp", Vt[:, k0:k1, :],
                          T["V"][aoff + k0 * 128:aoff + k1 * 128, g * 128:(g + 1) * 128].rearrange("(c p) d -> p c d", p=128),
                          Vt, writes=[Vt])
                for qh in range(c.NH // c.NKV):
                    h = g * (c.NH // c.NKV) + qh
                    for qb in range(ocnt // QB):
                        q0 = ooff + qb * QB
                        Qt = Qb.next()
                        S.dma("sp", Qt[:], T["QT"][h, :, q0:q0 + QB], Qt, writes=[Qt])
                        prep_emit(pbuf, per_block)
                        po, pl = pso.next(), psl.next()
                        pend = None
                        for kc in range(nkc + 1):
                            if kc < nkc:
                                ps = pss.next()
                                S.pe([mm(ps[:], Kt[:, kc * 128:(kc + 1) * 128], Qt[:], True, True)], reads=[Kt, Qt], writes=[ps])
                                p = pT.next()
                                S.op("act", lambda: A_.activation(out=p[:], in_=ps[:], func=AF.Exp, scale=scale), reads=[ps], writes=[p])
                            if pend is not None:
                                pk, pp = pend
                                S.pe([mm(po[:], Vt[:, pk, :], pp[:], pk == 0, pk == nkc - 1),
                                      mm(pl[:], onesb[:], pp[:], pk == 0, pk == nkc - 1)], reads=[Vt, pp, onesb], writes=[po, pl])
                            pend = (kc, p) if kc < nkc else None
                        r_, o_ = rl.next(), ob_.next()
                        S.op("dve", lambda: V_.reciprocal(out=r_[:], in_=pl[:]), reads=[pl], writes=[r_])
                        S.op("dve", lambda: V_.tensor_tensor(out=o_[:], in0=po[:], in1=r_[:], op=ALU.mult), reads=[po, r_], writes=[o_])
                        S.dma("sp", T["ATT_T"][h * 128:(h + 1) * 128, q0:q0 + QB], o_[:], o_, reads=[o_])
        prep_emit(pbuf, len(prep_items))
        S.barrier()
        st.close()

    def stage_dft():
        st = contextlib.ExitStack()
        NB = c.FW // 128
        GS = min(3, NB)
        SCB = 16
        GC = c.FGD // 128
        SB = TT
        Ct = Rot([S.sbuf(st, "Ct%d" % i, [128, SCB, SB], BF16) for i in range(2)])
        St_ = Rot([S.sbuf(st, "St%d" % i, [128, SCB, SB], BF16) for i in range(2)])
        Ut = Rot([S.sbuf(st, "Ut%d" % i, [128, SCB, GS * 128], BF16) for i in range(2)])
        P1 = S.sbuf(st, "P1", [128, NB, SB], BF16)
        P2 = S.sbuf(st, "P2", [128, NB, SB], BF16)
        CC = S.sbuf(st, "CC", [128, GC, c.FGD], BF16)
        SC = S.sbuf(st, "SC", [128, GC, c.FGD], BF16)
        acc = [S.psum(st, "dacc%d" % i, [128, SB]) for i in range(2 * GS)]
        psy = Rot([S.psum(st, "psy%d" % i, [128, SB]) for i in range(2)])
        yo = Rot([S.sbuf(st, "yo%d" % i, [128, SB], BF16) for i in range(2)])
        S.dma("sp", CC[:], T["dftC_c"].rearrange("(c p) n -> p c n", p=128), CC, writes=[CC])
        S.dma("sp", SC[:], T["dftC_s"].rearrange("(c p) n -> p c n", p=128), SC, writes=[SC])
        for si, (slen, aoff, ooff, ocnt, grp) in enumerate(c.seqs):
            tc_, ts_ = (T["dftP_c"], T["dftP_s"]) if si == 0 else (T["dftS_c"], T["dftS_s"])
            nkc = slen // 128
            for sb in range(ocnt // SB):
                for b0 in range(0, NB, GS):
                    gs = min(GS, NB - b0)
                    for sc0 in range(0, nkc, SCB):
                        sc1 = min(nkc, sc0 + SCB)
                        n = sc1 - sc0
                        ct, stt, ut = Ct.next(), St_.next(), Ut.next()
                        S.dma("sp", ct[:, 0:n, :], tc_[sc0 * 128:sc1 * 128, sb * SB:(sb + 1) * SB].rearrange("(c p) n -> p c n", p=128), ct, writes=[ct])
                        S.dma("sp", stt[:, 0:n, :], ts_[sc0 * 128:sc1 * 128, sb * SB:(sb + 1) * SB].rearrange("(c p) n -> p c n", p=128), stt, writes=[stt])
                        S.dma("sp", ut[:, 0:n, 0:gs * 128],
                              T["U"][aoff + sc0 * 128:aoff + sc1 * 128, b0 * 128:(b0 + gs) * 128].rearrange("(c p) n -> p c n", p=128),
                              ut, writes=[ut])
                        fns = []
                        for ci in range(n):
                            first, last = (sc0 + ci == 0), (sc0 + ci == nkc - 1)
                            for b in range(gs):
                                fns.append(mm(acc[2 * b][:], ut[:, ci, b * 128:(b + 1) * 128], ct[:, ci, :], first, last))
                                fns.append(mm(acc[2 * b + 1][:], ut[:, ci, b * 128:(b + 1) * 128], stt[:, ci, :], first, last))
                        S.pe(fns, reads=[ct, stt, ut], writes=acc[0:2 * gs])
                    for b in range(gs):
                        S.op("act", lambda: A_.activation(out=P1[:, b0 + b, :], in_=acc[2 * b][:], func=AF.Copy), reads=[acc[2 * b]], writes=[P1])
                        S.op("dve", lambda: V_.tensor_copy(out=P2[:, b0 + b, :], in_=acc[2 * b + 1][:]), reads=[acc[2 * b + 1]], writes=[P2])
                for ob in range(NB):
                    gi, ol = ob // GC, ob % GC
                    ps = psy.next()
                    fns = []
                    for cc in range(GC):
                        fns.append(mm(ps[:], CC[:, cc, ol * 128:(ol + 1) * 128], P1[:, gi * GC + cc, :], cc == 0, False))
                        fns.append(mm(ps[:], SC[:, cc, ol * 128:(ol + 1) * 128], P2[:, gi * GC + cc, :], False, cc == GC - 1))
                    S.pe(fns, reads=[CC, SC, P1, P2], writes=[ps])
                    y_ = yo.next()
                    S.op("act", lambda: A_.activation(out=y_[:], in_=ps[:], func=AF.Copy), reads=[ps], writes=[y_])
                    q0 = ooff + sb * SB
                    S.dma("sp", T["YT"][ob * 128:(ob + 1) * 128, q0:q0 + SB], y_[:], y_, reads=[y_])
        S.barrier()
        st.close()

    def stage_merge():
        st = contextlib.ExitStack()
        WC = 256
        KA, KF = c.ATT // 128, c.FW // 128
        atT = S.sbuf(st, "atT", [128, KA, TT], BF16)
        ytT = S.sbuf(st, "ytT", [128, KF, TT], BF16)
        mTt = S.raw_sbuf(st, "mT", [128, KC, TT], BF16)
        mT = Buf("mT", mTt)
        Wa = Rot([S.sbuf(st, "Wa%d" % i, [128, max(KA, KC), WC], BF16) for i in range(2)])
        Wf = Rot([S.sbuf(st, "Wf%d" % i, [128, KF, WC], BF16) for i in range(2)])
        gat = Rot([S.sbuf(st, "gat%d" % i, [128, WC // 128, TT], BF16) for i in range(2)])
        gft = Rot([S.sbuf(st, "gft%d" % i, [128, WC // 128, TT], BF16) for i in range(2)])
        g1bc = S.sbuf(st, "g1bc", [128, D], F32)
        psA = Rot([S.psum(st, "psA%d" % i, [128, 512]) for i in range(2)])
        psB = Rot([S.psum(st, "psB%d" % i, [128, 512]) for i in range(2)])
        psO = Rot([S.psum(st, "psO%d" % i, [128, 512]) for i in range(2)])
        ta = Rot([S.sbuf(st, "ta%d" % i, [128, TT], F32) for i in range(2)])
        tb = Rot([S.sbuf(st, "tb%d" % i, [128, TT], F32) for i in range(2)])
        xp = Rot([S.sbuf(st, "xp%d" % i, [128, WC], F32) for i in range(3)])
        xo = Rot([S.sbuf(st, "xo%d" % i, [128, WC], F32) for i in range(3)])
        for (slen, aoff, ooff, ocnt, grp) in c.seqs:
            S.dma("sp", g1bc[:], T["G1"][grp:grp + 1, :].partition_broadcast(128), g1bc, writes=[g1bc])
            for tt in range(ocnt // TT):
                tok0 = ooff + tt * TT
                S.dma("sp", atT[:], T["ATT_T"][:, tok0:tok0 + TT].rearrange("(c p) n -> p c n", p=128), atT, writes=[atT])
                S.dma("sp", ytT[:], T["YT"][:, tok0:tok0 + TT].rearrange("(c p) n -> p c n", p=128), ytT, writes=[ytT])
                for w0 in range(0, D, WC):
                    wa, wf, ga_, gf_ = Wa.next(), Wf.next(), gat.next(), gft.next()
                    load_wb(wa, T["WAb"][w0 // WC], KA, WC)
                    load_wb(wf, T["WFb"][w0 // WC], KF, WC)
                    S.dma("sp", ga_[:], T["GAT"][w0:w0 + WC, tok0:tok0 + TT].rearrange("(c p) n -> p c n", p=128), ga_, writes=[ga_])
                    S.dma("sp", gf_[:], T["GFT"][w0:w0 + WC, tok0:tok0 + TT].rearrange("(c p) n -> p c n", p=128), gf_, writes=[gf_])
                    for blk in range(WC // 128):
                        pa, pb = psA.next(), psB.next()
                        S.pe([mm(pa[:, 0:TT], wa[:, kc, blk * 128:(blk + 1) * 128], atT[:, kc, :], kc == 0, kc == KA - 1) for kc in range(KA)],
                             reads=[wa, atT], writes=[pa])
                        S.pe([mm(pb[:, 0:TT], wf[:, kc, blk * 128:(blk + 1) * 128], ytT[:, kc, :], kc == 0, kc == KF - 1) for kc in range(KF)],
                             reads=[wf, ytT], writes=[pb])
                        t1, t2 = ta.next(), tb.next()
                        S.op("dve", lambda: V_.tensor_tensor(out=t1[:], in0=pa[:, 0:TT], in1=ga_[:, blk, :], op=ALU.mult), reads=[pa, ga_], writes=[t1])
                        S.op("dve", lambda: V_.tensor_tensor(out=t2[:], in0=pb[:, 0:TT], in1=gf_[:, blk, :], op=ALU.mult), reads=[pb, gf_], writes=[t2])
                        ob = (w0 + blk * 128) // 128
                        S.op("pool", lambda: G_.tensor_tensor(out=mT[:, ob, :], in0=t1[:], in1=t2[:], op=ALU.add), reads=[t1, t2], writes=[mT])
                for w0 in range(0, D, WC):
                    wo = Wa.next()
                    load_wb(wo, T["WOb"][w0 // WC], KC, WC)
                    for sub in range(NSUB):
                        r0 = tok0 + sub * 128
                        ps = psO.next()
                        S.pe([mm(ps[:, 0:WC], mT[:, kc, sub * 128:(sub + 1) * 128], wo[:, kc, 0:WC], kc == 0, kc == KC - 1) for kc in range(KC)],
                             reads=[wo, mT], writes=[ps])
                        xp_, xo_ = xp.next(), xo.next()
                        S.dma("sp", xp_[:], T["xown"][r0:r0 + 128, w0:w0 + WC], xp_, writes=[xp_])
                        S.op("dve", lambda: V_.tensor_tensor(out=xo_[:], in0=ps[:, 0:WC], in1=g1bc[:, w0:w0 + WC], op=ALU.mult), reads=[ps, g1bc], writes=[xo_])
                        S.op("pool", lambda: G_.tensor_tensor(out=xo_[:], in0=xo_[:], in1=xp_[:], op=ALU.add), reads=[xo_, xp_], writes=[xo_])
                        S.dma("sp", T["X1"][r0:r0 + 128, w0:w0 + WC], xo_[:], xo_, reads=[xo_])
        S.barrier()
        st.close()

    def stage_peer_q():
        st = contextlib.ExitStack()
        HP, PH = c.HP, c.PH
        P = alloc_prologue(st)
        hTt = [S.raw_sbuf(st, "hT%d" % i, [128, KC, TT], BF16) for i in range(1)]
        hTr = Rot([[Buf("hT%d_%d" % (i, s_), hTt[i]) for s_ in range(NSUB)] for i in range(1)])
        Wb = Rot([S.sbuf(st, "W%d" % i, [128, KC, 256], BF16) for i in range(2)])
        keysT = S.sbuf(st, "keysT", [128, HP, 128], F32)
        psg = Rot([S.psum(st, "psg%d" % i, [128, 512]) for i in range(2)])
        pssc = Rot([S.psum(st, "pssc%d" % i, [128, 4, 128]) for i in range(2)])
        pqb = Rot([S.sbuf(st, "pqb%d" % i, [128, TT], F32) for i in range(2)])
        sc = [S.sbuf(st, "sc%d" % i, [128, HP, 128], F32) for i in range(NSUB)]
        mx = S.sbuf(st, "mx", [128, HP], F32)
        eb = Rot([S.sbuf(st, "eb%d" % i, [128, HP, 128], BF16) for i in range(2)])
        ef = S.sbuf(st, "ef", [128, HP, 128], F32)
        wk = S.sbuf(st, "wk", [128, 256], F32)
        tv = S.sbuf(st, "tv", [128, HP, 16], F32)
        cand = S.sbuf(st, "cand", [128, PH, 16, 16], F32)
        tcd = S.sbuf(st, "tcd", [128, PH, 16], F32)
        epz = Rot([S.sbuf(st, "epz%d" % i, [128, 2, PH], F32) for i in range(2)])
        S.dma("sp", keysT[:], T["keysT"], keysT, writes=[keysT])
        for (slen, aoff, ooff, ocnt, grp) in c.seqs:
            for tt in range(ocnt // TT):
                tok0 = ooff + tt * TT
                hTs = hTr.next()
                prologue(P, T["X1"], tok0, grp, A2, B2, hTs)
                hT = hTs[0]
                S.dma("sp", T["H2T"][:, tok0:tok0 + TT].rearrange("(c p) n -> p c n", p=128), hT[:], hT, reads=hTs)
                for w0 in range(0, c.PQ, 256):
                    wn = min(256, c.PQ - w0)
                    W = Wb.next()
                    load_w(W, T["w_pq"], 0, D, w0, wn)
                    for blk in range(wn // 128):
                        hp = (w0 + blk * 128) // 128
                        ps = psg.next()
                        S.pe([mm(ps[:, 0:TT], W[:, kc, blk * 128:(blk + 1) * 128], hT[:, kc, :], kc == 0, kc == KC - 1) for kc in range(KC)],
                             reads=[W] + hTs, writes=[ps])
                        pq = pqb.next()
                        S.op("act", lambda: A_.activation(out=pq[:], in_=ps[:, 0:TT], func=AF.Copy), reads=[ps], writes=[pq])
                        pk = pssc.next()
                        S.pe([mm(pk[:, sub, :], pq[:, sub * 128:(sub + 1) * 128], keysT[:, hp, :], True, True) for sub in range(NSUB)],
                             reads=[pq, keysT], writes=[pk])
                        for sub in range(NSUB):
                            S.op("dve", lambda: V_.tensor_copy(out=sc[sub][:, hp, :], in_=pk[:, sub, :]), reads=[pk], writes=[sc[sub]])
                for sub in range(NSUB):
                    s_ = sc[sub]
                    r0 = tok0 + sub * 128
                    S.op("dve", lambda: V_.tensor_reduce(out=mx[:], in_=s_[:], axis=AX.X, op=ALU.max), reads=[s_], writes=[mx])
                    S.op("dve", lambda: V_.tensor_tensor(out=s_[:], in0=s_[:], in1=mx[:].unsqueeze(2).broadcast_to([128, HP, 128]), op=ALU.subtract),
                         reads=[s_, mx], writes=[s_])
                    e_ = eb.next()
                    S.op("act", lambda: A_.activation(out=e_[:], in_=s_[:], func=AF.Exp), reads=[s_], writes=[e_])
                    S.dma("sp", T["EB"][r0:r0 + 128, :, :], e_[:], e_, reads=[e_])
                    S.op("dve", lambda: V_.tensor_copy(out=ef[:], in_=e_[:]), reads=[e_], writes=[ef])
                    for hp in range(HP):
                        S.op("dve", lambda: V_.max(out=tv[:, hp, 0:8], in_=ef[:, hp, :]), reads=[ef], writes=[tv])
                        S.op("dve", lambda: V_.match_replace(out=wk[:, 0:128], in_to_replace=tv[:, hp, 0:8], in_values=ef[:, hp, :], imm_value=-1.0),
                             reads=[ef, tv], writes=[wk])
                        S.op("dve", lambda: V_.max(out=tv[:, hp, 8:16], in_=wk[:, 0:128]), reads=[wk], writes=[tv])
                    tvv = tv[:].rearrange("p (h two) k -> p h two k", two=2)
                    S.op("dve", lambda: V_.tensor_tensor(out=cand[:], in0=tvv[:, :, 0, :].unsqueeze(3).broadcast_to([128, PH, 16, 16]),
                                                         in1=tvv[:, :, 1, :].unsqueeze(2).broadcast_to([128, PH, 16, 16]), op=ALU.mult),
                         reads=[tv], writes=[cand])
                    for h in range(PH):
                        cv = cand[:, h, :, :].rearrange("p a b -> p (a b)")
                        S.op("dve", lambda: V_.max(out=tcd[:, h, 0:8], in_=cv), reads=[cand], writes=[tcd])
                        S.op("dve", lambda: V_.match_replace(out=wk[:, 0:256], in_to_replace=tcd[:, h, 0:8], in_values=cv, imm_value=-1.0),
                             reads=[cand, tcd], writes=[wk])
                        S.op("dve", lambda: V_.max(out=tcd[:, h, 8:16], in_=wk[:, 0:256]), reads=[wk], writes=[tcd])
                    ez = epz.next()
                    S.op("dve", lambda: V_.tensor_reduce(out=ez[:, 0, :], in_=tcd[:], axis=AX.X, op=ALU.min), reads=[tcd], writes=[ez])
                    S.op("dve", lambda: V_.tensor_scalar(out=ez[:, 0, :], in0=ez[:, 0, :], scalar1=1.0 - 2.0 ** -12, scalar2=None, op0=ALU.mult),
                         reads=[ez], writes=[ez])
                    S.op("dve", lambda: V_.tensor_reduce(out=ez[:, 1, :], in_=tcd[:], axis=AX.X, op=ALU.add), reads=[tcd], writes=[ez])
                    S.op("dve", lambda: V_.reciprocal(out=ez[:, 1, :], in_=ez[:, 1, :]), reads=[ez], writes=[ez])
                    S.dma("sp", T["EPSD"][r0:r0 + 128, :], ez[:, 0, :], ez, reads=[ez])
                    S.dma("sp", T["RZD"][r0:r0 + 128, :], ez[:, 1, :], ez, reads=[ez])
        S.barrier()
        st.close()

    def stage_peer_e():
        st = contextlib.ExitStack()
        HP, PH = c.HP, c.PH
        SBE = c.SBE
        IB = 2
        h2T = S.sbuf(st, "h2T", [128, KC, TT], BF16)
        ebt = S.sbuf(st, "ebt", [128, NSUB, HP, 128], BF16)
        epst = S.sbuf(st, "epst", [128, NSUB, PH], F32)
        rzt = S.sbuf(st, "rzt", [128, NSUB, PH], F32)
        Dg = S.sbuf(st, "Dg", [128, NSUB * PH, 128], BF16)
        acct = S.raw_sbuf(st, "acc", [128, NSUB, D], F32)
        accs = [Buf("acc%d" % i, acct) for i in range(NSUB)]
        uTt = Rot([S.sbuf(st, "uT%d" % i, [128, KC, 128], BF16) for i in range(2)])
        vt = Rot([S.sbuf(st, "vt%d" % i, [128, SBE, 512], BF16) for i in range(2)])
        AGTt = S.raw_sbuf(st, "AGT", [128, SBE, TT], BF16)
        AGT = [Buf("AGT%d" % i, AGTt) for i in range(SBE)]
        gl = Rot([S.sbuf(st, "gl%d" % i, [128, TT], F32) for i in range(2)])
        Et = Rot([S.sbuf(st, "Et%d" % i, [128, IB, 128], F32) for i in range(4)])
        e1f = Rot([S.sbuf(st, "e1f%d" % i, [128, NSUB, PH, SBE], F32) for i in range(2)])
        Gtt = [[[S.sbuf(st, "G%d_%d_%d" % (pp, s_, h), [128, IB, 128], BF16) for h in range(PH)] for s_ in range(NSUB)] for pp in range(2)]
        psE = Rot([S.psum(st, "psE%d" % i, [128, 512]) for i in range(2)])
        psG = Rot([S.psum(st, "psG%d" % i, [128, 512]) for i in range(2)])
        ps2 = Rot([S.psum(st, "ps2%d" % i, [128, 512]) for i in range(3)])
        for (slen, aoff, ooff, ocnt, grp) in c.seqs:
            for tt in range(ocnt // TT):
                tok0 = ooff + tt * TT
                S.dma("sp", h2T[:], T["H2T"][:, tok0:tok0 + TT].rearrange("(c p) n -> p c n", p=128), h2T, writes=[h2T])
                S.dma("sp", ebt[:], T["EB"][tok0:tok0 + TT, :, :].rearrange("(s p) h n -> p s h n", p=128), ebt, writes=[ebt])
                S.dma("sp", epst[:], T["EPSD"][tok0:tok0 + TT, :].rearrange("(s p) h -> p s h", p=128), epst, writes=[epst])
                S.dma("sp", rzt[:], T["RZD"][tok0:tok0 + TT, :].rearrange("(s p) h -> p s h", p=128), rzt, writes=[rzt])
                for sub in range(NSUB):
                    for h in range(PH):
                        S.op("dve", lambda: V_.tensor_scalar(out=Dg[:, sub * PH + h, :], in0=identf[:], scalar1=rzt[:, sub, h:h + 1],
                                                             scalar2=None, op0=ALU.mult), reads=[identf, rzt], writes=[Dg])
                ebv = ebt[:].rearrange("p s (h two) n -> p s h two n", two=2)
                for sbk in range(c.NE // (128 * SBE)):
                    ef_ = e1f.next()
                    S.op("dve", lambda: V_.tensor_copy(out=ef_[:], in_=ebv[:, :, :, 0, sbk * SBE:(sbk + 1) * SBE]), reads=[ebt], writes=[ef_])
                    for ib in range(SBE // IB):
                        i0 = sbk * SBE + ib * IB
                        Gt = Gtt[ib % 2]
                        for sub in range(NSUB):
                            for h in range(PH):
                                E = Et.next()
                                G = Gt[sub][h]
                                for ii in range(IB):
                                    S.op("act", lambda: A_.activation(out=E[:, ii, :], in_=ebt[:, sub, 2 * h + 1, :], func=AF.Identity,
                                                                      scale=ef_[:, sub, h, ib * IB + ii:ib * IB + ii + 1]),
                                         reads=[ebt, ef_], writes=[E])
                                S.op("dve", lambda: V_.scalar_tensor_tensor(out=G[:], in0=E[:], scalar=epst[:, sub, h:h + 1], in1=E[:],
                                                                            op0=ALU.is_ge, op1=ALU.mult), reads=[E, epst], writes=[G])
                        for ii in range(IB):
                            ci = ib * IB + ii
                            ch = sbk * SBE + ci
                            ut = uTt.next()
                            load_wb(ut, T["UTb"][ch], KC, 128)
                            pe_ = psE.next()
                            S.pe([mm(pe_[:, 0:TT], ut[:, kc, :], h2T[:, kc, :], kc == 0, kc == KC - 1) for kc in range(KC)],
                                 reads=[ut, h2T], writes=[pe_])
                            g_ = gl.next()
                            S.op("act", lambda: A_.activation(out=g_[:], in_=pe_[:, 0:TT], func=AF.Gelu), reads=[pe_], writes=[g_])
                            pg = psG.next()
                            fns = []
                            for sub in range(NSUB):
                                for h in range(PH):
                                    fns.append(mm(pg[:, sub * 128:(sub + 1) * 128], Gt[sub][h][:, ii, :], Dg[:, sub * PH + h, :], h == 0, h == PH - 1))
                            S.pe(fns, reads=[Dg] + [Gt[s_][h] for s_ in range(NSUB) for h in range(PH)], writes=[pg])
                            S.op("dve", lambda: V_.tensor_tensor(out=AGT[ci][:, ci, :], in0=pg[:, 0:TT], in1=g_[:], op=ALU.mult),
                                 reads=[pg, g_], writes=[AGT[ci]])
                    for w0 in range(0, D, 512):
                        wn = min(512, D - w0)
                        v_ = vt.next()
                        load_wb(v_, T["VBb"][sbk, w0 // 512], SBE, wn)
                        for sub in range(NSUB):
                            p2 = ps2.next()
                            S.pe([mm(p2[:, 0:wn], AGT[ci][:, ci, sub * 128:(sub + 1) * 128], v_[:, ci, 0:wn], ci == 0, ci == SBE - 1) for ci in range(SBE)],
                                 reads=[v_] + AGT, writes=[p2])
                            a_ = accs[sub]
                            if sbk == 0:
                                S.op("dve", lambda: V_.tensor_copy(out=a_[:, sub, w0:w0 + wn], in_=p2[:, 0:wn]), reads=[p2], writes=[a_])
                            else:
                                S.op("dve", lambda: V_.tensor_tensor(out=a_[:, sub, w0:w0 + wn], in0=a_[:, sub, w0:w0 + wn], in1=p2[:, 0:wn], op=ALU.add),
                                     reads=[p2, a_], writes=[a_])
                for sub in range(NSUB):
                    r0 = tok0 + sub * 128
                    S.dma("sp", T["PO"][r0:r0 + 128, :], accs[sub][:, sub, :], accs[sub], reads=[accs[sub]])
        S.barrier()
        st.close()

    def stage_final():
        st = contextlib.ExitStack()
        g2bc = S.sbuf(st, "g2bc", [128, D], F32)
        fgbc = S.sbuf(st, "fgbc", [128, D], F32)
        x1 = Rot([S.sbuf(st, "x1_%d" % i, [128, D], F32) for i in range(2)])
        po = Rot([S.sbuf(st, "po_%d" % i, [128, D], F32) for i in range(2)])
        junk = S.sbuf(st, "junkf", [128, D], BF16)
        ss = Rot([S.sbuf(st, "ssf%d" % i, [128, 1], F32) for i in range(2)])
        S.dma("sp", fgbc[:], T["fg"][0:1, :].partition_broadcast(128), fgbc, writes=[fgbc])
        for (slen, aoff, ooff, ocnt, grp) in c.seqs:
            S.dma("sp", g2bc[:], T["G2"][grp:grp + 1, :].partition_broadcast(128), g2bc, writes=[g2bc])
            for t in range(ocnt // 128):
                r0 = ooff + t * 128
                a, b = x1.next(), po.next()
                S.dma("sp", a[:], T["X1"][r0:r0 + 128, :], a, writes=[a])
                S.dma("sp", b[:], T["PO"][r0:r0 + 128, :], b, writes=[b])
                S.op("dve", lambda: V_.tensor_tensor(out=b[:], in0=b[:], in1=g2bc[:], op=ALU.mult), reads=[b, g2bc], writes=[b])
                S.op("pool", lambda: G_.tensor_tensor(out=a[:], in0=a[:], in1=b[:], op=ALU.add), reads=[a, b], writes=[a])
                s_ = ss.next()
                S.op("dve", lambda: V_.memset(s_[:], 0.0), writes=[s_])
                S.op("act", lambda: A_.activation(out=junk[:], in_=a[:], func=AF.Square, accum_out=s_[:]), reads=[a, s_], writes=[junk, s_])
                S.op("dve", lambda: V_.tensor_scalar(out=s_[:], in0=s_[:], scalar1=1.0 / D, scalar2=c.EPS, op0=ALU.mult, op1=ALU.add), reads=[s_], writes=[s_])
                S.op("act", lambda: A_.sqrt(out=s_[:], in_=s_[:]), reads=[s_], writes=[s_])
                S.op("dve", lambda: V_.reciprocal(out=s_[:], in_=s_[:]), reads=[s_], writes=[s_])
                S.op("dve", lambda: V_.scalar_tensor_tensor(out=b[:], in0=a[:], scalar=s_[:, 0:1], in1=fgbc[:], op0=ALU.mult, op1=ALU.mult),
                     reads=[a, s_, fgbc], writes=[b])
                S.dma("sp", T["y"][r0:r0 + 128, :], b[:], b, reads=[b])
        S.barrier()
        st.close()

    stages = [stage_mod, stage_kvf, stage_qg, stage_attn, stage_dft, stage_merge, stage_peer_q, stage_peer_e, stage_final]
    for i, sfn in enumerate(stages):
        if i < getattr(c, "nstages", 99):
            sfn()
    S.drain()
    top.close()
    return nc


def _rope_tables(cfg, positions):
    axis_dim = 64
    inv_freq = (10000.0 ** (-np.arange(0, axis_dim, 2, dtype=np.float32) / axis_dim)).astype(np.float32)
    pos = np.asarray(positions)
    row = (pos // cfg.GRID_W).astype(np.float32)
    col = (pos % cfg.GRID_W).astype(np.float32)
    ang = np.concatenate([row[:, None] * inv_freq, col[:, None] * inv_freq], axis=-1).astype(np.float32)
    cos = np.cos(ang).astype(np.float32)
    sin = np.sin(ang).astype(np.float32)
    return np.ascontiguousarray(np.repeat(cos, 2, axis=1).T), np.ascontiguousarray(np.repeat(sin, 2, axis=1).T)


def _dft_tables(S_len, own_pos):
    s = np.arange(S_len, dtype=np.int64)[:, None]
    k = np.asarray(own_pos, dtype=np.int64)[None, :]
    ang = 2.0 * np.pi * ((s * k) % S_len).astype(np.float64) / S_len
    sc = 1.0 / np.sqrt(S_len)
    return (np.cos(ang) * sc).astype(ml_dtypes.bfloat16), (np.sin(ang) * sc).astype(ml_dtypes.bfloat16)


_NC_CACHE = {}


def run(cfg, x_prompt, x_sample, c_prompt, c_sample, w_ada, b_ada, norm1_g, norm2_g, w_in, q_norm_g, k_norm_g,
        w_attn_br, w_four_br, w_out, w_peer_q, peer_keys, peer_u, peer_v, final_g, return_all=False):
    c = cfg
    f32 = np.float32
    A = lambda a: np.ascontiguousarray(np.asarray(a, dtype=f32))
    x_prompt, x_sample = A(x_prompt), A(x_sample)
    key = (c.D, c.SP, c.SS, c.NH, c.NKV, c.FG, c.FGD, c.PH, c.TT, c.debug, getattr(c, "nstages", 99))
    if key not in _NC_CACHE:
        _NC_CACHE[key] = build(c)
    nc = _NC_CACHE[key]
    shared = {
        "w_ada": A(w_ada[0]), "b_ada": A(b_ada[0])[None, :], "n1g": A(norm1_g[0])[None, :], "n2g": A(norm2_g[0])[None, :],
        "fg": A(final_g)[None, :], "w_in": A(w_in[0]), "qg": A(q_norm_g[0]).reshape(128, 1), "kg": A(k_norm_g[0]).reshape(128, 1),
        "w_attn": A(w_attn_br[0]), "w_four": A(w_four_br[0]), "w_out": A(w_out[0]), "w_pq": A(w_peer_q[0]),
        "keysT": np.ascontiguousarray(A(peer_keys[0]).reshape(c.HP, 128, 128).transpose(2, 0, 1)),
        "uT": np.ascontiguousarray(A(peer_u[0]).T), "vtab": A(peer_v[0]),
    }
    R = np.zeros((128, 128), f32)
    for i in range(64):
        R[2 * i + 1, 2 * i] = -1.0
        R[2 * i, 2 * i + 1] = 1.0
    shared["rotR"] = R
    cosA, sinA = _rope_tables(c, np.arange(c.SS))
    shared["cosA"], shared["sinA"] = cosA, sinA
    n = np.arange(c.FGD, dtype=np.int64)
    angc = 2.0 * np.pi * ((n[:, None] * n[None, :]) % c.FGD).astype(np.float64) / c.FGD
    shared["dftC_c"] = (np.cos(angc) / np.sqrt(c.FGD)).astype(ml_dtypes.bfloat16)
    shared["dftC_s"] = (-np.sin(angc) / np.sqrt(c.FGD)).astype(ml_dtypes.bfloat16)
    in_maps = []
    for core in range(8):
        b, r = core // 4, core % 4
        posP = np.arange(r * c.OP, (r + 1) * c.OP)
        posS = np.arange(r * c.OS, (r + 1) * c.OS)
        m = dict(shared)
        m["xall"] = np.concatenate([x_prompt[b], x_sample[b]], axis=0)
        m["xown"] = np.concatenate([x_prompt[b, posP], x_sample[b, posS]], axis=0)
        cT = np.stack([A(c_prompt[b]).reshape(c.KC, 128).T, A(c_sample[b]).reshape(c.KC, 128).T], axis=-1)
        m["cT"] = np.ascontiguousarray(cT)
        cp, sp_ = _rope_tables(c, posP)
        cs_, ss_ = _rope_tables(c, posS)
        m["cosO"] = np.ascontiguousarray(np.concatenate([cp, cs_], axis=1))
        m["sinO"] = np.ascontiguousarray(np.concatenate([sp_, ss_], axis=1))
        m["dftP_c"], m["dftP_s"] = _dft_tables(c.SP, posP)
        m["dftS_c"], m["dftS_s"] = _dft_tables(c.SS, posS)
        in_maps.append(m)
    res = run_bass_kernel_spmd(nc, in_maps, core_ids=list(range(8)))
    yp = np.zeros((2, c.SP, c.D), f32)
    ys = np.zeros((2, c.SS, c.D), f32)
    for core in range(8):
        b, r = core // 4, core % 4
        y = res.results[core]["y"]
        yp[b, r * c.OP:(r + 1) * c.OP] = y[0:c.OP]
        ys[b, r * c.OS:(r + 1) * c.OS] = y[c.OP:]
    if return_all:
        return (yp, ys), res.results
    return (yp, ys)


def kernel(**inputs):
    return run(Cfg(), **inputs)
```
